# Optimizing a Trainium2 kernel written in Bass

```python
import math
import jax
import jax.numpy as jnp
from jax import lax
import numpy as np

D_MODEL = 1024
BATCH = 32
SEQ = 256
DEPTH = 4
DEC_BATCH = 2
DEC_SEQ = 4096
PAST_LEN = 512

GRID_W = 64
N_GROUPS = 4
GROUP_W = D_MODEL // N_GROUPS
HEAD_DIM = 64
F32 = jnp.float32
HY_W = GROUP_W
HY_COLS = 3 * HY_W
HY_BANDS = 16
HY_EMB = 2 * HY_BANDS + 1
HY_FFN = 64
HY_TARGET = 1e-2
HY_MAX_DECAY = math.log(HY_TARGET) / 0.3
HY_MIN_DECAY = math.log(HY_TARGET) / 1.5
ML_HEADS = GROUP_W // HEAD_DIM
ML_CHUNK = 128
ML_COLS = 4 * GROUP_W + 4 * ML_HEADS
RW_HEADS = GROUP_W // HEAD_DIM
RW_W_RANK = 64
RW_A_RANK = 64
RW_G_RANK = 128
RW_DECAY_SCALE = 0.606531
RW_GN_EPS = 64e-5
RW_COLS = 3 * GROUP_W + RW_W_RANK + RW_A_RANK + RW_G_RANK
AT_HEADS = GROUP_W // HEAD_DIM
AT_KV_HEADS = 2
AT_BLOCK = 128
AT_COLS = (AT_HEADS + 2 * AT_KV_HEADS) * HEAD_DIM
ROPE_THETA = 10000.0
ROPE_AXIS = HEAD_DIM // 2
ROPE_FREQS = ROPE_AXIS // 2
P_IN = HY_COLS + ML_COLS + RW_COLS + AT_COLS
IN_SPLITS = (HY_COLS, HY_COLS + ML_COLS, HY_COLS + ML_COLS + RW_COLS)
ML_SPLITS = (GROUP_W, 2 * GROUP_W, 3 * GROUP_W, 4 * GROUP_W)
RW_SPLITS = (GROUP_W, 2 * GROUP_W, 3 * GROUP_W, 3 * GROUP_W + RW_W_RANK, 3 * GROUP_W + RW_W_RANK + RW_A_RANK)
AT_SPLITS = (AT_HEADS * HEAD_DIM, (AT_HEADS + AT_KV_HEADS) * HEAD_DIM)
D_FF = 4 * D_MODEL
ALPHA = (2.0 * DEPTH) ** 0.25
BETA = (8.0 * DEPTH) ** -0.25

kernel_name = 'hybrid_flow_hymba_step'


def _prev(x):
    return jnp.pad(x, ((0, 0), (1, 0), (0, 0)))[:, :-1]


def _next(x):
    return jnp.pad(x, ((0, 0), (0, 1), (0, 0)))[:, 1:]


def _dwconv3(x, w):
    return _prev(x) * w[0] + x * w[1] + _next(x) * w[2]


def _layernorm(x, g, b, eps=1e-5):
    xf = x.astype(F32)
    mu = jnp.mean(xf, -1, keepdims=True)
    var = jnp.mean(jnp.square(xf - mu), -1, keepdims=True)
    return ((xf - mu) * lax.rsqrt(var + eps) * g.astype(F32) + b.astype(F32)).astype(x.dtype)


def _rmsnorm(x, g, eps=1e-6):
    xf = x.astype(F32)
    return (xf * lax.rsqrt(jnp.mean(jnp.square(xf), -1, keepdims=True) + eps) * g.astype(F32)).astype(x.dtype)


def _head_norm(y, eps):
    mu = jnp.mean(y, -1, keepdims=True)
    var = jnp.mean(jnp.square(y - mu), -1, keepdims=True)
    return (y - mu) * lax.rsqrt(var + eps)


def _hyena_filters(L, w1, b1, freq, w2, b2, w3):
    t = jnp.linspace(0.0, 1.0, L, dtype=F32)[:, None]
    ang = (2.0 * math.pi / L) * jnp.arange(L, dtype=F32)[:, None]
    bands = jnp.linspace(1e-4, HY_BANDS - 1, HY_BANDS, dtype=F32)[None, :]
    z = jnp.concatenate([t, jnp.cos(bands * ang), jnp.sin(bands * ang)], axis=-1)
    fr = freq.astype(F32)
    hid = jnp.sin(fr * (z @ w1.astype(F32) + b1.astype(F32)))
    hid = jnp.sin(fr * (hid @ w2.astype(F32) + b2.astype(F32)))
    h = (hid @ w3.astype(F32)).reshape(L, 2, 2, HY_W)
    deltas = jnp.abs(jnp.linspace(HY_MIN_DECAY, HY_MAX_DECAY, HY_W, dtype=F32))
    h = h * jnp.exp(-t * deltas)[:, None, None, :]
    causal = h[:, :, 0]
    anti = h[1:, :, 1][::-1]
    filt = jnp.concatenate([causal, jnp.zeros((1, 2, HY_W), F32), anti], axis=0)
    return filt / jnp.sum(jnp.abs(filt), axis=0, keepdims=True)


def _long_conv(u, filt, bias):
    L = u.shape[1]
    uf = u.astype(F32)
    spec = jnp.fft.rfft(uf, n=2 * L, axis=1) * jnp.fft.rfft(filt, axis=0)[None]
    y = jnp.fft.irfft(spec, n=2 * L, axis=1)[:, :L]
    return (y + uf * bias.astype(F32)).astype(u.dtype)


def _hyena(p, conv_w, filt, bias):
    p = _dwconv3(p, conv_w)
    v, x1, x2 = jnp.split(p, 3, axis=-1)
    z = x1 * _long_conv(v, filt[:, 0], bias[0])
    return x2 * _long_conv(z, filt[:, 1], bias[1])


def _mlstm_scan(q, k, v, log_i, log_f, C0, n0, m0):
    B, L, H, Dh = q.shape
    nc = L // ML_CHUNK

    def to_chunks(a):
        a = a.reshape((B, nc, ML_CHUNK) + a.shape[2:])
        return jnp.moveaxis(jnp.moveaxis(a, 1, 0), 3, 2)

    tril = jnp.tril(jnp.ones((ML_CHUNK, ML_CHUNK), bool))

    def step(carry, inp):
        C, n, m = carry
        qc, kc, vc, ic, fc = inp
        b = jnp.cumsum(fc, axis=-1)
        logd = jnp.where(tril, b[..., :, None] - b[..., None, :] + ic[..., None, :], -jnp.inf)
        inter = b + m[..., None]
        m_t = jnp.maximum(inter, jnp.max(logd, axis=-1))
        s = jnp.einsum('bhtd,bhsd->bhts', qc, kc) * jnp.exp(logd - m_t[..., None])
        w_in = jnp.exp(inter - m_t)
        num = jnp.einsum('bhts,bhse->bhte', s, vc) + w_in[..., None] * jnp.einsum('bhtd,bhde->bhte', qc, C)
        den = jnp.sum(s, -1) + w_in * jnp.einsum('bhtd,bhd->bht', qc, n)
        h = num / jnp.maximum(jnp.abs(den), jnp.exp(-m_t))[..., None]
        b_last = b[..., -1]
        logw = b_last[..., None] - b + ic
        m_new = jnp.maximum(b_last + m, jnp.max(logw, axis=-1))
        wk = jnp.exp(logw - m_new[..., None])
        decay = jnp.exp(b_last + m - m_new)
        C_new = decay[..., None, None] * C + jnp.einsum('bhs,bhsd,bhse->bhde', wk, kc, vc)
        n_new = decay[..., None] * n + jnp.einsum('bhs,bhsd->bhd', wk, kc)
        return (C_new, n_new, m_new), h

    xs = (to_chunks(q), to_chunks(k), to_chunks(v), to_chunks(log_i), to_chunks(log_f))
    (C, n, m), h = lax.scan(step, (C0, n0, m0), xs)
    h = jnp.swapaxes(jnp.moveaxis(h, 0, 1), 2, 3).reshape(B, L, H, Dh)
    return h, C, n, m


def _mlstm(p, gate_b, norm_g, C0, n0, m0):
    B, L, _ = p.shape
    pf = p.astype(F32)
    q, k, v, o, g = jnp.split(pf, ML_SPLITS, axis=-1)
    hd = lambda t: t.reshape(B, L, ML_HEADS, HEAD_DIM)
    q, k, v = hd(q) * (HEAD_DIM ** -0.5), hd(k), hd(v)
    g = (g + gate_b.astype(F32)).reshape(B, L, 2, 2, ML_HEADS)
    log_i = g[:, :, :, 0]
    log_f = jax.nn.log_sigmoid(g[:, :, :, 1])
    h_f, Cf, nf, mf = _mlstm_scan(q, k, v, log_i[:, :, 0], log_f[:, :, 0],
                                  C0[:, 0].astype(F32), n0[:, 0].astype(F32), m0[:, 0].astype(F32))
    fl = lambda t: jnp.flip(t, axis=1)
    h_b, Cb, nb, mb = _mlstm_scan(fl(q), fl(k), fl(v), fl(log_i[:, :, 1]), fl(log_f[:, :, 1]),
                                  C0[:, 1].astype(F32), n0[:, 1].astype(F32), m0[:, 1].astype(F32))
    h = _head_norm(h_f + fl(h_b), 1e-6).reshape(B, L, GROUP_W) * norm_g.astype(F32)
    out = (h * jax.nn.sigmoid(o)).astype(p.dtype)
    return out, jnp.stack([Cf, Cb], 1), jnp.stack([nf, nb], 1), jnp.stack([mf, mb], 1)


def _rwkv_scan(r, w, k, v, a_vec, b_vec, S0, reverse):
    def step(S, inp):
        r_t, w_t, k_t, v_t, a_t, b_t = inp
        sa = jnp.einsum('bhvk,bhk->bhv', S, a_t)
        S = S * w_t[:, :, None, :] + sa[..., None] * b_t[:, :, None, :] + v_t[..., :, None] * k_t[:, :, None, :]
        return S, jnp.einsum('bhvk,bhk->bhv', S, r_t)

    xs = tuple(jnp.moveaxis(t, 1, 0) for t in (r, w, k, v, a_vec, b_vec))
    S, ys = lax.scan(step, S0, xs, reverse=reverse)
    return S, jnp.moveaxis(ys, 0, 1)


def _rwkv(p, lp, S0):
    B, L, _ = p.shape
    f = lambda name: lp[name].astype(F32)
    pf = p.astype(F32)
    pf = pf + f('rw_mu') * (0.5 * (_prev(pf) + _next(pf)) - pf)
    r, k, v, lw, la, lg = jnp.split(pf, RW_SPLITS, axis=-1)
    hd = lambda t: t.reshape(B, L, RW_HEADS, HEAD_DIM)
    g = jax.nn.sigmoid(lg) @ f('rw_g2')
    kk = hd(k * f('rw_kk'))
    kk = kk / jnp.maximum(jnp.linalg.norm(kk, axis=-1, keepdims=True), 1e-12)
    tw = jnp.tanh(lw)
    rh, vh = hd(r), hd(v)
    rk = f('rw_rk').reshape(RW_HEADS, HEAD_DIM)
    scans, bonus, states = [], [], []
    for d in range(2):
        w = jnp.exp(-RW_DECAY_SCALE * jax.nn.sigmoid(f('rw_w0')[d] + tw @ f('rw_w2')[d]))
        a = jax.nn.sigmoid(f('rw_a0')[d] + la @ f('rw_a2')[d])
        kt = hd(k * (1.0 + (a - 1.0) * f('rw_ka')))
        S, y = _rwkv_scan(rh, hd(w), kt, vh, -kk, kk * hd(a), S0[:, d].astype(F32), d == 1)
        scans.append(y)
        bonus.append(jnp.sum(rh * kt * rk, axis=-1, keepdims=True) * vh)
        states.append(S)
    y = _head_norm(scans[0] + scans[1], RW_GN_EPS).reshape(B, L, GROUP_W) * f('rw_ln_g') + f('rw_ln_b')
    y = y + (bonus[0] + bonus[1]).reshape(B, L, GROUP_W)
    return (y * g).astype(p.dtype), jnp.stack(states, axis=1)


def _rope_2d(x):
    L = x.shape[1]
    rows = L // GRID_W
    pos_r = jnp.repeat(jnp.arange(rows, dtype=F32), GRID_W)
    pos_c = jnp.tile(jnp.arange(GRID_W, dtype=F32), rows)
    inv = ROPE_THETA ** (-jnp.arange(ROPE_FREQS, dtype=F32) / ROPE_FREQS)

    def rot(xa, pos):
        ang = pos[:, None] * inv[None, :]
        cos = jnp.cos(ang)[None, :, None, :]
        sin = jnp.sin(ang)[None, :, None, :]
        x1, x2 = xa[..., :ROPE_FREQS], xa[..., ROPE_FREQS:]
        return jnp.concatenate([x1 * cos - x2 * sin, x1 * sin + x2 * cos], -1)

    xf = x.astype(F32)
    return jnp.concatenate([rot(xf[..., :ROPE_AXIS], pos_r), rot(xf[..., ROPE_AXIS:], pos_c)], -1).astype(x.dtype)


def _attend_blocks(q, k, v):
    B, Lq, H, Dh = q.shape
    kvh = k.shape[2]
    G = H // kvh
    qb = jnp.moveaxis(q.reshape(B, Lq // AT_BLOCK, AT_BLOCK, kvh, G, Dh), 1, 0)

    def one(qblk):
        s = jnp.einsum('btkgd,bskd->bkgts', qblk, k).astype(F32) * (Dh ** -0.5)
        prob = jax.nn.softmax(s, axis=-1)
        return jnp.einsum('bkgts,bskd->btkgd', prob.astype(v.dtype), v)

    o = lax.map(one, qb)
    return jnp.moveaxis(o, 0, 1).reshape(B, Lq, H * Dh)


def _attn(p, qn, kn, ctx_kv):
    B, L, _ = p.shape
    q, k, v = jnp.split(p, AT_SPLITS, axis=-1)
    q = _rmsnorm(q.reshape(B, L, AT_HEADS, HEAD_DIM), qn)
    k = _rmsnorm(k.reshape(B, L, AT_KV_HEADS, HEAD_DIM), kn)
    v = v.reshape(B, L, AT_KV_HEADS, HEAD_DIM)
    if ctx_kv is None:
        return _attend_blocks(q, k, v).astype(p.dtype), k, v
    q, k = _rope_2d(q), _rope_2d(k)
    kc, vc = ctx_kv
    keys = jnp.concatenate([k, kc.astype(k.dtype)], axis=1)
    vals = jnp.concatenate([v, vc.astype(v.dtype)], axis=1)
    return _attend_blocks(q, keys, vals).astype(p.dtype), k, v


def _layer(x, mod, lp, ctx):
    B, L, _ = x.shape
    sh_a, sc_a, g_a, sh_m, sc_m, g_m = jnp.split(mod[:, None, :].astype(x.dtype), 6, axis=-1)
    h = x * (1 + sc_a) + sh_a
    p = h @ lp['w_in']
    p_hy, p_ml, p_rw, p_at = jnp.split(p, IN_SPLITS, axis=-1)
    filt = _hyena_filters(L, lp['hy_w1'], lp['hy_b1'], lp['hy_freq'], lp['hy_w2'], lp['hy_b2'], lp['hy_w3'])
    y_hy = _hyena(p_hy, lp['hy_conv'], filt, lp['hy_bias'])
    if ctx is None:
        C0 = jnp.zeros((B, 2, ML_HEADS, HEAD_DIM, HEAD_DIM), F32)
        n0 = jnp.zeros((B, 2, ML_HEADS, HEAD_DIM), F32)
        m0 = jnp.zeros((B, 2, ML_HEADS), F32)
        S0 = jnp.zeros((B, 2, RW_HEADS, HEAD_DIM, HEAD_DIM), F32)
        kv = None
    else:
        kc, vc, C0, n0, m0, S0 = ctx
        kv = (kc, vc)
    y_ml, C, n, m = _mlstm(p_ml, lp['ml_gate_b'], lp['ml_norm_g'], C0, n0, m0)
    y_rw, S = _rwkv(p_rw, lp, S0)
    y_at, k, v = _attn(p_at, lp['at_qn'], lp['at_kn'], kv)
    mix = jnp.concatenate([y_hy, y_ml, y_rw, y_at], axis=-1) @ lp['w_out']
    x = _layernorm(ALPHA * x + g_a * mix, lp['ln1_g'], lp['ln1_b'])
    h = x * (1 + sc_m) + sh_m
    ff = jnp.square(jax.nn.relu(h @ lp['mlp_w1'])) @ lp['mlp_w2']
    x = _layernorm(ALPHA * x + g_m * ff, lp['ln2_g'], lp['ln2_b'])
    return x, (k, v, C, n, m, S)


def setup_inputs(seed: int = 0) -> dict:
    key = jax.random.key(seed)
    ks = iter(jax.random.split(key, 64))
    nrm = lambda shape, scale: scale * jax.random.normal(next(ks), shape, F32)
    D = D_MODEL
    W = GROUP_W
    gate_off = jnp.concatenate([jnp.zeros(2 * D), jnp.ones(D), jnp.zeros(2 * D), jnp.ones(D)]).astype(F32)
    ml_off = jnp.concatenate([jnp.zeros(ML_HEADS), jnp.linspace(3.0, 6.0, ML_HEADS)] * 2).astype(F32)
    conv_off = jnp.array([0.0, 1.0, 0.0], F32)[:, None]
    return {
        'x_prompt': nrm((BATCH, SEQ, D), 1.0),
        'x_sample': nrm((DEC_BATCH, DEC_SEQ, D), 1.0),
        'cache_attn_k': nrm((DEC_BATCH, DEPTH, PAST_LEN, AT_KV_HEADS, HEAD_DIM), 1.0),
        'cache_attn_v': nrm((DEC_BATCH, DEPTH, PAST_LEN, AT_KV_HEADS, HEAD_DIM), 1.0),
        'state_mlstm_C': nrm((DEC_BATCH, DEPTH, 2, ML_HEADS, HEAD_DIM, HEAD_DIM), 0.3),
        'state_mlstm_n': nrm((DEC_BATCH, DEPTH, 2, ML_HEADS, HEAD_DIM), 0.3),
        'state_mlstm_m': nrm((DEC_BATCH, DEPTH, 2, ML_HEADS), 0.5),
        'state_rwkv_S': nrm((DEC_BATCH, DEPTH, 2, RW_HEADS, HEAD_DIM, HEAD_DIM), 0.3),
        'c': nrm((DEC_BATCH, D), 1.0),
        'c_ctx': nrm((D,), 1.0),
        'w_mod': nrm((DEPTH, D, 6 * D), 0.1 * D ** -0.5),
        'b_mod': nrm((DEPTH, 6 * D), 0.02) + gate_off,
        'w_in': nrm((DEPTH, D, P_IN), D ** -0.5),
        'hy_conv': nrm((DEPTH, 3, HY_COLS), 0.3) + conv_off,
        'hy_w1': nrm((DEPTH, HY_EMB, HY_FFN), HY_EMB ** -0.5),
        'hy_b1': nrm((DEPTH, HY_FFN), 0.1),
        'hy_freq': 1.0 + nrm((DEPTH, HY_FFN), 0.02),
        'hy_w2': nrm((DEPTH, HY_FFN, HY_FFN), HY_FFN ** -0.5),
        'hy_b2': nrm((DEPTH, HY_FFN), 0.1),
        'hy_w3': nrm((DEPTH, HY_FFN, 4 * HY_W), HY_FFN ** -0.5),
        'hy_bias': nrm((DEPTH, 2, HY_W), 0.1),
        'ml_gate_b': nrm((DEPTH, 4 * ML_HEADS), 0.1) + ml_off,
        'ml_norm_g': 1.0 + nrm((DEPTH, W), 0.02),
        'rw_mu': 0.5 + nrm((DEPTH, RW_COLS), 0.1),
        'rw_w0': nrm((DEPTH, 2, W), 0.5),
        'rw_w2': nrm((DEPTH, 2, RW_W_RANK, W), 0.1 * RW_W_RANK ** -0.5),
        'rw_a0': nrm((DEPTH, 2, W), 0.5),
        'rw_a2': nrm((DEPTH, 2, RW_A_RANK, W), 0.1 * RW_A_RANK ** -0.5),
        'rw_g2': nrm((DEPTH, RW_G_RANK, W), RW_G_RANK ** -0.5),
        'rw_kk': 1.0 + nrm((DEPTH, W), 0.1),
        'rw_ka': 1.0 + nrm((DEPTH, W), 0.1),
        'rw_rk': nrm((DEPTH, W), 0.1),
        'rw_ln_g': 1.0 + nrm((DEPTH, W), 0.02),
        'rw_ln_b': nrm((DEPTH, W), 0.02),
        'at_qn': 1.0 + nrm((DEPTH, HEAD_DIM), 0.02),
        'at_kn': 1.0 + nrm((DEPTH, HEAD_DIM), 0.02),
        'w_out': nrm((DEPTH, D, D), BETA * D ** -0.5),
        'ln1_g': 1.0 + nrm((DEPTH, D), 0.02),
        'ln1_b': nrm((DEPTH, D), 0.02),
        'mlp_w1': nrm((DEPTH, D, D_FF), D ** -0.5),
        'mlp_w2': nrm((DEPTH, D_FF, D), BETA * D_FF ** -0.5),
        'ln2_g': 1.0 + nrm((DEPTH, D), 0.02),
        'ln2_b': nrm((DEPTH, D), 0.02),
    }


def reference(x_prompt, x_sample, cache_attn_k, cache_attn_v, state_mlstm_C, state_mlstm_n, state_mlstm_m,
              state_rwkv_S, c, c_ctx, w_mod, b_mod, w_in, hy_conv, hy_w1, hy_b1, hy_freq, hy_w2, hy_b2, hy_w3,
              hy_bias, ml_gate_b, ml_norm_g, rw_mu, rw_w0, rw_w2, rw_a0, rw_a2, rw_g2, rw_kk, rw_ka, rw_rk,
              rw_ln_g, rw_ln_b, at_qn, at_kn, w_out, ln1_g, ln1_b, mlp_w1, mlp_w2, ln2_g, ln2_b):
    params = {
        'w_in': w_in, 'hy_conv': hy_conv, 'hy_w1': hy_w1, 'hy_b1': hy_b1, 'hy_freq': hy_freq,
        'hy_w2': hy_w2, 'hy_b2': hy_b2, 'hy_w3': hy_w3, 'hy_bias': hy_bias,
        'ml_gate_b': ml_gate_b, 'ml_norm_g': ml_norm_g,
        'rw_mu': rw_mu, 'rw_w0': rw_w0, 'rw_w2': rw_w2, 'rw_a0': rw_a0, 'rw_a2': rw_a2, 'rw_g2': rw_g2,
        'rw_kk': rw_kk, 'rw_ka': rw_ka, 'rw_rk': rw_rk, 'rw_ln_g': rw_ln_g, 'rw_ln_b': rw_ln_b,
        'at_qn': at_qn, 'at_kn': at_kn, 'w_out': w_out, 'ln1_g': ln1_g, 'ln1_b': ln1_b,
        'mlp_w1': mlp_w1, 'mlp_w2': mlp_w2, 'ln2_g': ln2_g, 'ln2_b': ln2_b,
    }
    x = x_prompt
    ctx_states = []
    for l in range(DEPTH):
        lp = {name: arr[l] for name, arr in params.items()}
        mod = jax.nn.silu(c_ctx)[None, :] @ w_mod[l] + b_mod[l]
        x, st = _layer(x, mod, lp, None)
        ctx_states.append(st)
    y_prompt = x
    new_attn_k = jnp.stack([st[0] for st in ctx_states], axis=1)
    new_attn_v = jnp.stack([st[1] for st in ctx_states], axis=1)
    new_mlstm_C = jnp.stack([st[2] for st in ctx_states], axis=1)
    new_mlstm_n = jnp.stack([st[3] for st in ctx_states], axis=1)
    new_mlstm_m = jnp.stack([st[4] for st in ctx_states], axis=1)
    new_rwkv_S = jnp.stack([st[5] for st in ctx_states], axis=1)
    x = x_sample
    for l in range(DEPTH):
        lp = {name: arr[l] for name, arr in params.items()}
        mod = jax.nn.silu(c) @ w_mod[l] + b_mod[l]
        ctx = (cache_attn_k[:, l], cache_attn_v[:, l], state_mlstm_C[:, l], state_mlstm_n[:, l],
               state_mlstm_m[:, l], state_rwkv_S[:, l])
        x, _ = _layer(x, mod, lp, ctx)
    y_sample = x
    return (y_prompt, y_sample, new_attn_k, new_attn_v, new_mlstm_C, new_mlstm_n, new_mlstm_m, new_rwkv_S)
```

```python
import math
import numpy as np
import concourse.bass as bass
import concourse.mybir as mybir
from concourse.bass_utils import run_bass_kernel_spmd

F32 = mybir.dt.float32
AF = mybir.ActivationFunctionType
ALU = mybir.AluOpType
AX = mybir.AxisListType

D = 1024
DEPTH = 4
GW = 256
HD = 64
P_IN = 3344
OFF_HY, OFF_ML, OFF_RW, OFF_AT = 0, 768, 1808, 2832
D_FF = 4096
ALPHA = (2.0 * DEPTH) ** 0.25
PAST = 512
RW_DECAY = 0.606531
MAGIC = 12582912.0
TWO_PI = 2.0 * math.pi


class T:
    __slots__ = ("h", "key")

    def __init__(self, h, key):
        self.h = h
        self.key = key

    def __getitem__(self, idx):
        return self.h[idx]


class TP(T):
    __slots__ = ("f", "np_")

    def __init__(self, h, key, np_):
        T.__init__(self, h, key)
        self.f = h
        self.np_ = np_

    def __getitem__(self, idx):
        return self.h[0:self.np_][idx]


class FW:
    def __init__(self, n_dma_sems=32):
        self.nc = bass.Bass("TRN2", target_bir_lowering=False)
        nc = self.nc
        self.eng = {"pe": nc.tensor, "dve": nc.vector, "act": nc.scalar, "pool": nc.gpsimd, "sp": nc.sync}
        self.sem = {}
        self.cnt = {}
        self._ctx = []
        for e in self.eng:
            cm = nc.semaphore("s_" + e)
            self.sem[e] = cm.__enter__()
            self.cnt[e] = 0
        cm = nc.semaphore("s_bar")
        self.sem_bar = cm.__enter__()
        self.n_bar = 0
        self.dma_sems = []
        for i in range(n_dma_sems):
            cm = nc.semaphore("d_%d" % i)
            self.dma_sems.append(cm.__enter__())
        self.dma_use = [0] * n_dma_sems
        self.dma_rr = {"hw": 0, "sw": 0}
        self.n_hw = n_dma_sems - 8
        self.waited = {e: {} for e in self.eng}
        self.lastw = {}
        self.readers = {}
        self.uid = 0
        self.n_inst = 0
        self.psum_keys = set()

    def _name(self, name):
        self.uid += 1
        return "%s_%d" % (name, self.uid)

    def sb(self, name, shape, dtype=F32):
        nm = self._name(name)
        cm = self.nc.sbuf_tensor(nm, list(shape), dtype)
        h = cm.__enter__()
        self._ctx.append(cm)
        return T(h, nm)

    def sbp(self, name, shape, dtype=F32):
        nm = self._name(name)
        np_ = shape[0]
        cm = self.nc.sbuf_tensor(nm, [128] + list(shape[1:]), dtype)
        h = cm.__enter__()
        self._ctx.append(cm)
        t = TP(h, nm, np_)
        if np_ == 64:
            self.op("pool", lambda e: e.memset(h[64:128], 0.0), writes=[t])
        else:
            self.op("pool", lambda e: e.memset(h[0:128], 0.0), writes=[t])
        return t

    def ps(self, name, shape, dtype=F32):
        nm = self._name(name)
        cm = self.nc.psum_tensor(nm, list(shape), dtype)
        h = cm.__enter__()
        self._ctx.append(cm)
        self.psum_keys.add(nm)
        return T(h, nm)

    def dram(self, name, shape, dtype=F32, kind="Internal"):
        return self.nc.dram_tensor(name, list(shape), dtype, kind=kind).ap()

    def _wait_token(self, e, tok):
        sk, val = tok
        w = self.waited[e]
        if w.get(sk, 0) >= val:
            return
        w[sk] = val
        if sk == "bar":
            sem = self.sem_bar
        elif isinstance(sk, str):
            sem = self.sem[sk]
        else:
            sem = self.dma_sems[sk]
        self.eng[e].wait_ge(sem, val)

    def _deps(self, e, reads, writes):
        for k in reads:
            for t in self.lastw.get(k, ()):
                if not (e == "pe" and t[0] == "pe"):
                    self._wait_token(e, t)
        for k in writes:
            for t in self.lastw.get(k, ()):
                if not (e == "pe" and t[0] == "pe"):
                    self._wait_token(e, t)
            for t in self.readers.get(k, ()):
                if not (e == "pe" and t[0] == "pe"):
                    self._wait_token(e, t)

    def _record(self, tok, reads, writes):
        for k in writes:
            self.lastw[k] = [tok]
            self.readers[k] = []
        for k in reads:
            if k in writes:
                continue
            self.readers.setdefault(k, []).append(tok)

    @staticmethod
    def _keys(lst):
        return [x.key if isinstance(x, T) else x for x in lst]

    def op(self, e, fn, reads=(), writes=()):
        reads = self._keys(reads)
        writes = self._keys(writes)
        pr = [k for k in reads if k in self.psum_keys and k not in writes]
        if pr:
            reads = [k for k in reads if k not in self.psum_keys]
            writes = writes + pr
        self._deps(e, reads, writes)
        ins = fn(self.eng[e])
        self.cnt[e] += 1
        ins.then_inc(self.sem[e], 1)
        self._record((e, self.cnt[e]), reads, writes)
        self.n_inst += 1

    def dma(self, q, out, in_, reads=(), writes=(), **kw):
        reads = self._keys(reads)
        writes = self._keys(writes)
        self._deps(q, reads, writes)
        if q == "pool":
            i = self.n_hw + self.dma_rr["sw"]
            self.dma_rr["sw"] = (self.dma_rr["sw"] + 1) % (len(self.dma_sems) - self.n_hw)
        else:
            i = self.dma_rr["hw"]
            self.dma_rr["hw"] = (self.dma_rr["hw"] + 1) % self.n_hw
        if self.dma_use[i] > 0:
            self._wait_token(q, (i, 16 * self.dma_use[i]))
        self.dma_use[i] += 1
        ins = self.eng[q].dma_start(out=out, in_=in_, **kw)
        ins.then_inc(self.dma_sems[i], 16)
        self._record((i, 16 * self.dma_use[i]), reads, writes)
        self.n_inst += 1

    def barrier(self):
        sp = "sp"
        for i, u in enumerate(self.dma_use):
            if u > 0:
                self._wait_token(sp, (i, 16 * u))
        for x in ("pe", "dve", "act", "pool"):
            if self.cnt[x] > 0:
                self._wait_token(sp, (x, self.cnt[x]))
        self.n_bar += 1
        self.eng[sp].sem_inc(self.sem_bar, 1)
        for x in ("pe", "dve", "act", "pool"):
            self._wait_token(x, ("bar", self.n_bar))
        for e in self.eng:
            w = self.waited[e]
            for x in self.eng:
                w[x] = self.cnt[x]
            for i, u in enumerate(self.dma_use):
                w[i] = 16 * u
        self.lastw = {}
        self.readers = {}

    def mark(self):
        return len(self._ctx)

    def release(self, mark):
        self.barrier()
        while len(self._ctx) > mark:
            cm = self._ctx.pop()
            cm.__exit__(None, None, None)

    def finish(self):
        self.barrier()


def _bc(ap, n):
    return ap.partition_broadcast(n)


class Kern(FW):
    def __init__(self):
        super().__init__()
        self.ins = {}
        self.outs = {}
        self.banks = None
        self.bank_rr = 0
        self.q_rr = 0

    def inp(self, name, shape):
        ap = self.nc.dram_tensor(name, list(shape), F32, kind="ExternalInput").ap()
        self.ins[name] = ap
        return ap

    def outp(self, name, shape):
        ap = self.nc.dram_tensor(name, list(shape), F32, kind="ExternalOutput").ap()
        self.outs[name] = ap
        return ap

    def init_banks(self):
        self.banks = [self.ps("bank%d" % i, [128, 512]) for i in range(8)]

    def bank(self):
        b = self.banks[self.bank_rr]
        self.bank_rr = (self.bank_rr + 1) % 8
        return b

    def load(self, t, ap_out, ap_in, reads=(), q="sp", **kw):
        self.dma(q, ap_out, ap_in, reads=reads, writes=[t], **kw)

    def store(self, ap_out, ap_in, t, key, q="pool", **kw):
        self.dma(q, ap_out, ap_in, reads=[t], writes=[key], **kw)

    def mm(self, ps, out_ap, lhsT, rhs, reads, start=True, stop=True):
        self.op("pe", lambda e: e.matmul(out_ap, lhsT, rhs, start=start, stop=stop), reads=reads, writes=[ps])

    def tr(self, ps, out_ap, in_ap, ident_ap, reads):
        self.op("pe", lambda e: e.transpose(out_ap, in_ap, ident_ap), reads=list(reads) + [self.ident], writes=[ps])

    def consts(self):
        c = self.inp("c_ident", [128, 128])
        self.ident = self.sb("ident", [128, 128])
        self.load(self.ident, self.ident[:], c)
        self.c_tri = self.inp("c_tri", [4, 128, 128])
        self.tri = self.sb("tri", [128, 4, 128])
        self.load(self.tri, self.tri[:], self.c_tri.rearrange("a p n -> p a n"))
        self.ones = self.sb("ones", [128, 128])
        self.op("dve", lambda e: e.memset(self.ones[:], 1.0), writes=[self.ones])
        self.zrow = self.dram("zrow", [1, 1024])
        mz = self.mark()
        zt = self.sb("zt", [1, 1024])
        self.op("dve", lambda e: e.memset(zt[:], 0.0), writes=[zt])
        self.store(self.zrow, zt[:], zt, "zrow", q="sp")
        self.release(mz)

    def phase_mod(self, cond_ap, wmod_l, bmod_l, tag):
        m0 = self.mark()
        mod = self.sb("mod" + tag, [128, 6 * D])
        m1 = self.mark()
        cT = self.sb("cT", [128, 8])
        self.load(cT, cT[:], cond_ap.rearrange("(k p) -> p k", p=128), allow_slow_non_contiguous=True)
        sg = self.sb("sg", [128, 8])
        self.op("act", lambda e: e.activation(out=sg[:], in_=cT[:], func=AF.Sigmoid), reads=[cT], writes=[sg])
        self.op("dve", lambda e: e.tensor_tensor(sg[:], sg[:], cT[:], ALU.mult), reads=[sg, cT], writes=[sg])
        lb = self.sb("lb", [128, 8, 128])
        self.op("dve", lambda e: e.tensor_copy(lb[:], sg[:].unsqueeze(2).to_broadcast([128, 8, 128])), reads=[sg], writes=[lb])
        bm = self.sb("bm", [128, 6 * D])
        self.load(bm, bm[:], _bc(bmod_l.rearrange("(o n) -> o n", o=1), 128))
        wv = wmod_l.rearrange("(k p) n -> p k n", p=128)
        wts = [self.sb("wm%d" % i, [128, 8, 512]) for i in range(2)]
        for c in range(12):
            wt = wts[c % 2]
            self.load(wt, wt[:], wv[:, :, c * 512:(c + 1) * 512])
            b = self.bank()
            for k in range(8):
                self.mm(b, b[:, :], lb[:, k, :], wt[:, k, :], [lb, wt], start=(k == 0), stop=(k == 7))
            self.op("dve", lambda e: e.tensor_tensor(mod[:, c * 512:(c + 1) * 512], b[:, :], bm[:, c * 512:(c + 1) * 512], ALU.add),
                    reads=[b, bm], writes=[mod])
        for j in (1, 4):
            self.op("dve", lambda e: e.tensor_scalar_add(mod[:, j * D:(j + 1) * D], mod[:, j * D:(j + 1) * D], 1.0), reads=[mod], writes=[mod])
        self.release(m1)
        return mod, m0

    def to_fm(self, src, dstT, col0, reads, width=128):
        for half in range(2):
            b = self.bank()
            for kk in range(4):
                k = half * 4 + kk
                self.tr(b, b[:, kk * 128:(kk + 1) * 128], src[:, k * 128:(k + 1) * 128], self.ident[:], reads)
            eng = "act" if half == 0 else "dve"
            if eng == "act":
                self.op("act", lambda e: e.copy(dstT[:, half * 4:half * 4 + 4, col0:col0 + 128],
                                                 b[:, :].rearrange("p (k n) -> p k n", n=128)), reads=[b], writes=[dstT])
            else:
                self.op("dve", lambda e: e.tensor_copy(dstT[:, half * 4:half * 4 + 4, col0:col0 + 128],
                                                       b[:, :].rearrange("p (k n) -> p k n", n=128)), reads=[b], writes=[dstT])

    def phase_proj(self, x_d, Ltot, mod, w_in_l, p_d, tag):
        m = self.mark()
        nblk = Ltot // 512
        wv = w_in_l.rearrange("(k p) n -> p k n", p=128)
        xs = [self.sb("px%d" % i, [128, D]) for i in range(2)]
        hT = [self.sb("phT%d" % i, [128, 8, 512]) for i in range(2)]
        wts = [self.sb("pw%d" % i, [128, 8, 512]) for i in range(2)]
        po = [self.sb("po%d" % i, [128, 512]) for i in range(2)]
        cols = [(c * 512, min(512, P_IN - c * 512)) for c in range(7)]
        it = 0
        for blk in range(nblk):
            h_t = hT[blk % 2]
            for j in range(4):
                tt = blk * 4 + j
                x = xs[tt % 2]
                self.load(x, x[:], x_d[tt * 128:(tt + 1) * 128, :], reads=[("x" + tag, tt)])
                self.op("dve", lambda e: e.tensor_tensor(x[:], x[:], mod[:, D:2 * D], ALU.mult), reads=[x, mod], writes=[x])
                self.op("pool", lambda e: e.tensor_tensor(x[:], x[:], mod[:, 0:D], ALU.add), reads=[x, mod], writes=[x])
                self.to_fm(x, h_t, j * 128, [x])
            for ci, (c0, cw) in enumerate(cols):
                wt = wts[it % 2]
                it += 1
                self.load(wt, wt[:, :, 0:cw], wv[:, :, c0:c0 + cw])
                for j in range(4):
                    tt = blk * 4 + j
                    b = self.bank()
                    for k in range(8):
                        self.mm(b, b[:, 0:cw], h_t[:, k, j * 128:(j + 1) * 128], wt[:, k, 0:cw], [h_t, wt], start=(k == 0), stop=(k == 7))
                    o = po[(ci * 4 + j) % 2]
                    if j % 2 == 0:
                        self.op("act", lambda e: e.copy(o[:, 0:cw], b[:, 0:cw]), reads=[b], writes=[o])
                    else:
                        self.op("dve", lambda e: e.tensor_copy(o[:, 0:cw], b[:, 0:cw]), reads=[b], writes=[o])
                    self.store(p_d[tt * 128:(tt + 1) * 128, c0:c0 + cw], o[:, 0:cw], o, ("p" + tag, tt, ci))
        self.release(m)

    @staticmethod
    def pkeys(tag, tt, c0, c1):
        return [("p" + tag, tt, ci) for ci in range(c0 // 512, (c1 - 1) // 512 + 1)]


def _attn_phase(self, p_d, tag, nseq, L, lp, mix_d, ctx=None, rope=None, outk=None, outv=None):
    m = self.mark()
    nt = L // 128
    nkc = nt + (4 if ctx is not None else 0)
    Lk = nkc * 128
    QB = min(512, L)
    nq = QB // 128
    qT = self.sbp("qT", [64, 4, L])
    kT = self.sbp("kT", [64, 2, Lk])
    Vx = self.sb("Vx", [128, nkc, 2, 65])
    gq = self.sb("gq", [128, 6, 64])
    g1 = self.sb("g1", [128, 2, 64])
    self.load(g1, g1[:, 0, :], _bc(lp["at_qn"].rearrange("(o n) -> o n", o=1), 128))
    self.load(g1, g1[:, 1, :], _bc(lp["at_kn"].rearrange("(o n) -> o n", o=1), 128))
    self.op("dve", lambda e: e.tensor_copy(gq[:, 0:4, :], g1[:, 0:1, :].to_broadcast([128, 4, 64])), reads=[g1], writes=[gq])
    self.op("dve", lambda e: e.tensor_copy(gq[:, 4:6, :], g1[:, 1:2, :].to_broadcast([128, 2, 64])), reads=[g1], writes=[gq])
    pas = [self.sb("pa%d" % i, [128, 512]) for i in range(2)]
    sqs = [self.sb("sq%d" % i, [128, 384]) for i in range(2)]
    sss = [self.sb("ss%d" % i, [128, 6]) for i in range(2)]
    qks = [self.sb("qk%d" % i, [128, 6, 64]) for i in range(2)]
    if rope is not None:
        cs = [self.sb("rc%d" % i, [128, 2, 64]) for i in range(2)]
        sw = [self.sb("sw%d" % i, [128, 6, 64]) for i in range(2)]
    pts = [self.sb("PT%d" % i, [128, 512]) for i in range(3)]
    yo = [self.sb("yo%d" % i, [128, 4, 64]) for i in range(2)]
    rcp = [self.sb("rcp%d" % i, [128, 4, 1]) for i in range(2)]
    for s in range(nseq):
        self.op("pool", lambda e: e.memset(Vx[:], 1.0), writes=[Vx])
        for tt in range(nt):
            g = s * nt + tt
            i2 = g % 2
            pa, sq, ss, qk = pas[i2], sqs[i2], sss[i2], qks[i2]
            self.load(pa, pa[:], p_d[g * 128:(g + 1) * 128, OFF_AT:OFF_AT + 512], reads=self.pkeys(tag, g, OFF_AT, OFF_AT + 512))
            self.op("act", lambda e: e.activation(out=sq[:], in_=pa[:, 0:384], func=AF.Square), reads=[pa], writes=[sq])
            self.op("dve", lambda e: e.tensor_reduce(ss[:], sq[:].rearrange("p (a b) -> p a b", b=64), AX.X, ALU.add), reads=[sq], writes=[ss])
            self.op("dve", lambda e: e.tensor_scalar(ss[:], ss[:], 1.0 / 64, 1e-6, ALU.mult, ALU.add), reads=[ss], writes=[ss])
            self.op("act", lambda e: e.activation(out=ss[:], in_=ss[:], func=AF.Sqrt), reads=[ss], writes=[ss]); self.op("dve", lambda e: e.reciprocal(ss[:], ss[:]), reads=[ss], writes=[ss])
            self.op("dve", lambda e: e.tensor_tensor(qk[:], pa[:, 0:384].rearrange("p (a b) -> p a b", b=64),
                                                     ss[:].unsqueeze(2).to_broadcast([128, 6, 64]), ALU.mult), reads=[pa, ss], writes=[qk])
            self.op("pool", lambda e: e.tensor_tensor(qk[:], qk[:], gq[:], ALU.mult), reads=[qk, gq], writes=[qk])
            if rope is not None:
                c, w = cs[i2], sw[i2]
                self.load(c, c[:, 0, :], rope[0][tt * 128:(tt + 1) * 128, :])
                self.load(c, c[:, 1, :], rope[1][tt * 128:(tt + 1) * 128, :])
                qv = qk[:].rearrange("p h (a t f) -> p (h a) t f", a=2, t=2)
                wv = w[:].rearrange("p h (a t f) -> p (h a) t f", a=2, t=2)
                self.op("dve", lambda e: e.tensor_copy(wv[:, :, 0, :], qv[:, :, 1, :]), reads=[qk], writes=[w])
                self.op("dve", lambda e: e.tensor_copy(wv[:, :, 1, :], qv[:, :, 0, :]), reads=[qk], writes=[w])
                self.op("dve", lambda e: e.tensor_tensor(qk[:], qk[:], c[:, 0:1, :].to_broadcast([128, 6, 64]), ALU.mult), reads=[qk, c], writes=[qk])
                self.op("pool", lambda e: e.tensor_tensor(w[:], w[:], c[:, 1:2, :].to_broadcast([128, 6, 64]), ALU.mult), reads=[w, c], writes=[w])
                self.op("dve", lambda e: e.tensor_tensor(qk[:], qk[:], w[:], ALU.add), reads=[qk, w], writes=[qk])
            if outk is not None:
                self.store(outk[s][tt * 128:(tt + 1) * 128, :], qk[:, 4:6, :], qk, ("outk", s, tt))
                self.store(outv[s][tt * 128:(tt + 1) * 128, :], pa[:, 384:512], pa, ("outv", s, tt))
            ba = self.banks[(2 * g) % 4]
            for h in range(4):
                self.tr(ba, ba[0:64, h * 128:(h + 1) * 128], qk[:, h, :], self.ident[:], [qk])
            self.op("act", lambda e: e.copy(qT[:, :, tt * 128:(tt + 1) * 128], ba[0:64, :].rearrange("p (h n) -> p h n", n=128)), reads=[ba], writes=[qT])
            bb = self.banks[(2 * g + 1) % 4]
            for h in range(2):
                self.tr(bb, bb[0:64, h * 128:(h + 1) * 128], qk[:, 4 + h, :], self.ident[:], [qk])
            self.op("dve", lambda e: e.tensor_copy(kT[:, :, tt * 128:(tt + 1) * 128], bb[0:64, 0:256].rearrange("p (h n) -> p h n", n=128)), reads=[bb], writes=[kT])
            self.op("pool", lambda e: e.tensor_copy(Vx[:, tt, :, 0:64], pa[:, 384:512].rearrange("p (a b) -> p a b", b=64)), reads=[pa], writes=[Vx])
        if ctx is not None:
            kc_d, vc_d = ctx
            for c in range(4):
                pa = pas[c % 2]
                self.load(pa, pa[:, 0:128], kc_d[c * 128:(c + 1) * 128, :])
                self.load(pa, pa[:, 128:256], vc_d[c * 128:(c + 1) * 128, :])
                bb = self.banks[c % 4]
                for h in range(2):
                    self.tr(bb, bb[0:64, h * 128:(h + 1) * 128], pa[:, h * 64:(h + 1) * 64], self.ident[:], [pa])
                self.op("dve", lambda e: e.tensor_copy(kT[:, :, L + c * 128:L + (c + 1) * 128], bb[0:64, 0:256].rearrange("p (h n) -> p h n", n=128)), reads=[bb], writes=[kT])
                self.op("pool", lambda e: e.tensor_copy(Vx[:, nt + c, :, 0:64], pa[:, 128:256].rearrange("p (a b) -> p a b", b=64)), reads=[pa], writes=[Vx])
        iters = []
        for g in range(2):
            for qb in range(L // QB):
                for hh in range(2):
                    for kc in range(nkc):
                        iters.append((g, qb, 2 * g + hh, kc))
        oas = [self.banks[4 + qs] for qs in range(nq)]

        def emit_S(i):
            g, qb, head, kc = iters[i]
            sb_ = self.banks[i % 4]
            self.mm(sb_, sb_[:, 0:QB], kT.f[:, g, kc * 128:(kc + 1) * 128], qT.f[:, head, qb * QB:(qb + 1) * QB], [kT, qT])

        emit_S(0)
        for i, (g, qb, head, kc) in enumerate(iters):
            if i + 1 < len(iters):
                emit_S(i + 1)
            sb_ = self.banks[i % 4]
            pt = pts[i % 3]
            self.op("act", lambda e: e.activation(out=pt[:, 0:QB], in_=sb_[:, 0:QB], func=AF.Exp, scale=0.125), reads=[sb_], writes=[pt])
            for qs in range(nq):
                self.mm(oas[qs], oas[qs][:, 0:65], pt[:, qs * 128:(qs + 1) * 128], Vx[:, kc, g, :], [pt, Vx],
                        start=(kc == 0), stop=(kc == nkc - 1))
            if kc == nkc - 1:
                u_ = i // nkc
                y, r = yo[u_ % 2], rcp[u_ % 2]
                for qs in range(nq):
                    oa = oas[qs]
                    self.op("dve", lambda e: e.reciprocal(r[:, qs, :], oa[:, 64:65]), reads=[oa], writes=[r])
                    self.op("dve", lambda e: e.tensor_scalar(y[:, qs, :], oa[:, 0:64], r[:, qs, :], None, ALU.mult), reads=[oa, r], writes=[y])
                t0 = s * L + qb * QB
                dst = mix_d[t0:t0 + QB, 768 + head * 64:768 + (head + 1) * 64].rearrange("(q p) c -> p q c", p=128)
                self.store(dst, y[:, 0:nq, :], y, ("mix", tag, "at", s, qb, head))
    self.release(m)


Kern.phase_attn = _attn_phase


def _mlstm_phase(self, p_d, tag, nseq, L, lp, mix_d, st0=None, outs=None):
    m = self.mark()
    nc_ = L // 128
    gb = self.sb("gb", [128, 16])
    self.load(gb, gb[:], _bc(lp["ml_gate_b"].rearrange("(o n) -> o n", o=1), 128))
    ng = self.sb("ng", [128, 256])
    self.load(ng, ng[:], _bc(lp["ml_norm_g"].rearrange("(o n) -> o n", o=1), 128))
    pms = [self.sb("pm%d" % i, [128, 1040]) for i in range(2)]
    Cn = [self.sbp("Cn%d" % i, [64, 4, 65]) for i in range(2)]
    mst = self.sb("mst", [4, 1])
    W = {}
    for nm, shp in (("g", [128, 16]), ("lf", [128, 4]), ("b", [128, 4]), ("Es", [128, 4]), ("eb", [128, 4]), ("ebl", [64, 4]),
                    ("kE", [128, 4, 64]), ("qT", [64, 4, 128]), ("kT", [64, 4, 128]), ("Vx", [128, 4, 65]), ("sT", [128, 4, 128]),
                    ("den", [128, 4, 1]), ("hh", [128, 4, 64]), ("A4", [4, 1]), ("bl4", [4, 1]), ("a", [128, 4]),
                    ("st", [128, 4, 2]), ("hs", [128, 4, 64]), ("sg", [128, 256]), ("sq", [128, 4, 64]), ("y", [128, 256])):
        W[nm] = [(self.sbp if nm in ("qT", "kT") else self.sb)("ml_" + nm + str(i), shp) for i in range(2)]
    hf = self.sb("hf2", [128, nseq * nc_, 256])
    ml_dg = self.sb("ml_dg", [4, 1, 4]); ml_em = self.sb("ml_em", [64, 4])
    for d in range(2):
        if True:
            C = Cn[d]

            def init_state(C=C, d=d):
                if st0 is None:
                    self.op("dve", lambda e: e.memset(C[:], 0.0), writes=[C])
                    self.op("dve", lambda e: e.memset(mst[:], 0.0), writes=[mst])
                else:
                    C0, n0, m0 = st0
                    self.load(C, C[:, :, 0:64], C0[d].rearrange("h d e -> d h e"))
                    self.load(C, C[:, :, 64], n0[d].rearrange("h d -> d h"), allow_slow_non_contiguous=True)
                    em = ml_em
                    self.load(em, em[:], _bc(m0[d:d + 1, :], 64))
                    self.op("act", lambda e: e.activation(out=em[:], in_=em[:], func=AF.Exp), reads=[em], writes=[em])
                    self.op("dve", lambda e: e.tensor_tensor(C[:], C[:], em[:].unsqueeze(2).to_broadcast([64, 4, 65]), ALU.mult), reads=[C, em], writes=[C])

            def out_state(s, C=C, d=d):
                if outs is not None:
                    Co, no, mo = outs[s]
                    dg = ml_dg
                    self.op("dve", lambda e: e.tensor_scalar(dg[0:4, 0, 0:4], self.ident[0:4, 0:4], mst[:, 0:1], None, ALU.mult), reads=[mst, self.ident], writes=[dg])
                    bt = self.bank()
                    self.mm(bt, bt[0:64, 0:4], self.ones[0:4, 0:64], dg[0:4, 0, 0:4], [self.ones, dg])
                    em = ml_em
                    self.op("act", lambda e: e.activation(out=em[:], in_=bt[0:64, 0:4], func=AF.Exp, scale=-1.0), reads=[bt], writes=[em])
                    Co_t = W["hh"][0]
                    self.op("dve", lambda e: e.tensor_tensor(C[:], C[:], em[:].unsqueeze(2).to_broadcast([64, 4, 65]), ALU.mult), reads=[C, em], writes=[C])
                    self.store(Co[d].rearrange("h d e -> d h e"), C[:, :, 0:64], C, ("oC", s, d))
                    self.store(no[d].rearrange("h d -> d h"), C[:, :, 64], C, ("on", s, d), allow_slow_non_contiguous=True)
                    self.store(mo[d].rearrange("(h o) -> h o", o=1), mst[:], mst, ("om", s, d))
                return

            order = range(nc_) if d == 0 else range(nc_ - 1, -1, -1)
            def genA(it_, s, c, d=d, C=C):
                i2 = it_ % 2
                w = {k: v[i2] for k, v in W.items()}
                g = s * nc_ + c
                pm = pms[i2]
                gg, lf, b, Es, eb, ebl = w["g"], w["lf"], w["b"], w["Es"], w["eb"], w["ebl"]
                a = w["a"]
                kE, qT, kT, Vx, sT = w["kE"], w["qT"], w["kT"], w["Vx"], w["sT"]
                self.load(pm, pm[:], p_d[g * 128:(g + 1) * 128, OFF_ML:OFF_ML + 1040], reads=self.pkeys(tag, g, OFF_ML, OFF_ML + 1040))
                gg, lf, b, Es, eb, ebl = w["g"], w["lf"], w["b"], w["Es"], w["eb"], w["ebl"]
                self.op("dve", lambda e: e.tensor_tensor(gg[:], pm[:, 1024:1040], gb[:], ALU.add), reads=[pm, gb], writes=[gg])
                self.op("act", lambda e: e.activation(out=lf[:], in_=gg[:, d * 8 + 4:d * 8 + 8], func=AF.Exp, scale=-1.0), reads=[gg], writes=[lf])
                self.op("act", lambda e: e.activation(out=lf[:], in_=lf[:], func=AF.Ln, bias=1.0), reads=[lf], writes=[lf])
                self.op("dve", lambda e: e.tensor_scalar_mul(lf[:], lf[:], -1.0), reads=[lf], writes=[lf])
                bk = self.bank()
                self.mm(bk, bk[:, 0:4], self.tri[:, d, :], lf[:], [self.tri, lf])
                self.mm(bk, bk[0:64, 8:12], self.ones[:, 0:64], lf[:], [self.ones, lf])
                self.mm(bk, bk[0:4, 16:17], lf[:], self.ones[:, 0:1], [self.ones, lf])
                self.op("dve", lambda e: e.tensor_copy(b[:], bk[:, 0:4]), reads=[bk], writes=[b])
                self.op("act", lambda e: e.activation(out=ebl[:], in_=bk[0:64, 8:12], func=AF.Exp), reads=[bk], writes=[ebl])
                self.op("dve", lambda e: e.tensor_copy(w["bl4"][:], bk[0:4, 16:17]), reads=[bk], writes=[w["bl4"]])
                yield
                a = w["a"]
                self.op("dve", lambda e: e.tensor_tensor(a[:], gg[:, d * 8:d * 8 + 4], b[:], ALU.subtract), reads=[gg, b], writes=[a])
                self.op("act", lambda e: e.activation(out=Es[:], in_=a[:], func=AF.Exp), reads=[a], writes=[Es])
                self.op("act", lambda e: e.activation(out=eb[:], in_=b[:], func=AF.Exp, scale=-1.0), reads=[b], writes=[eb])
                kE, qT, kT, Vx, sT = w["kE"], w["qT"], w["kT"], w["Vx"], w["sT"]
                self.op("pool", lambda e: e.tensor_tensor(kE[:], pm[:, 256:512].rearrange("p (h x) -> p h x", x=64),
                                                          Es[:].unsqueeze(2).to_broadcast([128, 4, 64]), ALU.mult), reads=[pm, Es], writes=[kE])
                self.op("pool", lambda e: e.memset(Vx[:, :, 64:65], 1.0), writes=[Vx])
                self.op("pool", lambda e: e.tensor_copy(Vx[:, :, 0:64], pm[:, 512:768].rearrange("p (h x) -> p h x", x=64)), reads=[pm], writes=[Vx])
                bq = self.bank()
                for h in range(4):
                    self.tr(bq, bq[0:64, h * 128:(h + 1) * 128], pm[:, h * 64:(h + 1) * 64], self.ident[:], [pm])
                self.op("act", lambda e: e.mul(qT[:], bq[0:64, :].rearrange("p (h n) -> p h n", n=128), 0.125), reads=[bq], writes=[qT])
                yield
                bk2 = self.bank()
                for h in range(4):
                    self.tr(bk2, bk2[0:64, h * 128:(h + 1) * 128], pm[:, 256 + h * 64:256 + (h + 1) * 64], self.ident[:], [pm])
                self.op("dve", lambda e: e.tensor_copy(kT[:], bk2[0:64, :].rearrange("p (h n) -> p h n", n=128)), reads=[bk2], writes=[kT])
                yield
                bg = self.bank()
                for h in range(4):
                    self.mm(bg, bg[:, h * 128:(h + 1) * 128], kT.f[:, h, :], qT.f[:, h, :], [kT, qT])
                for h in range(4):
                    self.op("dve", lambda e: e.scalar_tensor_tensor(sT[:, h, :], bg[:, h * 128:(h + 1) * 128], Es[:, h:h + 1], self.tri[:, d, :],
                                                                    ALU.mult, ALU.mult), reads=[bg, Es, self.tri], writes=[sT])
                yield

            def genB(it_, s, c, first, last, d=d, C=C):
                i2 = it_ % 2
                w = {k: v[i2] for k, v in W.items()}
                g = s * nc_ + c
                pm = pms[i2]
                gg, lf, b, Es, eb, ebl = w["g"], w["lf"], w["b"], w["Es"], w["eb"], w["ebl"]
                a = w["a"]
                kE, qT, kT, Vx, sT = w["kE"], w["qT"], w["kT"], w["Vx"], w["sT"]
                hh, den = w["hh"], w["den"]
                if first:
                    init_state()
                    yield
                hh, den = w["hh"], w["den"]
                for h in range(4):
                    bn = self.bank()
                    self.mm(bn, bn[:, 0:65], sT[:, h, :], Vx[:, h, :], [sT, Vx], start=True, stop=False)
                    self.mm(bn, bn[:, 0:65], qT.f[:, h, :], C.f[:, h, :], [qT, C], start=False, stop=True)
                    self.op("act", lambda e: e.activation(out=den[:, h, :], in_=bn[:, 64:65], func=AF.Abs), reads=[bn], writes=[den])
                    self.op("dve", lambda e: e.tensor_tensor(den[:, h, :], den[:, h, :], eb[:, h:h + 1], ALU.max), reads=[den, eb], writes=[den])
                    self.op("dve", lambda e: e.reciprocal(den[:, h, :], den[:, h, :]), reads=[den], writes=[den])
                    self.op("dve", lambda e: e.tensor_scalar(hh[:, h, :], bn[:, 0:64], den[:, h, :], None, ALU.mult), reads=[bn, den], writes=[hh])
                    yield
                for h in range(4):
                    bs = self.bank()
                    self.mm(bs, bs[0:64, 0:65], kE[:, h, :], Vx[:, h, :], [kE, Vx], start=True, stop=False)
                    self.mm(bs, bs[0:64, 0:65], self.ident[:, 0:64], C.f[:, h, :], [self.ident, C], start=False, stop=True)
                    self.op("dve", lambda e: e.tensor_scalar(C[:, h, :], bs[0:64, 0:65], ebl[:, h:h + 1], None, ALU.mult), reads=[bs, ebl], writes=[C])
                    yield
                if outs is not None:
                    bt = self.bank()
                    self.tr(bt, bt[0:4, 0:128], a[:], self.ident[:], [a])
                    self.op("dve", lambda e: e.tensor_reduce(w["A4"][:], bt[0:4, 0:128], AX.X, ALU.max), reads=[bt], writes=[w["A4"]])
                    self.op("dve", lambda e: e.scalar_tensor_tensor(mst[:], mst[:], w["A4"][:, 0:1], w["bl4"][:], ALU.max, ALU.add),
                            reads=[mst, w["A4"], w["bl4"]], writes=[mst])
                if d == 0:
                    self.op("pool", lambda e: e.tensor_copy(hf[:, g, :], hh[:].rearrange("p h x -> p (h x)")), reads=[hh], writes=[hf])
                else:
                    hs, st, sq, sg, y = w["hs"], w["st"], w["sq"], w["sg"], w["y"]
                    self.op("dve", lambda e: e.tensor_tensor(hs[:], hh[:], hf[:, g, :].rearrange("p (h x) -> p h x", x=64), ALU.add), reads=[hh, hf], writes=[hs])
                    self.op("dve", lambda e: e.tensor_reduce(st[:, :, 0], hs[:], AX.X, ALU.add), reads=[hs], writes=[st])
                    self.op("dve", lambda e: e.tensor_scalar_mul(st[:, :, 0], st[:, :, 0], 1.0 / 64), reads=[st], writes=[st])
                    self.op("dve", lambda e: e.tensor_tensor(hs[:], hs[:], st[:, :, 0:1].to_broadcast([128, 4, 64]), ALU.subtract), reads=[hs, st], writes=[hs])
                    self.op("act", lambda e: e.activation(out=sq[:], in_=hs[:], func=AF.Square), reads=[hs], writes=[sq])
                    self.op("dve", lambda e: e.tensor_reduce(st[:, :, 1], sq[:], AX.X, ALU.add), reads=[sq], writes=[st])
                    self.op("dve", lambda e: e.tensor_scalar(st[:, :, 1], st[:, :, 1], 1.0 / 64, 1e-6, ALU.mult, ALU.add), reads=[st], writes=[st])
                    self.op("act", lambda e: e.activation(out=st[:, :, 1], in_=st[:, :, 1], func=AF.Sqrt), reads=[st], writes=[st]); self.op("dve", lambda e: e.reciprocal(st[:, :, 1], st[:, :, 1]), reads=[st], writes=[st])
                    self.op("dve", lambda e: e.tensor_tensor(hs[:], hs[:], st[:, :, 1:2].to_broadcast([128, 4, 64]), ALU.mult), reads=[hs, st], writes=[hs])
                    self.op("act", lambda e: e.activation(out=sg[:], in_=pm[:, 768:1024], func=AF.Sigmoid), reads=[pm], writes=[sg])
                    self.op("pool", lambda e: e.tensor_tensor(sg[:], sg[:], ng[:], ALU.mult), reads=[sg, ng], writes=[sg])
                    self.op("dve", lambda e: e.tensor_tensor(y[:], hs[:].rearrange("p h x -> p (h x)"), sg[:], ALU.mult), reads=[hs, sg], writes=[y])
                    self.store(mix_d[g * 128:(g + 1) * 128, 256:512], y[:], y, ("mix", tag, "ml", g))
                if last:
                    out_state(s)
                yield

            def runN(*gs):
                alive = [x for x in gs if x is not None]
                while alive:
                    for x in list(alive):
                        try:
                            next(x)
                        except StopIteration:
                            alive.remove(x)

            ol = list(order)
            tiles = [(s_, c_) for s_ in range(nseq) for c_ in ol]
            n_ = len(tiles)
            base_it = d * n_
            runN(genA(base_it, *tiles[0]))
            for i_ in range(n_):
                s_, c_ = tiles[i_]
                runN(genA(base_it + i_ + 1, *tiles[i_ + 1]) if i_ + 1 < n_ else None,
                     genB(base_it + i_, s_, c_, c_ == ol[0], c_ == ol[-1]))
    self.release(m)


Kern.phase_mlstm = _mlstm_phase


def _rwkv_phase(self, p_d, tag, nseq, L, lp, mix_d, yf_d, S0=None, outS=None):
    m = self.mark()
    nt = L // 128
    row = lambda ap: ap.rearrange("(o n) -> o n", o=1)
    mu = self.sb("mu", [64, 1024]); self.load(mu, mu[:], _bc(row(lp["rw_mu"]), 64))
    bcs = {}
    for nm in ("rw_kk", "rw_ka", "rw_rk", "rw_ln_g", "rw_ln_b"):
        t = self.sb(nm, [64, 256]); self.load(t, t[:], _bc(row(lp[nm]), 64)); bcs[nm] = t
    w0 = self.sb("w0", [64, 2, 256]); a0 = self.sb("a0", [64, 2, 256])
    w2 = self.sbp("w2", [64, 2, 256]); a2 = self.sbp("a2", [64, 2, 256])
    for d in range(2):
        self.load(w0, w0[:, d, :], _bc(lp["rw_w0"][d:d + 1, :], 64))
        self.load(a0, a0[:, d, :], _bc(lp["rw_a0"][d:d + 1, :], 64))
        self.load(w2, w2[:, d, :], lp["rw_w2"][d])
        self.load(a2, a2[:, d, :], lp["rw_a2"][d])
    g2 = self.sb("g2", [128, 256]); self.load(g2, g2[:], lp["rw_g2"])
    pcs = [self.sbp("pc%d" % i, [64, 2, 1024]) for i in range(3)]
    pp = self.sb("pp", [64, 2, 1024]); pn = self.sb("pn", [64, 2, 1024])
    S = [self.sbp("S%d" % i, [64, 4, 64]) for i in range(2)]
    t2 = lambda nm: self.sbp("rw_" + nm, [64, 2, 256])
    kk, sq, lwt, aa, t1, kt, bv, ecl, encl, At, Rt, yy, sq2 = [t2(n) for n in
        ("kk", "sq", "lwt", "aa", "t1", "kt", "bv", "ecl", "encl", "At", "Rt", "yy", "sq2")]
    kt0, yf = [t2(n) for n in ("kt0", "yf")]
    ss = self.sb("rw_ss", [64, 2, 4]); st = self.sb("rw_st", [64, 8, 2])
    tw = self.sb("rw_tw", [64, 2, 128]); twT = self.sbp("rw_twT", [64, 2, 2, 64])
    sgl = self.sb("rw_sgl", [64, 2, 128]); sgT = self.sb("rw_sgT", [128, 2, 64])
    f3 = lambda nm: self.sbp("rw_" + nm, [64, 8, 64])
    SBUF2 = [[f3(n + str(i)) for i in range(2)] for n in ("Pm", "Nak", "Nbr", "Nkr")]
    DBUF = [[t2(n + str(i)) for i in range(3)] for n in ("Bt", "Kt")] + [[f3(n + str(i)) for i in range(3)] for n in ("AT", "BT", "KT", "RT")] \
        + [[self.sb("rw_wcT%d" % i, [64, 8]) for i in range(3)]] + [[t2(n + str(i)) for i in range(3)] for n in ("gS", "bon")]
    Aj = [f3("Aj%d" % i) for i in range(2)]; AjT = [f3("AjT%d" % i) for i in range(2)]
    Xs = self.sbp("rw_Xs", [64, 4, 64]); UTs = self.sbp("rw_UTs", [64, 4, 64])
    I64 = self.ident[0:64, 0:64]

    def mask(idx):
        return self.tri[0:64, idx, 0:64].unsqueeze(1).to_broadcast([64, 8, 64])

    def hv(t, c2, h):
        return t[:, c2, h * 64:(h + 1) * 64]

    def hvf(t, c2, h):
        return t.f[:, c2, h * 64:(h + 1) * 64]

    I128 = self.ident[:, 0:64]

    def cm(ap_rows):
        return ap_rows.rearrange("(c p) n -> p c n", p=64)

    def mk_alloc(ids):
        st_ = [0]

        def f():
            b_ = self.banks[ids[st_[0] % len(ids)]]
            st_[0] += 1
            return b_
        return f

    bkP, bkS, bkQ = mk_alloc([0, 1, 2]), mk_alloc([3, 4, 5]), mk_alloc([6, 7])

    def mm8(alloc, lh, rh, reads):
        b_ = alloc()
        for u in range(8):
            self.mm(b_, b_[0:64, u * 64:(u + 1) * 64], lh.f[:, u, :], rh.f[:, u, :], reads)
        return b_, b_[0:64, :].rearrange("p (u n) -> p u n", n=64)

    for d in range(2):
        if True:
            Sd = S[d]

            def init_state(Sd=Sd, d=d):
                if S0 is None:
                    self.op("dve", lambda e: e.memset(Sd[:], 0.0), writes=[Sd])
                else:
                    self.load(Xs, Xs[:], S0[d].rearrange("h v k -> v h k"))
                    bk = bkQ()
                    for h in range(4):
                        self.tr(bk, bk[0:64, h * 64:(h + 1) * 64], Xs[:, h, :], I64, [Xs])
                    self.op("dve", lambda e: e.tensor_copy(Sd[:], bk[0:64, 0:256].rearrange("p (h n) -> p h n", n=64)), reads=[bk], writes=[Sd])

            def out_state(s, Sd=Sd, d=d):
                if outS is not None:
                    bk = bkQ()
                    for h in range(4):
                        self.tr(bk, bk[0:64, h * 64:(h + 1) * 64], Sd[:, h, :], I64, [Sd])
                    self.op("dve", lambda e: e.tensor_copy(Xs[:], bk[0:64, 0:256].rearrange("p (h n) -> p h n", n=64)), reads=[bk], writes=[Xs])
                    self.store(outS[s][d].rearrange("h v k -> v h k"), Xs[:], Xs, ("oS", s, d))
                return

            order = range(nt) if d == 0 else range(nt - 1, -1, -1)
            def prep(it_, s, tt, d=d, Sd=Sd):
                g = s * nt + tt
                base = g * 128
                pc = pcs[it_ % 3]
                Bt, Kt, AT, BT, KT, RT, wcT, gS, bon = [x[it_ % 3] for x in DBUF]
                Pm, Nak, Nbr, Nkr = [x[it_ % 2] for x in SBUF2]
                r_, k_, v_ = pc[:, :, 0:256], pc[:, :, 256:512], pc[:, :, 512:768]
                bcv = lambda nm: bcs[nm][:].unsqueeze(1).to_broadcast([64, 2, 256])
                g = s * nt + tt
                base = g * 128
                pc = pcs[it_ % 3]
                C0, C1 = OFF_RW, OFF_RW + 1024
                rk_ = self.pkeys(tag, g, C0, C1)
                self.load(pc, pc[:], cm(p_d[base:base + 128, C0:C1]), reads=rk_)
                if tt == 0:
                    self.load(pp, pp[0:1, 0, :], self.zrow[0:1, 0:1024], reads=["zrow"])
                    self.load(pp, pp[1:64, 0, :], p_d[base:base + 63, C0:C1], reads=rk_)
                    self.load(pp, pp[:, 1, :], p_d[base + 63:base + 127, C0:C1], reads=rk_)
                else:
                    self.load(pp, pp[:], cm(p_d[base - 1:base + 127, C0:C1]), reads=rk_ + self.pkeys(tag, g - 1, C0, C1))
                if tt == nt - 1:
                    self.load(pn, pn[:, 0, :], p_d[base + 1:base + 65, C0:C1], reads=rk_)
                    self.load(pn, pn[0:63, 1, :], p_d[base + 65:base + 128, C0:C1], reads=rk_)
                    self.load(pn, pn[63:64, 1, :], self.zrow[0:1, 0:1024], reads=["zrow"])
                else:
                    self.load(pn, pn[:], cm(p_d[base + 1:base + 129, C0:C1]), reads=rk_ + self.pkeys(tag, g + 1, C0, C1))
                mub = mu[:].unsqueeze(1).to_broadcast([64, 2, 1024])
                self.op("pool", lambda e: e.tensor_tensor(pp[:], pp[:], pn[:], ALU.add), reads=[pp, pn], writes=[pp])
                yield
                self.op("dve", lambda e: e.scalar_tensor_tensor(pp[:], pp[:], 0.5, pc[:], ALU.mult, ALU.subtract), reads=[pp, pc], writes=[pp])
                self.op("pool", lambda e: e.tensor_tensor(pp[:], pp[:], mub, ALU.mult), reads=[pp, mu], writes=[pp])
                yield
                self.op("dve", lambda e: e.tensor_tensor(pc[:], pc[:], pp[:], ALU.add), reads=[pc, pp], writes=[pc])
                r_, k_, v_ = pc[:, :, 0:256], pc[:, :, 256:512], pc[:, :, 512:768]
                bcv = lambda nm: bcs[nm][:].unsqueeze(1).to_broadcast([64, 2, 256])
                self.op("dve", lambda e: e.tensor_tensor(kk[:], k_, bcv("rw_kk"), ALU.mult), reads=[pc, bcs["rw_kk"]], writes=[kk])
                yield
                self.op("act", lambda e: e.activation(out=sq[:], in_=kk[:], func=AF.Square), reads=[kk], writes=[sq])
                self.op("dve", lambda e: e.tensor_reduce(ss[:], sq[:].rearrange("p c (h x) -> p c h x", x=64), AX.X, ALU.add), reads=[sq], writes=[ss])
                yield
                self.op("dve", lambda e: e.tensor_scalar_max(ss[:], ss[:], 1e-24), reads=[ss], writes=[ss])
                self.op("act", lambda e: e.activation(out=ss[:], in_=ss[:], func=AF.Sqrt), reads=[ss], writes=[ss]); self.op("dve", lambda e: e.reciprocal(ss[:], ss[:]), reads=[ss], writes=[ss])
                yield
                self.op("dve", lambda e: e.tensor_tensor(kk[:].rearrange("p c (h x) -> p c h x", x=64), kk[:].rearrange("p c (h x) -> p c h x", x=64),
                                                         ss[:].unsqueeze(3).to_broadcast([64, 2, 4, 64]), ALU.mult), reads=[kk, ss], writes=[kk])
                self.op("act", lambda e: e.activation(out=tw[:, :, 0:64], in_=pc[:, :, 768:832], func=AF.Tanh), reads=[pc], writes=[tw])
                self.op("pool", lambda e: e.tensor_copy(tw[:, :, 64:128], pc[:, :, 832:896]), reads=[pc], writes=[tw])
                yield
                bk = bkP()
                for c2 in range(2):
                    for j in range(2):
                        self.tr(bk, bk[0:64, (c2 * 2 + j) * 64:(c2 * 2 + j + 1) * 64], tw[:, c2, j * 64:(j + 1) * 64], I64, [tw])
                self.op("dve", lambda e: e.tensor_copy(twT[:], bk[0:64, 0:256].rearrange("p (c j n) -> p c j n", j=2, n=64)), reads=[bk], writes=[twT])

                def lowrank(dd, dst_w, dst_a):
                    bw = bkP()
                    for c2 in range(2):
                        self.mm(bw, bw[0:64, c2 * 256:(c2 + 1) * 256], twT.f[:, c2, 0, :], w2.f[:, dd, :], [twT, w2])
                    ba = bkP()
                    for c2 in range(2):
                        self.mm(ba, ba[0:64, c2 * 256:(c2 + 1) * 256], twT.f[:, c2, 1, :], a2.f[:, dd, :], [twT, a2])
                    if dst_w is not None:
                        self.op("dve", lambda e: e.tensor_tensor(dst_w[:], bw[0:64, :].rearrange("p (c n) -> p c n", n=256),
                                                                 w0[:, dd:dd + 1, :].to_broadcast([64, 2, 256]), ALU.add), reads=[bw, w0], writes=[dst_w])
                        self.op("act", lambda e: e.activation(out=dst_w[:], in_=dst_w[:], func=AF.Sigmoid), reads=[dst_w], writes=[dst_w])
                        self.op("dve", lambda e: e.tensor_scalar_mul(dst_w[:], dst_w[:], -RW_DECAY), reads=[dst_w], writes=[dst_w])
                    self.op("dve", lambda e: e.tensor_tensor(dst_a[:], ba[0:64, :].rearrange("p (c n) -> p c n", n=256),
                                                             a0[:, dd:dd + 1, :].to_broadcast([64, 2, 256]), ALU.add), reads=[ba, a0], writes=[dst_a])
                    self.op("act", lambda e: e.activation(out=dst_a[:], in_=dst_a[:], func=AF.Sigmoid), reads=[dst_a], writes=[dst_a])

                def make_kt(dst, a_t):
                    self.op("dve", lambda e: e.scalar_tensor_tensor(t1[:], a_t[:], -1.0, bcv("rw_ka"), ALU.add, ALU.mult), reads=[a_t, bcs["rw_ka"]], writes=[t1])
                    self.op("dve", lambda e: e.scalar_tensor_tensor(dst[:], t1[:], 1.0, k_, ALU.add, ALU.mult), reads=[t1, pc], writes=[dst])

                if d == 1:
                    lowrank(0, None, aa)
                    make_kt(kt0, aa)
                lowrank(d, lwt, aa)
                yield
                make_kt(kt, aa)
                self.op("pool", lambda e: e.tensor_tensor(bv[:], kk[:], aa[:], ALU.mult), reads=[kk, aa], writes=[bv])
                yield
                bc_ = bkP()
                self.mm(bc_, bc_[0:64, :], self.tri[:, d, 0:64], lwt.f[:].rearrange("p c n -> p (c n)"), [self.tri, lwt])
                clv = bc_[0:64, :].rearrange("p (c n) -> p c n", n=256)
                self.op("act", lambda e: e.activation(out=ecl[:], in_=clv, func=AF.Exp), reads=[bc_], writes=[ecl])
                self.op("act", lambda e: e.activation(out=encl[:], in_=clv, func=AF.Exp, scale=-1.0), reads=[bc_], writes=[encl])
                yield
                self.op("dve", lambda e: e.tensor_tensor(t1[:], clv, lwt[:], ALU.subtract), reads=[bc_, lwt], writes=[t1])
                self.op("act", lambda e: e.activation(out=t1[:], in_=t1[:], func=AF.Exp), reads=[t1], writes=[t1])
                yield
                self.op("dve", lambda e: e.scalar_tensor_tensor(At[:], t1[:], -1.0, kk[:], ALU.mult, ALU.mult), reads=[t1, kk], writes=[At])
                self.op("pool", lambda e: e.tensor_tensor(Bt[:], bv[:], encl[:], ALU.mult), reads=[bv, encl], writes=[Bt])
                yield
                self.op("dve", lambda e: e.tensor_tensor(Kt[:], kt[:], encl[:], ALU.mult), reads=[kt, encl], writes=[Kt])
                self.op("pool", lambda e: e.tensor_tensor(Rt[:], r_, ecl[:], ALU.mult), reads=[pc, ecl], writes=[Rt])
                yield
                bwc = bkP()
                for c2 in range(2):
                    for h in range(4):
                        u = c2 * 4 + h
                        self.mm(bwc, bwc[0:64, u:u + 1], hvf(lwt, c2, h), self.ones[:, 0:1], [lwt, self.ones])
                self.op("act", lambda e: e.activation(out=wcT[:], in_=bwc[0:64, 0:8], func=AF.Exp), reads=[bwc], writes=[wcT])
                for src, dst, eng in ((At, AT, "dve"), (Bt, BT, "act"), (Kt, KT, "dve"), (Rt, RT, "act")):
                    bt_ = bkP()
                    for c2 in range(2):
                        for h in range(4):
                            u = c2 * 4 + h
                            self.tr(bt_, bt_[0:64, u * 64:(u + 1) * 64], hv(src, c2, h), I64, [src])
                    v3 = bt_[0:64, :].rearrange("p (u n) -> p u n", n=64)
                    if eng == "dve":
                        self.op("dve", lambda e: e.tensor_copy(dst[:], v3), reads=[bt_], writes=[dst])
                    else:
                        self.op("act", lambda e: e.copy(dst[:], v3), reads=[bt_], writes=[dst])


                if d == 1:
                    self.op("act", lambda e: e.activation(out=sgl[:], in_=pc[:, :, 896:1024], func=AF.Sigmoid), reads=[pc], writes=[sgl])
                    bk = bkP()
                    for c2 in range(2):
                        self.tr(bk, bk[:, c2 * 64:(c2 + 1) * 64], sgl[:, c2, :], I64, [sgl])
                    self.op("dve", lambda e: e.tensor_copy(sgT[:], bk[:, 0:128].rearrange("p (c n) -> p c n", n=64)), reads=[bk], writes=[sgT])
                    bgm = bkP()
                    for c2 in range(2):
                        self.mm(bgm, bgm[0:64, c2 * 256:(c2 + 1) * 256], sgT[:, c2, :], g2[:], [sgT, g2])
                    self.op("act", lambda e: e.copy(gS[:], bgm[0:64, :].rearrange("p (c n) -> p c n", n=256)), reads=[bgm], writes=[gS])
                    self.op("pool", lambda e: e.tensor_tensor(kt0[:], kt0[:], kt[:], ALU.add), reads=[kt0, kt], writes=[kt0])
                    self.op("pool", lambda e: e.tensor_tensor(kt0[:], kt0[:], bcv("rw_rk"), ALU.mult), reads=[kt0, bcs["rw_rk"]], writes=[kt0])
                    self.op("dve", lambda e: e.tensor_tensor(kt0[:], kt0[:], r_, ALU.mult), reads=[kt0, pc], writes=[kt0])
                    self.op("dve", lambda e: e.tensor_reduce(ss[:], kt0[:].rearrange("p c (h x) -> p c h x", x=64), AX.X, ALU.add), reads=[kt0], writes=[ss])
                    self.op("dve", lambda e: e.tensor_tensor(bon[:].rearrange("p c (h x) -> p c h x", x=64), v_.rearrange("p c (h x) -> p c h x", x=64),
                                                             ss[:].unsqueeze(3).to_broadcast([64, 2, 4, 64]), ALU.mult), reads=[pc, ss], writes=[bon])
                yield

            def solve(it_, s, tt, d=d, Sd=Sd):
                g = s * nt + tt
                base = g * 128
                pc = pcs[it_ % 3]
                Bt, Kt, AT, BT, KT, RT, wcT, gS, bon = [x[it_ % 3] for x in DBUF]
                Pm, Nak, Nbr, Nkr = [x[it_ % 2] for x in SBUF2]
                r_, k_, v_ = pc[:, :, 0:256], pc[:, :, 256:512], pc[:, :, 512:768]
                bcv = lambda nm: bcs[nm][:].unsqueeze(1).to_broadcast([64, 2, 256])
                sm, smT, im_ = (2, 3, 0) if d == 0 else (3, 2, 1)
                b_, v3 = mm8(bkS, BT, AT, [BT, AT])
                yield
                self.op("dve", lambda e: e.tensor_tensor(Aj[0][:], v3, mask(sm), ALU.mult), reads=[b_, self.tri], writes=[Aj[0]])
                yield
                b_, v3 = mm8(bkS, AT, BT, [BT, AT])
                yield
                self.op("dve", lambda e: e.tensor_tensor(AjT[0][:], v3, mask(smT), ALU.mult), reads=[b_, self.tri], writes=[AjT[0]])
                yield
                self.op("pool", lambda e: e.tensor_tensor(Pm[:], Aj[0][:], I64.unsqueeze(1).to_broadcast([64, 8, 64]), ALU.add), reads=[Aj[0], self.ident], writes=[Pm])
                yield
                b_, v3 = mm8(bkS, KT, AT, [KT, AT])
                yield
                self.op("dve", lambda e: e.tensor_tensor(Nak[:], v3, mask(sm), ALU.mult), reads=[b_, self.tri], writes=[Nak])
                yield
                b_, v3 = mm8(bkS, BT, RT, [BT, RT])
                yield
                self.op("dve", lambda e: e.tensor_tensor(Nbr[:], v3, mask(im_), ALU.mult), reads=[b_, self.tri], writes=[Nbr])
                yield
                b_, v3 = mm8(bkS, KT, RT, [KT, RT])
                yield
                self.op("dve", lambda e: e.tensor_tensor(Nkr[:], v3, mask(im_), ALU.mult), reads=[b_, self.tri], writes=[Nkr])
                yield
                for j in range(1, 6):
                    pa_, pat = Aj[(j - 1) % 2], AjT[(j - 1) % 2]
                    na, nat = Aj[j % 2], AjT[j % 2]
                    b1, v1 = mm8(bkS, pa_, pat, [pa_, pat])
                    if j < 5:
                        b2, v2 = mm8(bkS, pat, pa_, [pa_, pat])
                    self.op("act", lambda e: e.copy(nat[:], v1), reads=[b1], writes=[nat])
                    if j < 5:
                        self.op("dve", lambda e: e.tensor_copy(na[:], v2), reads=[b2], writes=[na])
                    b3, v3 = mm8(bkS, nat, Pm, [nat, Pm])
                    self.op("dve", lambda e: e.tensor_tensor(Pm[:], v3, Pm[:], ALU.add), reads=[b3, Pm], writes=[Pm])
                    yield
                yield

            def seq(it_, s, tt, first, last, d=d, Sd=Sd):
                g = s * nt + tt
                base = g * 128
                pc = pcs[it_ % 3]
                Bt, Kt, AT, BT, KT, RT, wcT, gS, bon = [x[it_ % 3] for x in DBUF]
                Pm, Nak, Nbr, Nkr = [x[it_ % 2] for x in SBUF2]
                r_, k_, v_ = pc[:, :, 0:256], pc[:, :, 256:512], pc[:, :, 512:768]
                bcv = lambda nm: bcs[nm][:].unsqueeze(1).to_broadcast([64, 2, 256])
                if first:
                    init_state()
                    yield
                for c2 in ((0, 1) if d == 0 else (1, 0)):
                    bx = bkQ()
                    for h in range(4):
                        u = c2 * 4 + h
                        o = bx[0:64, h * 64:(h + 1) * 64]
                        self.mm(bx, o, AT.f[:, u, :], Sd.f[:, h, :], [AT, Sd], start=True, stop=False)
                        self.mm(bx, o, Nak.f[:, u, :], hvf(pc, c2, 8 + h), [Nak, pc], start=False, stop=True)
                    self.op("act", lambda e: e.copy(Xs[:], bx[0:64, 0:256].rearrange("p (h n) -> p h n", n=64)), reads=[bx], writes=[Xs])
                    yield
                    bu = bkQ()
                    for h in range(4):
                        u = c2 * 4 + h
                        self.mm(bu, bu[0:64, h * 64:(h + 1) * 64], Pm.f[:, u, :], Xs.f[:, h, :], [Pm, Xs])
                    self.op("dve", lambda e: e.tensor_copy(UTs[:], bu[0:64, 0:256].rearrange("p (h n) -> p h n", n=64)), reads=[bu], writes=[UTs])
                    yield
                    by = bkQ()
                    for h in range(4):
                        u = c2 * 4 + h
                        o = by[0:64, h * 64:(h + 1) * 64]
                        self.mm(by, o, RT.f[:, u, :], Sd.f[:, h, :], [RT, Sd], start=True, stop=False)
                        self.mm(by, o, Nbr.f[:, u, :], UTs.f[:, h, :], [Nbr, UTs], start=False, stop=False)
                        self.mm(by, o, Nkr.f[:, u, :], hvf(pc, c2, 8 + h), [Nkr, pc], start=False, stop=True)
                    self.op("act", lambda e: e.copy(yy[:, c2, :], by[0:64, 0:256]), reads=[by], writes=[yy])
                    yield
                    bs = bkQ()
                    for h in range(4):
                        o = bs[0:64, h * 64:(h + 1) * 64]
                        self.mm(bs, o, I128, Sd.f[:, h, :], [self.ident, Sd], start=True, stop=False)
                        self.mm(bs, o, hvf(Bt, c2, h), UTs.f[:, h, :], [Bt, UTs], start=False, stop=False)
                        self.mm(bs, o, hvf(Kt, c2, h), hvf(pc, c2, 8 + h), [Kt, pc], start=False, stop=True)
                    self.op("dve", lambda e: e.tensor_tensor(Sd[:], bs[0:64, 0:256].rearrange("p (h n) -> p h n", n=64),
                                                             wcT[:, c2 * 4:(c2 + 1) * 4].unsqueeze(2).to_broadcast([64, 4, 64]), ALU.mult), reads=[bs, wcT], writes=[Sd])
                if d == 0:
                    self.store(cm(yf_d[base:base + 128, :]), yy[:], yy, ("yf", tag, g))
                else:
                    self.load(yf, yf[:], cm(yf_d[base:base + 128, :]), reads=[("yf", tag, g)])
                    self.op("dve", lambda e: e.tensor_tensor(yy[:], yy[:], yf[:], ALU.add), reads=[yy, yf], writes=[yy])
                    y4 = yy[:].rearrange("p c (h x) -> p (c h) x", x=64)
                    self.op("dve", lambda e: e.tensor_reduce(st[:, :, 0], y4, AX.X, ALU.add), reads=[yy], writes=[st])
                    self.op("dve", lambda e: e.tensor_scalar_mul(st[:, :, 0], st[:, :, 0], 1.0 / 64), reads=[st], writes=[st])
                    self.op("dve", lambda e: e.tensor_tensor(y4, y4, st[:, :, 0:1].to_broadcast([64, 8, 64]), ALU.subtract), reads=[yy, st], writes=[yy])
                    self.op("act", lambda e: e.activation(out=sq2[:], in_=yy[:], func=AF.Square), reads=[yy], writes=[sq2])
                    self.op("dve", lambda e: e.tensor_reduce(st[:, :, 1], sq2[:].rearrange("p c (h x) -> p (c h) x", x=64), AX.X, ALU.add), reads=[sq2], writes=[st])
                    self.op("dve", lambda e: e.tensor_scalar(st[:, :, 1], st[:, :, 1], 1.0 / 64, 64e-5, ALU.mult, ALU.add), reads=[st], writes=[st])
                    self.op("act", lambda e: e.activation(out=st[:, :, 1], in_=st[:, :, 1], func=AF.Sqrt), reads=[st], writes=[st]); self.op("dve", lambda e: e.reciprocal(st[:, :, 1], st[:, :, 1]), reads=[st], writes=[st])
                    self.op("dve", lambda e: e.tensor_tensor(y4, y4, st[:, :, 1:2].to_broadcast([64, 8, 64]), ALU.mult), reads=[yy, st], writes=[yy])
                    self.op("pool", lambda e: e.tensor_tensor(yy[:], yy[:], bcv("rw_ln_g"), ALU.mult), reads=[yy, bcs["rw_ln_g"]], writes=[yy])
                    self.op("dve", lambda e: e.tensor_tensor(yy[:], yy[:], bcv("rw_ln_b"), ALU.add), reads=[yy, bcs["rw_ln_b"]], writes=[yy])
                    self.op("pool", lambda e: e.tensor_tensor(yy[:], yy[:], bon[:], ALU.add), reads=[yy, bon], writes=[yy])
                    self.op("dve", lambda e: e.tensor_tensor(yy[:], yy[:], gS[:], ALU.mult), reads=[yy, gS], writes=[yy])
                    self.store(cm(mix_d[base:base + 128, 512:768]), yy[:], yy, ("mix", tag, "rw", g))
                if last:
                    out_state(s)
                yield

            def runN(*gs):
                alive = [x for x in gs if x is not None]
                while alive:
                    for x in list(alive):
                        try:
                            next(x)
                        except StopIteration:
                            alive.remove(x)

            ol = list(order)
            tiles = [(s_, tt_) for s_ in range(nseq) for tt_ in ol]
            n_ = len(tiles)
            mk = lambda f, i: f(i, *tiles[i]) if i < n_ else None
            runN(mk(prep, 0))
            runN(mk(prep, 1), mk(solve, 0))
            for i_ in range(n_):
                s_, tt_ = tiles[i_]
                runN(mk(prep, i_ + 2), mk(solve, i_ + 1), seq(i_, s_, tt_, tt_ == ol[0], tt_ == ol[-1]))
    self.release(m)


Kern.phase_rwkv = _rwkv_phase


def hy_consts_np(L):
    N = 2 * L
    N1 = N // 128
    S1 = N1 // 2
    s1 = np.arange(S1)[:, None, None]
    s2 = np.arange(128)[None, :, None]
    f1 = np.arange(N1)[None, None, :]
    ang = 2 * np.pi * ((f1 * (128 * s1 + s2)) % N) / N
    mA = np.stack([np.cos(ang), -np.sin(ang)], axis=2).astype(np.float32)
    angD = np.transpose(ang, (2, 1, 0))
    mD = np.stack([np.cos(angD), -np.sin(angD)], axis=2).astype(np.float32)
    k = np.arange(128)
    a2 = 2 * np.pi * ((k[:, None] * k[None, :]) % 128) / 128
    cs = np.stack([np.cos(a2), np.sin(a2), -np.sin(a2)], axis=0).astype(np.float32)
    t = np.linspace(0.0, 1.0, L, dtype=np.float32)[:, None]
    angz = (2.0 * np.pi / L) * np.arange(L, dtype=np.float32)[:, None]
    bands = np.linspace(1e-4, 15, 16, dtype=np.float32)[None, :]
    z = np.concatenate([t, np.cos(bands * angz), np.sin(bands * angz)], axis=-1).astype(np.float32)
    mn, mx = math.log(1e-2) / 1.5, math.log(1e-2) / 0.3
    deltas = np.abs(np.linspace(mn, mx, 256, dtype=np.float32))
    dec = np.exp(-t * deltas[None, :]).astype(np.float32)
    return {"mA": mA, "mD": mD, "cs": cs, "zT": np.ascontiguousarray(z.T), "dec": dec}


def _hyena_phase(self, p_d, tag, nseq, L, lp, mix_d, hc, scr):
    N = 2 * L
    N1 = N // 128
    S1 = N1 // 2
    TB = 8
    FB = 2
    row = lambda ap: ap.rearrange("(o n) -> o n", o=1)
    m = self.mark()
    mAs = [self.sbp("mA%d" % i, [S1, TB, 2, N1]) for i in range(2)]
    mDs = [self.sbp("mD%d" % i, [N1, TB, 2, S1]) for i in range(2)]
    cs = self.sb("cs", [128, 3, 128]); self.load(cs, cs[:], hc["cs"].rearrange("a p n -> p a n"))
    ua = [self.sbp("hy_u%d" % i, [S1, TB, 256]) for i in range(2)]
    ea = [self.sb("hy_ea%d" % i, [N1, 512]) for i in range(2)]
    rn = self.sb("hy_rn", [128, 1024])
    bias2 = self.sb("hy_bias", [1, 512]); self.load(bias2, bias2[:], lp["hy_bias"].rearrange("(o f) c -> o (f c)", o=1))
    Bds, Fd, Dd = [scr["Bd"], scr["Bd2"]], scr["Fd"], scr["Dd"]

    def stageA(src_rows, c0, kin, filt_k=None, bi=0):
        src3 = src_rows[:, c0:c0 + 256].rearrange("(a b) c -> a b c", b=128)
        Bd = Bds[bi]
        for blk in range(128 // TB):
            u = ua[blk % 2]
            self.load(u, u[:], src3[:, blk * TB:(blk + 1) * TB, :], reads=kin)
            mA = mAs[blk % 2]
            self.load(mA, mA[:], hc["mA"][:, blk * TB:(blk + 1) * TB, :, :])
            if filt_k is not None:
                self.op("dve", lambda e: e.tensor_tensor(u[:], u[:], rn[0:S1, filt_k * 256:(filt_k + 1) * 256].unsqueeze(1).to_broadcast([S1, TB, 256]), ALU.mult),
                        reads=[u, rn], writes=[u])
                if filt_k < 2 and blk == 0:
                    self.op("dve", lambda e: e.tensor_tensor(u[0:1, 0, :], u[0:1, 0, :], bias2[0:1, filt_k * 256:(filt_k + 1) * 256], ALU.add),
                            reads=[u, bias2], writes=[u])
            for j in range(TB):
                s2 = blk * TB + j
                b_ = self.bank()
                self.mm(b_, b_[0:N1, 0:256], mA.f[:, j, 0, :], u.f[:, j, :], [mA, u])
                self.mm(b_, b_[0:N1, 256:512], mA.f[:, j, 1, :], u.f[:, j, :], [mA, u])
                e_ = ea[s2 % 2]
                if s2 % 2 == 0:
                    self.op("act", lambda e: e.copy(e_[:], b_[0:N1, :]), reads=[b_], writes=[e_])
                else:
                    self.op("dve", lambda e: e.tensor_copy(e_[:], b_[0:N1, :]), reads=[b_], writes=[e_])
                self.store(Bd[s2].rearrange("f r c -> f (r c)"), e_[:], e_, ("Bd", bi, s2))
                yield

    bb = [self.sb("hy_bb%d" % i, [128, FB, 2, 256]) for i in range(2)]
    ff = [self.sb("hy_ff%d" % i, [128, FB, 2, 256]) for i in range(2)]
    uu = [self.sb("hy_uu%d" % i, [128, 2, FB, 256]) for i in range(2)]
    yy = [self.sb("hy_yy%d" % i, [128, 2, FB, 256]) for i in range(2)]
    tm = [self.sb("hy_tm%d" % i, [128, FB, 256]) for i in range(2)]
    co = [self.sb("hy_co%d" % i, [128, FB, 2, 256]) for i in range(2)]

    def stageBC(filt_out=None, filt_in=None, bi=0):
        Bd = Bds[bi]
        for ib, f0 in enumerate(range(0, N1, FB)):
            b = bb[ib % 2]
            self.load(b, b[:], Bd[:, f0:f0 + FB, :, :], reads=[("Bd", bi, s2) for s2 in range(128)])
            bre = b[:, :, 0, :]
            bim = b[:, :, 1, :]
            p_re = self.bank()
            self.mm(p_re, p_re[:, :].rearrange("p (f c) -> p f c", c=256), cs[:, 0, :], bre, [cs, b], start=True, stop=False)
            self.mm(p_re, p_re[:, :].rearrange("p (f c) -> p f c", c=256), cs[:, 1, :], bim, [cs, b], start=False, stop=True)
            p_im = self.bank()
            self.mm(p_im, p_im[:, :].rearrange("p (f c) -> p f c", c=256), cs[:, 0, :], bim, [cs, b], start=True, stop=False)
            self.mm(p_im, p_im[:, :].rearrange("p (f c) -> p f c", c=256), cs[:, 2, :], bre, [cs, b], start=False, stop=True)
            u = uu[ib % 2]
            self.op("act", lambda e: e.copy(u[:, 0, :, :], p_re[:, :].rearrange("p (f c) -> p f c", c=256)), reads=[p_re], writes=[u])
            self.op("dve", lambda e: e.tensor_copy(u[:, 1, :, :], p_im[:, :].rearrange("p (f c) -> p f c", c=256)), reads=[p_im], writes=[u])
            if filt_out is not None:
                self.store(Fd[filt_out][:, f0:f0 + FB, 0, :], u[:, 0, :, :], u, ("Fd", filt_out, f0, 0))
                self.store(Fd[filt_out][:, f0:f0 + FB, 1, :], u[:, 1, :, :], u, ("Fd", filt_out, f0, 1))
                yield
                continue
            f = ff[ib % 2]
            self.load(f, f[:, :, 0, :], Fd[filt_in][:, f0:f0 + FB, 0, :], reads=[("Fd", filt_in, f0, 0)])
            self.load(f, f[:, :, 1, :], Fd[2 + filt_in][:, f0:f0 + FB, 1, :], reads=[("Fd", 2 + filt_in, f0, 1)])
            y = yy[ib % 2]
            t_ = tm[ib % 2]
            self.op("dve", lambda e: e.tensor_tensor(y[:, 0], u[:, 0], f[:, :, 0, :], ALU.mult), reads=[u, f], writes=[y])
            self.op("pool", lambda e: e.tensor_tensor(t_[:], u[:, 1], f[:, :, 1, :], ALU.mult), reads=[u, f], writes=[t_])
            self.op("dve", lambda e: e.tensor_tensor(y[:, 0], y[:, 0], t_[:], ALU.subtract), reads=[y, t_], writes=[y])
            self.op("pool", lambda e: e.tensor_tensor(y[:, 1], u[:, 0], f[:, :, 1, :], ALU.mult), reads=[u, f], writes=[y])
            self.op("dve", lambda e: e.tensor_tensor(t_[:], u[:, 1], f[:, :, 0, :], ALU.mult), reads=[u, f], writes=[t_])
            self.op("pool", lambda e: e.tensor_tensor(y[:, 1], y[:, 1], t_[:], ALU.add), reads=[y, t_], writes=[y])
            q_re = self.bank()
            self.mm(q_re, q_re[:, :].rearrange("p (f c) -> p f c", c=256), cs[:, 0, :], y[:, 0], [cs, y], start=True, stop=False)
            self.mm(q_re, q_re[:, :].rearrange("p (f c) -> p f c", c=256), cs[:, 2, :], y[:, 1], [cs, y], start=False, stop=True)
            q_im = self.bank()
            self.mm(q_im, q_im[:, :].rearrange("p (f c) -> p f c", c=256), cs[:, 0, :], y[:, 1], [cs, y], start=True, stop=False)
            self.mm(q_im, q_im[:, :].rearrange("p (f c) -> p f c", c=256), cs[:, 1, :], y[:, 0], [cs, y], start=False, stop=True)
            c_ = co[ib % 2]
            self.op("act", lambda e: e.copy(c_[:, :, 0, :], q_re[:, :].rearrange("p (f c) -> p f c", c=256)), reads=[q_re], writes=[c_])
            self.op("dve", lambda e: e.tensor_copy(c_[:, :, 1, :], q_im[:, :].rearrange("p (f c) -> p f c", c=256)), reads=[q_im], writes=[c_])
            self.store(Dd[f0:f0 + FB].rearrange("f t r c -> t f r c"), c_[:], c_, ("Dd", f0))
            yield

    dl = [self.sbp("hy_dl%d" % i, [N1, TB, 2, 256]) for i in range(2)]
    gl = [self.sb("hy_gl%d" % i, [S1, TB, 256]) for i in range(2)]
    yo = [self.sb("hy_yo%d" % i, [S1, TB, 256]) for i in range(2)]

    def stageD(gate_rows, gc0, gkeys, dst_rows, dc0, dkey):
        g3 = gate_rows[:, gc0:gc0 + 256].rearrange("(a b) c -> a b c", b=128)
        d3 = dst_rows[:, dc0:dc0 + 256].rearrange("(a b) c -> a b c", b=128)
        for blk in range(128 // TB):
            dt_ = dl[blk % 2]
            self.load(dt_, dt_[:], Dd[:, blk * TB:(blk + 1) * TB, :, :], reads=[("Dd", f0) for f0 in range(0, N1, FB)])
            g_ = gl[blk % 2]
            self.load(g_, g_[:], g3[:, blk * TB:(blk + 1) * TB, :], reads=gkeys)
            mD = mDs[blk % 2]
            self.load(mD, mD[:], hc["mD"][:, blk * TB:(blk + 1) * TB, :, :])
            o_ = yo[blk % 2]
            for j in range(0, TB, 2):
                b_ = self.bank()
                for jj in range(2):
                    t2 = blk * TB + j + jj
                    o = b_[0:S1, jj * 256:(jj + 1) * 256]
                    self.mm(b_, o, mD.f[:, j + jj, 0, :], dt_.f[:, j + jj, 0, :], [mD, dt_], start=True, stop=False)
                    self.mm(b_, o, mD.f[:, j + jj, 1, :], dt_.f[:, j + jj, 1, :], [mD, dt_], start=False, stop=True)
                self.op("dve", lambda e: e.scalar_tensor_tensor(o_[:, j:j + 2, :], b_[0:S1, :].rearrange("p (t c) -> p t c", c=256), 1.0 / N,
                                                                g_[:, j:j + 2, :], ALU.mult, ALU.mult), reads=[b_, g_], writes=[o_])
            self.store(d3[:, blk * TB:(blk + 1) * TB, :], o_[:], o_, (dkey, blk))

    mf = self.mark()
    w1 = self.sb("hw1", [33, 64]); self.load(w1, w1[:], lp["hy_w1"])
    w2_ = self.sb("hw2", [64, 64]); self.load(w2_, w2_[:], lp["hy_w2"])
    w3 = self.sb("hw3", [64, 1024]); self.load(w3, w3[:], lp["hy_w3"])
    col = lambda ap: ap.rearrange("(n o) -> n o", o=1)
    b1 = self.sb("hb1", [64, 1]); self.load(b1, b1[:], col(lp["hy_b1"]))
    b2 = self.sb("hb2", [64, 1]); self.load(b2, b2[:], col(lp["hy_b2"]))
    fr = self.sb("hfr", [64, 1]); self.load(fr, fr[:], col(lp["hy_freq"]))
    zT = self.sb("hzT", [33, 512]); hid = self.sb("hhid", [64, 512]); nq = self.sb("hnq", [64, 512]); hid2 = self.sb("hhid2", [64, 512])
    dc = self.sb("hdec", [128, 256])
    ht = self.sb("hht", [128, 2, 2, 256])
    g12 = self.sb("hg12", [128, 2, 2, 256])
    ab = self.sb("hab", [128, 2, 256])
    pab = self.banks[7]
    gd = scr["gd"]

    def sin_layer(dst, ps, bb_, w):
        self.op("dve", lambda e: e.tensor_scalar(dst[:, 0:w], ps, bb_[:, 0:1], fr[:, 0:1], ALU.add, ALU.mult), reads=[ps_t, bb_, fr], writes=[dst])
        self.op("dve", lambda e: e.tensor_scalar(nq[:, 0:w], dst[:, 0:w], 1.0 / TWO_PI, MAGIC, ALU.mult, ALU.add), reads=[dst], writes=[nq])
        self.op("dve", lambda e: e.tensor_scalar(nq[:, 0:w], nq[:, 0:w], MAGIC, -TWO_PI, ALU.subtract, ALU.mult), reads=[nq], writes=[nq])
        self.op("dve", lambda e: e.tensor_tensor(dst[:, 0:w], dst[:, 0:w], nq[:, 0:w], ALU.add), reads=[dst, nq], writes=[dst])
        self.op("act", lambda e: e.activation(out=dst[:, 0:w], in_=dst[:, 0:w], func=AF.Sin), reads=[dst], writes=[dst])

    ntile = L // 128
    for tb in range((L + 511) // 512):
        w = min(512, L - tb * 512)
        self.load(zT, zT[:, 0:w], hc["zT"][:, tb * 512:tb * 512 + w])
        ps_t = self.banks[0]
        self.mm(ps_t, ps_t[0:64, 0:w], w1[:], zT[:, 0:w], [w1, zT])
        sin_layer(hid, ps_t[0:64, 0:w], b1, w)
        ps_t = self.banks[1]
        self.mm(ps_t, ps_t[0:64, 0:w], w2_[:], hid[:, 0:w], [w2_, hid])
        sin_layer(hid2, ps_t[0:64, 0:w], b2, w)
        for j in range(w // 128):
            tt = tb * 4 + j
            self.load(dc, dc[:], hc["dec"][tt * 128:(tt + 1) * 128, :])
            for half in range(2):
                ph = self.banks[2 + half]
                self.mm(ph, ph[:, :], hid2[:, j * 128:(j + 1) * 128], w3[:, half * 512:(half + 1) * 512], [hid2, w3])
                self.op("dve", lambda e: e.tensor_tensor(ht[:, half], ph[:, :].rearrange("p (d c) -> p d c", c=256),
                                                         dc[:].unsqueeze(1).to_broadcast([128, 2, 256]), ALU.mult), reads=[ph, dc], writes=[ht])
            if tt == 0:
                self.op("dve", lambda e: e.memset(ht[0:1, :, 1, :], 0.0), writes=[ht])
            self.op("dve", lambda e: e.tensor_tensor(g12[:, 0], ht[:, :, 0, :], ht[:, :, 1, :], ALU.add), reads=[ht], writes=[g12])
            self.op("pool", lambda e: e.tensor_tensor(g12[:, 1], ht[:, :, 0, :], ht[:, :, 1, :], ALU.subtract), reads=[ht], writes=[g12])
            self.store(gd[tt * 128:(tt + 1) * 128, :], g12[:].rearrange("p a f c -> p (a f c)"), g12, ("gd", tt))
            self.op("act", lambda e: e.activation(out=ht[:].rearrange("p f d c -> p (f d c)"), in_=ht[:].rearrange("p f d c -> p (f d c)"), func=AF.Abs), reads=[ht], writes=[ht])
            self.op("dve", lambda e: e.tensor_tensor(ab[:], ht[:, :, 0, :], ht[:, :, 1, :], ALU.add), reads=[ht], writes=[ab])
            self.mm(pab, pab[:, :], self.ones[:, :], ab[:].rearrange("p f c -> p (f c)"), [self.ones, ab], start=(tt == 0), stop=(tt == ntile - 1))
    self.op("dve", lambda e: e.reciprocal(rn[:, 0:512], pab[:, :]), reads=[pab], writes=[rn])
    self.op("dve", lambda e: e.tensor_copy(rn[:, 512:1024], rn[:, 0:512]), reads=[rn], writes=[rn])
    gkeys = [("gd", tt) for tt in range(ntile)]
    def runN(*gs):
        alive = [x for x in gs if x is not None]
        while alive:
            for x in list(alive):
                try:
                    next(x)
                except StopIteration:
                    alive.remove(x)

    runN(stageA(gd, 0, gkeys, filt_k=0, bi=0))
    for k in range(4):
        runN(stageA(gd, (k + 1) * 256, gkeys, filt_k=k + 1, bi=(k + 1) % 2) if k < 3 else None, stageBC(filt_out=k, bi=k % 2))
    self.release(mf)
    cw = self.sb("hcw", [128, 3, 768])
    for i in range(3):
        self.load(cw, cw[:, i, :], _bc(lp["hy_conv"][i:i + 1, :], 128))
    xc = [self.sb("hxc%d" % i, [128, 768]) for i in range(2)]
    xp = [self.sb("hxp%d" % i, [128, 768]) for i in range(2)]
    xn = [self.sb("hxn%d" % i, [128, 768]) for i in range(2)]
    pcv, zd = scr["pcv"], scr["zd"]
    nt = L // 128
    for s in range(nseq):
        for tt in range(nt):
            g = s * nt + tt
            base = g * 128
            i2 = g % 2
            c_, p_, n_ = xc[i2], xp[i2], xn[i2]
            rk_ = self.pkeys(tag, g, 0, 768)
            self.load(c_, c_[:], p_d[base:base + 128, 0:768], reads=rk_)
            if tt == 0:
                self.load(p_, p_[0:1, :], self.zrow[0:1, 0:768], reads=["zrow"])
                self.load(p_, p_[1:128, :], p_d[base:base + 127, 0:768], reads=rk_)
            else:
                self.load(p_, p_[:], p_d[base - 1:base + 127, 0:768], reads=rk_ + self.pkeys(tag, g - 1, 0, 768))
            if tt == nt - 1:
                self.load(n_, n_[0:127, :], p_d[base + 1:base + 128, 0:768], reads=rk_)
                self.load(n_, n_[127:128, :], self.zrow[0:1, 0:768], reads=["zrow"])
            else:
                self.load(n_, n_[:], p_d[base + 1:base + 129, 0:768], reads=rk_ + self.pkeys(tag, g + 1, 0, 768))
            self.op("dve", lambda e: e.tensor_tensor(c_[:], c_[:], cw[:, 1, :], ALU.mult), reads=[c_, cw], writes=[c_])
            self.op("pool", lambda e: e.tensor_tensor(p_[:], p_[:], cw[:, 0, :], ALU.mult), reads=[p_, cw], writes=[p_])
            self.op("pool", lambda e: e.tensor_tensor(n_[:], n_[:], cw[:, 2, :], ALU.mult), reads=[n_, cw], writes=[n_])
            self.op("dve", lambda e: e.tensor_tensor(c_[:], c_[:], p_[:], ALU.add), reads=[c_, p_], writes=[c_])
            self.op("dve", lambda e: e.tensor_tensor(c_[:], c_[:], n_[:], ALU.add), reads=[c_, n_], writes=[c_])
            self.store(pcv[base:base + 128, :], c_[:], c_, ("pcv", g))
    for s in range(nseq):
        rows = pcv[s * L:(s + 1) * L, :]
        ck = [("pcv", s * nt + tt) for tt in range(nt)]
        runN(stageA(rows, 0, ck))
        runN(stageBC(filt_in=0))
        stageD(rows, 256, ck, zd[s * L:(s + 1) * L, :], 0, ("zd", s))
        zk = [(("zd", s), blk) for blk in range(128 // TB)]
        runN(stageA(zd[s * L:(s + 1) * L, :], 0, zk))
        runN(stageBC(filt_in=1))
        stageD(rows, 512, ck, mix_d[s * L:(s + 1) * L, :], 0, ("mix", tag, "hy", s))
    self.release(m)


Kern.phase_hyena = _hyena_phase


def _ln(self, x2, g_bc, b_bc, W):
    st, sq = W["st"], W["sq"]
    self.op("dve", lambda e: e.tensor_reduce(st[:, 0:1], x2[:], AX.X, ALU.add), reads=[x2], writes=[st])
    self.op("dve", lambda e: e.tensor_scalar_mul(st[:, 0:1], st[:, 0:1], -1.0 / D), reads=[st], writes=[st])
    self.op("dve", lambda e: e.tensor_scalar(x2[:], x2[:], st[:, 0:1], None, ALU.add), reads=[x2, st], writes=[x2])
    self.op("act", lambda e: e.activation(out=sq[:], in_=x2[:], func=AF.Square), reads=[x2], writes=[sq])
    self.op("dve", lambda e: e.tensor_reduce(st[:, 1:2], sq[:], AX.X, ALU.add), reads=[sq], writes=[st])
    self.op("dve", lambda e: e.tensor_scalar(st[:, 1:2], st[:, 1:2], 1.0 / D, 1e-5, ALU.mult, ALU.add), reads=[st], writes=[st])
    self.op("act", lambda e: e.activation(out=st[:, 1:2], in_=st[:, 1:2], func=AF.Sqrt), reads=[st], writes=[st]); self.op("dve", lambda e: e.reciprocal(st[:, 1:2], st[:, 1:2]), reads=[st], writes=[st])
    self.op("dve", lambda e: e.scalar_tensor_tensor(x2[:], x2[:], st[:, 1:2], g_bc[:], ALU.mult, ALU.mult), reads=[x2, st, g_bc], writes=[x2])
    self.op("pool", lambda e: e.tensor_tensor(x2[:], x2[:], b_bc[:], ALU.add), reads=[x2, b_bc], writes=[x2])


def _out_phase(self, x_d, mix_d, Ltot, mod, w_out_l, lng, lnb, x1_d, tag):
    m = self.mark()
    row = lambda ap: ap.rearrange("(o n) -> o n", o=1)
    wo = self.sb("wo", [128, 8, D]); self.load(wo, wo[:], w_out_l.rearrange("(k p) n -> p k n", p=128))
    g_bc = self.sb("lng", [128, D]); self.load(g_bc, g_bc[:], _bc(row(lng), 128))
    b_bc = self.sb("lnb", [128, D]); self.load(b_bc, b_bc[:], _bc(row(lnb), 128))
    mx = [self.sb("omx%d" % i, [128, D]) for i in range(2)]
    mT = [self.sb("omT%d" % i, [128, 8, 128]) for i in range(2)]
    xs = [self.sb("oxs%d" % i, [128, D]) for i in range(2)]
    tp = [self.sb("otp%d" % i, [128, D]) for i in range(2)]
    W = {"st": self.sb("ost", [128, 2]), "sq": self.sb("osq", [128, D])}
    for tt in range(Ltot // 128):
        i2 = tt % 2
        mt, t_, x_, tmp = mx[i2], mT[i2], xs[i2], tp[i2]
        self.load(mt, mt[:], mix_d[tt * 128:(tt + 1) * 128, :], reads=[])
        self.to_fm(mt, t_, 0, [mt])
        self.load(x_, x_[:], x_d[tt * 128:(tt + 1) * 128, :], reads=[("x" + tag, tt)])
        for c in range(2):
            b = self.bank()
            for k in range(8):
                self.mm(b, b[:, :], t_[:, k, :], wo[:, k, c * 512:(c + 1) * 512], [t_, wo], start=(k == 0), stop=(k == 7))
            self.op("dve", lambda e: e.tensor_tensor(tmp[:, c * 512:(c + 1) * 512], b[:, :], mod[:, 2 * D + c * 512:2 * D + (c + 1) * 512], ALU.mult),
                    reads=[b, mod], writes=[tmp])
        self.op("dve", lambda e: e.scalar_tensor_tensor(x_[:], x_[:], ALPHA, tmp[:], ALU.mult, ALU.add), reads=[x_, tmp], writes=[x_])
        _ln(self, x_, g_bc, b_bc, W)
        self.store(x1_d[tt * 128:(tt + 1) * 128, :], x_[:], x_, ("x1" + tag, tt))
    self.release(m)


def _mlp_phase(self, x1_d, Ltot, mod, w1_l, w2_l, lng, lnb, xo_d, tag):
    m = self.mark()
    row = lambda ap: ap.rearrange("(o n) -> o n", o=1)
    g_bc = self.sb("lng2", [128, D]); self.load(g_bc, g_bc[:], _bc(row(lng), 128))
    b_bc = self.sb("lnb2", [128, D]); self.load(b_bc, b_bc[:], _bc(row(lnb), 128))
    w1v = w1_l.rearrange("(k p) n -> p k n", p=128)
    w2v = w2_l.rearrange("(f p) n -> p f n", p=128)
    xs = [self.sb("mxs%d" % i, [128, D]) for i in range(2)]
    tp = [self.sb("mtp%d" % i, [128, D]) for i in range(2)]
    hT = self.sb("mhT", [128, 8, 512])
    hid = self.sb("mhid", [128, 32, 512])
    w1t = [self.sb("mw1%d" % i, [128, 8, 512]) for i in range(2)]
    w2t = [self.sb("mw2%d" % i, [128, 4, D]) for i in range(2)]
    W = {"st": self.sb("mst", [128, 2]), "sq": self.sb("msq", [128, D])}
    for blk in range(Ltot // 512):
        for j in range(4):
            tt = blk * 4 + j
            x_ = xs[tt % 2]
            self.load(x_, x_[:], x1_d[tt * 128:(tt + 1) * 128, :], reads=[("x1" + tag, tt)])
            self.op("dve", lambda e: e.tensor_tensor(x_[:], x_[:], mod[:, 4 * D:5 * D], ALU.mult), reads=[x_, mod], writes=[x_])
            self.op("pool", lambda e: e.tensor_tensor(x_[:], x_[:], mod[:, 3 * D:4 * D], ALU.add), reads=[x_, mod], writes=[x_])
            self.to_fm(x_, hT, j * 128, [x_])
        for fc in range(8):
            wt = w1t[fc % 2]
            self.load(wt, wt[:], w1v[:, :, fc * 512:(fc + 1) * 512])
            for f4 in range(4):
                f = fc * 4 + f4
                b = self.banks[f % 8]
                for k in range(8):
                    self.mm(b, b[:, :], wt[:, k, f4 * 128:(f4 + 1) * 128], hT[:, k, :], [wt, hT], start=(k == 0), stop=(k == 7))
                self.op("act", lambda e: e.activation(out=hid[:, f, :], in_=b[:, :], func=AF.Relu), reads=[b], writes=[("hid", f)])
                self.op("pool", lambda e: e.tensor_tensor(hid[:, f, :], hid[:, f, :], hid[:, f, :], ALU.mult), reads=[("hid", f)], writes=[("hid", f)])
        for fg in range(8):
            wt = w2t[fg % 2]
            self.load(wt, wt[:], w2v[:, fg * 4:(fg + 1) * 4, :])
            for j in range(4):
                for c in range(2):
                    b = self.banks[j * 2 + c]
                    for i in range(4):
                        f = fg * 4 + i
                        self.mm(b, b[:, :], hid[:, f, j * 128:(j + 1) * 128], wt[:, i, c * 512:(c + 1) * 512], [("hid", f), wt],
                                start=(fg == 0 and i == 0), stop=(fg == 7 and i == 3))
        for j in range(4):
            tt = blk * 4 + j
            x_, tmp = xs[tt % 2], tp[tt % 2]
            for c in range(2):
                b = self.banks[j * 2 + c]
                self.op("dve", lambda e: e.tensor_tensor(tmp[:, c * 512:(c + 1) * 512], b[:, :], mod[:, 5 * D + c * 512:5 * D + (c + 1) * 512], ALU.mult),
                        reads=[b, mod], writes=[tmp])
            self.load(x_, x_[:], x1_d[tt * 128:(tt + 1) * 128, :], reads=[("x1" + tag, tt)])
            self.op("dve", lambda e: e.scalar_tensor_tensor(x_[:], x_[:], ALPHA, tmp[:], ALU.mult, ALU.add), reads=[x_, tmp], writes=[x_])
            _ln(self, x_, g_bc, b_bc, W)
            self.store(xo_d[tt * 128:(tt + 1) * 128, :], x_[:], x_, ("x" + tag, tt))
    self.release(m)


Kern.phase_out = _out_phase
Kern.phase_mlp = _mlp_phase

LAYER_PARAMS = {"w_in": [D, P_IN], "hy_conv": [3, 768], "hy_w1": [33, 64], "hy_b1": [64], "hy_freq": [64], "hy_w2": [64, 64], "hy_b2": [64],
                "hy_w3": [64, 1024], "hy_bias": [2, 256], "ml_gate_b": [16], "ml_norm_g": [256], "rw_mu": [1024], "rw_w0": [2, 256],
                "rw_w2": [2, 64, 256], "rw_a0": [2, 256], "rw_a2": [2, 64, 256], "rw_g2": [128, 256], "rw_kk": [256], "rw_ka": [256],
                "rw_rk": [256], "rw_ln_g": [256], "rw_ln_b": [256], "at_qn": [64], "at_kn": [64], "w_out": [D, D], "ln1_g": [D], "ln1_b": [D],
                "mlp_w1": [D, D_FF], "mlp_w2": [D_FF, D], "ln2_g": [D], "ln2_b": [D], "w_mod": [D, 6 * D], "b_mod": [6 * D]}


def rope_np():
    rows = 4096 // 64
    pos_r = np.repeat(np.arange(rows, dtype=np.float32), 64)
    pos_c = np.tile(np.arange(64, dtype=np.float32), rows)
    inv = (10000.0 ** (-np.arange(16, dtype=np.float32) / 16)).astype(np.float32)
    C = np.zeros((4096, 64), np.float32); S = np.zeros((4096, 64), np.float32)
    for a, pos in enumerate((pos_r, pos_c)):
        ang = pos[:, None] * inv[None, :]
        c, s = np.cos(ang), np.sin(ang)
        C[:, a * 32:a * 32 + 16] = c; C[:, a * 32 + 16:a * 32 + 32] = c
        S[:, a * 32:a * 32 + 16] = -s; S[:, a * 32 + 16:a * 32 + 32] = s
    return C, S


def consts_np():
    tri = np.zeros((4, 128, 128), np.float32)
    s = np.arange(128)[:, None]; t = np.arange(128)[None, :]
    tri[0] = (s <= t); tri[1] = (s >= t); tri[2] = (s < t); tri[3] = (s > t)
    return {"c_ident": np.eye(128, dtype=np.float32), "c_tri": tri}


_BUILD = {}


def build(depth=DEPTH, groups=("p", "s"), phases=None):
    ph = lambda n: phases is None or n in phases
    K = Kern(); K.init_banks(); K.consts()
    LP = {k: K.inp(k, [DEPTH] + v) for k, v in LAYER_PARAMS.items()}
    xp = K.inp("x_prompt", [1024, D]); xsm = K.inp("x_sample", [4096, D])
    ck = K.inp("cache_attn_k", [DEPTH, 512, 128]); cv = K.inp("cache_attn_v", [DEPTH, 512, 128])
    sC = K.inp("state_mlstm_C", [DEPTH, 2, 4, 64, 64]); sn = K.inp("state_mlstm_n", [DEPTH, 2, 4, 64]); sm = K.inp("state_mlstm_m", [DEPTH, 2, 4])
    sS = K.inp("state_rwkv_S", [DEPTH, 2, 4, 64, 64])
    cc = K.inp("c", [D]); cctx = K.inp("c_ctx", [D])
    rc = K.inp("rope_c", [4096, 64]); rs = K.inp("rope_s", [4096, 64])
    hcs = {}
    for L in (256, 4096):
        hn = hy_consts_np(L)
        hcs[L] = {k: K.inp("hc%d_%s" % (L, k), list(v.shape)) for k, v in hn.items()}
    hdn = hyd_consts_np()
    hdc = {k: K.inp("hd_" + k, list(v.shape)) for k, v in hdn.items()}
    yp = K.outp("y_prompt", [1024, D]); ys = K.outp("y_sample", [4096, D])
    ok = K.outp("new_attn_k", [4, DEPTH, 256, 128]); ov = K.outp("new_attn_v", [4, DEPTH, 256, 128])
    oC = K.outp("new_mlstm_C", [4, DEPTH, 2, 4, 64, 64]); on = K.outp("new_mlstm_n", [4, DEPTH, 2, 4, 64]); om = K.outp("new_mlstm_m", [4, DEPTH, 2, 4])
    oS = K.outp("new_rwkv_S", [4, DEPTH, 2, 4, 64, 64])
    p_d = K.dram("p_d", [4096, P_IN]); mix_d = K.dram("mix_d", [4096, D]); x1_d = K.dram("x1_d", [4096, D])
    xa = K.dram("xa_d", [4096, D]); xb = K.dram("xb_d", [4096, D]); yf_d = K.dram("yf_d", [4096, 256])
    scr = {"pcv": K.dram("pcv", [4096, 768]), "gd": K.dram("gd", [4096, 1024]), "Bd": K.dram("Bd", [128, 64, 2, 256]), "Bd2": K.dram("Bd2", [128, 64, 2, 256]),
           "Fd": K.dram("Fd", [4, 128, 64, 2, 256]), "Dd": K.dram("Dd", [64, 128, 2, 256]), "zd": K.dram("zd", [4096, 256])}
    for tag in groups:
        if tag == "p":
            nseq, L, x_in, x_out, cond = 4, 256, xp, yp, cctx
        else:
            nseq, L, x_in, x_out, cond = 1, 4096, xsm, ys, cc
        Ltot = nseq * L
        N1 = 2 * L // 128
        sc = {"pcv": scr["pcv"], "gd": scr["gd"][0:L], "Bd": scr["Bd"][:, 0:N1], "Bd2": scr["Bd2"][:, 0:N1], "Fd": scr["Fd"][:, :, 0:N1], "Dd": scr["Dd"][0:N1], "zd": scr["zd"]}
        x_cur = x_in
        for l in range(depth):
            lp = {k: v[l] for k, v in LP.items()}
            x_next = x_out if l == depth - 1 else (xa if l % 2 == 0 else xb)
            mod, m0 = K.phase_mod(cond, lp["w_mod"], lp["b_mod"], tag)
            if ph('proj'):
                K.phase_proj(x_cur, Ltot, mod, lp["w_in"], p_d, tag)
            if ph('hy'):
                if L == 256:
                    K.phase_hyena_direct(p_d, tag, nseq, lp, mix_d, hcs[L], hdc)
                else:
                    K.phase_hyena(p_d, tag, nseq, L, lp, mix_d, hcs[L], sc)
            if tag == "p":
                if ph('ml'):
                    K.phase_mlstm(p_d, tag, nseq, L, lp, mix_d, outs=[(oC[s, l], on[s, l], om[s, l]) for s in range(4)])
                if ph('rw'):
                    K.phase_rwkv(p_d, tag, nseq, L, lp, mix_d, yf_d, outS=[oS[s, l] for s in range(4)])
                if ph('at'):
                    K.phase_attn(p_d, tag, nseq, L, lp, mix_d, outk=[ok[s, l] for s in range(4)], outv=[ov[s, l] for s in range(4)])
            else:
                if ph('ml'):
                    K.phase_mlstm(p_d, tag, nseq, L, lp, mix_d, st0=(sC[l], sn[l], sm[l]))
                if ph('rw'):
                    K.phase_rwkv(p_d, tag, nseq, L, lp, mix_d, yf_d, S0=sS[l])
                if ph('at'):
                    K.phase_attn(p_d, tag, nseq, L, lp, mix_d, ctx=(ck[l], cv[l]), rope=(rc, rs))
            if ph('out'):
                K.phase_out(x_cur, mix_d, Ltot, mod, lp["w_out"], lp["ln1_g"], lp["ln1_b"], x1_d, tag)
            if ph('mlp'):
                K.phase_mlp(x1_d, Ltot, mod, lp["mlp_w1"], lp["mlp_w2"], lp["ln2_g"], lp["ln2_b"], x_next, tag)
            K.release(m0)
            x_cur = x_next
    K.finish()
    return K


def kernel(**inputs):
    f = lambda a: np.ascontiguousarray(np.asarray(a, dtype=np.float32))
    K = build()
    C, S = rope_np()
    common = dict(consts_np())
    common.update({"rope_c": C, "rope_s": S})
    for L in (256, 4096):
        for k, v in hy_consts_np(L).items():
            common["hc%d_%s" % (L, k)] = v
    for k, v in hyd_consts_np().items():
        common["hd_" + k] = v
    for k in LAYER_PARAMS:
        common[k] = f(inputs[k])
    common["c_ctx"] = f(inputs["c_ctx"])
    in_maps = []
    for core in range(8):
        b = core % 2
        im = dict(common)
        im["x_prompt"] = f(inputs["x_prompt"][core * 4:(core + 1) * 4]).reshape(1024, D)
        im["x_sample"] = f(inputs["x_sample"][b])
        im["cache_attn_k"] = f(inputs["cache_attn_k"][b]).reshape(DEPTH, 512, 128)
        im["cache_attn_v"] = f(inputs["cache_attn_v"][b]).reshape(DEPTH, 512, 128)
        im["state_mlstm_C"] = f(inputs["state_mlstm_C"][b]); im["state_mlstm_n"] = f(inputs["state_mlstm_n"][b])
        im["state_mlstm_m"] = f(inputs["state_mlstm_m"][b]); im["state_rwkv_S"] = f(inputs["state_rwkv_S"][b])
        im["c"] = f(inputs["c"][b])
        in_maps.append(im)
    res = run_bass_kernel_spmd(K.nc, in_maps, core_ids=list(range(8)))
    R = res.results
    cat = lambda name: np.concatenate([R[c][name] for c in range(8)], axis=0)
    y_prompt = cat("y_prompt").reshape(32, 256, D)
    y_sample = np.stack([R[0]["y_sample"], R[1]["y_sample"]], axis=0)
    nk = cat("new_attn_k").reshape(32, DEPTH, 256, 2, 64)
    nv = cat("new_attn_v").reshape(32, DEPTH, 256, 2, 64)
    return (y_prompt, y_sample, nk, nv, cat("new_mlstm_C"), cat("new_mlstm_n"), cat("new_mlstm_m"), cat("new_rwkv_S"))


def hyd_consts_np():
    L, N = 256, 512
    s = np.arange(L)[:, None]
    f = np.arange(N)[None, :]
    ang = 2 * np.pi * ((s * f) % N) / N
    Wf = np.stack([np.cos(ang), -np.sin(ang)], axis=1).astype(np.float32)
    Wf = np.ascontiguousarray(Wf.reshape(2, 128, 2, 512))
    angi = ang.T
    Wi = (np.stack([np.cos(angi), -np.sin(angi)], axis=1) / N).astype(np.float32)
    Wi = np.ascontiguousarray(Wi.reshape(4, 128, 2, 256))
    return {"Wf": Wf, "Wi": Wi}


def _hyena_direct(self, p_d, tag, nseq, lp, mix_d, hc, hd):
    L = 256
    m = self.mark()
    Wf = self.sb("Wf", [128, 2, 2, 512]); self.load(Wf, Wf[:], hd["Wf"].rearrange("k p r f -> p k r f"))
    Wi = self.sb("Wi", [128, 4, 2, 256]); self.load(Wi, Wi[:], hd["Wi"].rearrange("k p r t -> p k r t"))
    G = self.sb("hG", [128, 2, 1024])
    F = self.sb("hF", [128, 4, 2, 512])
    rn = self.sb("hy_rn", [128, 512])
    bias2 = self.sb("hy_bias", [1, 512]); self.load(bias2, bias2[:], lp["hy_bias"].rearrange("(o f) c -> o (f c)", o=1))
    mf = self.mark()
    w1 = self.sb("hw1", [33, 64]); self.load(w1, w1[:], lp["hy_w1"])
    w2_ = self.sb("hw2", [64, 64]); self.load(w2_, w2_[:], lp["hy_w2"])
    w3 = self.sb("hw3", [64, 1024]); self.load(w3, w3[:], lp["hy_w3"])
    col = lambda ap: ap.rearrange("(n o) -> n o", o=1)
    b1 = self.sb("hb1", [64, 1]); self.load(b1, b1[:], col(lp["hy_b1"]))
    b2 = self.sb("hb2", [64, 1]); self.load(b2, b2[:], col(lp["hy_b2"]))
    fr = self.sb("hfr", [64, 1]); self.load(fr, fr[:], col(lp["hy_freq"]))
    zT = self.sb("hzT", [33, 256]); hid = self.sb("hhid", [64, 256]); nq = self.sb("hnq", [64, 256]); hid2 = self.sb("hhid2", [64, 256])
    dc = self.sb("hdec", [128, 2, 256])
    ht = self.sb("hht", [128, 2, 2, 256])
    ab = self.sb("hab", [128, 2, 256])
    pab = self.banks[7]

    def sin_layer(dst, ps_t, bb_):
        self.op("dve", lambda e: e.tensor_scalar(dst[:], ps_t[0:64, 0:256], bb_[:, 0:1], fr[:, 0:1], ALU.add, ALU.mult), reads=[ps_t, bb_, fr], writes=[dst])
        self.op("dve", lambda e: e.tensor_scalar(nq[:], dst[:], 1.0 / TWO_PI, MAGIC, ALU.mult, ALU.add), reads=[dst], writes=[nq])
        self.op("dve", lambda e: e.tensor_scalar(nq[:], nq[:], MAGIC, -TWO_PI, ALU.subtract, ALU.mult), reads=[nq], writes=[nq])
        self.op("dve", lambda e: e.tensor_tensor(dst[:], dst[:], nq[:], ALU.add), reads=[dst, nq], writes=[dst])
        self.op("act", lambda e: e.activation(out=dst[:], in_=dst[:], func=AF.Sin), reads=[dst], writes=[dst])

    self.load(zT, zT[:], hc["zT"])
    self.load(dc, dc[:], hc["dec"].rearrange("(k p) c -> p k c", p=128))
    ps_t = self.banks[0]
    self.mm(ps_t, ps_t[0:64, 0:256], w1[:], zT[:], [w1, zT])
    sin_layer(hid, ps_t, b1)
    ps_t = self.banks[1]
    self.mm(ps_t, ps_t[0:64, 0:256], w2_[:], hid[:], [w2_, hid])
    sin_layer(hid2, ps_t, b2)
    for tt in range(2):
        for half in range(2):
            ph = self.banks[2 + half]
            self.mm(ph, ph[:, :], hid2[:, tt * 128:(tt + 1) * 128], w3[:, half * 512:(half + 1) * 512], [hid2, w3])
            self.op("dve", lambda e: e.tensor_tensor(ht[:, half], ph[:, :].rearrange("p (d c) -> p d c", c=256),
                                                     dc[:, tt:tt + 1, :].to_broadcast([128, 2, 256]), ALU.mult), reads=[ph, dc], writes=[ht])
        if tt == 0:
            self.op("dve", lambda e: e.memset(ht[0:1, :, 1, :], 0.0), writes=[ht])
        Gv = G[:, tt, :].rearrange("p (a f c) -> p a f c", a=2, f=2)
        self.op("dve", lambda e: e.tensor_tensor(Gv[:, 0], ht[:, :, 0, :], ht[:, :, 1, :], ALU.add), reads=[ht], writes=[G])
        self.op("pool", lambda e: e.tensor_tensor(Gv[:, 1], ht[:, :, 0, :], ht[:, :, 1, :], ALU.subtract), reads=[ht], writes=[G])
        self.op("act", lambda e: e.activation(out=ht[:].rearrange("p f d c -> p (f d c)"), in_=ht[:].rearrange("p f d c -> p (f d c)"), func=AF.Abs), reads=[ht], writes=[ht])
        self.op("dve", lambda e: e.tensor_tensor(ab[:], ht[:, :, 0, :], ht[:, :, 1, :], ALU.add), reads=[ht], writes=[ab])
        self.mm(pab, pab[:, :], self.ones[:, :], ab[:].rearrange("p f c -> p (f c)"), [self.ones, ab], start=(tt == 0), stop=(tt == 1))
    self.op("dve", lambda e: e.reciprocal(rn[:], pab[:, :]), reads=[pab], writes=[rn])
    self.op("dve", lambda e: e.tensor_tensor(G[:].rearrange("p k (a n) -> p (k a) n", a=2), G[:].rearrange("p k (a n) -> p (k a) n", a=2),
                                             rn[:].unsqueeze(1).to_broadcast([128, 4, 512]), ALU.mult), reads=[G, rn], writes=[G])
    self.op("dve", lambda e: e.tensor_tensor(G[0:1, 0, 0:512], G[0:1, 0, 0:512], bias2[0:1, :], ALU.add), reads=[G, bias2], writes=[G])
    for ft in range(4):
        for ri in range(2):
            b_ = self.bank()
            for kt in range(2):
                self.mm(b_, b_[:, :], Wf[:, kt, ri, ft * 128:(ft + 1) * 128], G[:, kt, ri * 512:(ri + 1) * 512], [Wf, G], start=(kt == 0), stop=(kt == 1))
            if ri == 0:
                self.op("act", lambda e: e.copy(F[:, ft, ri, :], b_[:, :]), reads=[b_], writes=[F])
            else:
                self.op("dve", lambda e: e.tensor_copy(F[:, ft, ri, :], b_[:, :]), reads=[b_], writes=[F])
    self.release(mf)
    cw = self.sb("hcw", [128, 3, 768])
    for i in range(3):
        self.load(cw, cw[:, i, :], _bc(lp["hy_conv"][i:i + 1, :], 128))
    nt = 2
    pcv = self.sb("hpcv", [128, nseq * nt, 768])
    xp = [self.sb("hxp%d" % i, [128, 768]) for i in range(2)]
    xn = [self.sb("hxn%d" % i, [128, 768]) for i in range(2)]
    for s in range(nseq):
        for tt in range(nt):
            g = s * nt + tt
            base = g * 128
            p_, n_ = xp[g % 2], xn[g % 2]
            c_ = pcv[:, g, :]
            ck = ("pcv", g)
            rk_ = self.pkeys(tag, g, 0, 768)
            self.dma("sp", c_, p_d[base:base + 128, 0:768], reads=rk_, writes=[ck])
            if tt == 0:
                self.load(p_, p_[0:1, :], self.zrow[0:1, 0:768], reads=["zrow"])
                self.load(p_, p_[1:128, :], p_d[base:base + 127, 0:768], reads=rk_)
            else:
                self.load(p_, p_[:], p_d[base - 1:base + 127, 0:768], reads=rk_ + self.pkeys(tag, g - 1, 0, 768))
            if tt == nt - 1:
                self.load(n_, n_[0:127, :], p_d[base + 1:base + 128, 0:768], reads=rk_)
                self.load(n_, n_[127:128, :], self.zrow[0:1, 0:768], reads=["zrow"])
            else:
                self.load(n_, n_[:], p_d[base + 1:base + 129, 0:768], reads=rk_ + self.pkeys(tag, g + 1, 0, 768))
            self.op("dve", lambda e: e.tensor_tensor(c_, c_, cw[:, 1, :], ALU.mult), reads=[ck, cw], writes=[ck])
            self.op("pool", lambda e: e.tensor_tensor(p_[:], p_[:], cw[:, 0, :], ALU.mult), reads=[p_, cw], writes=[p_])
            self.op("pool", lambda e: e.tensor_tensor(n_[:], n_[:], cw[:, 2, :], ALU.mult), reads=[n_, cw], writes=[n_])
            self.op("dve", lambda e: e.tensor_tensor(c_, c_, p_[:], ALU.add), reads=[ck, p_], writes=[ck])
            self.op("dve", lambda e: e.tensor_tensor(c_, c_, n_[:], ALU.add), reads=[ck, n_], writes=[ck])
    Z = self.sb("hZ", [128, nseq * nt, 256])
    Ys = [self.sb("hY%d" % i, [128, 4, 2, 512]) for i in range(2)]
    tms = [self.sb("htm%d" % i, [128, 512]) for i in range(2)]
    outs = [self.sb("hout%d" % i, [128, 2, 256]) for i in range(2)]
    pv4 = pcv[:].rearrange("p (s k) c -> p s k c", k=nt)
    z4 = Z[:].rearrange("p (s k) c -> p s k c", k=nt)
    it = 0
    for stage in range(2):
        for s0 in range(0, nseq, 2):
            Y = Ys[it % 2]
            it += 1
            if stage == 0:
                src = lambda kt: pv4[:, s0:s0 + 2, kt, 0:256]
                skeys = [("pcv", (s0 + i) * nt + k) for i in range(2) for k in range(nt)]
            else:
                src = lambda kt: z4[:, s0:s0 + 2, kt, :]
                skeys = [("hZ", (s0 + i) * nt + k) for i in range(2) for k in range(nt)]
            for ft in range(4):
                bre = self.bank()
                for kt in range(2):
                    self.mm(bre, bre[:, :].rearrange("p (s c) -> p s c", c=256), Wf[:, kt, 0, ft * 128:(ft + 1) * 128], src(kt), [Wf] + skeys, start=(kt == 0), stop=(kt == 1))
                bim = self.bank()
                for kt in range(2):
                    self.mm(bim, bim[:, :].rearrange("p (s c) -> p s c", c=256), Wf[:, kt, 1, ft * 128:(ft + 1) * 128], src(kt), [Wf] + skeys, start=(kt == 0), stop=(kt == 1))
                fre = F[:, ft, 0, stage * 256:(stage + 1) * 256].unsqueeze(1).to_broadcast([128, 2, 256])
                fim = F[:, ft, 1, stage * 256:(stage + 1) * 256].unsqueeze(1).to_broadcast([128, 2, 256])
                v3 = lambda ap: ap.rearrange("p (s c) -> p s c", c=256)
                tm = tms[ft % 2]
                yk = ("hY", it % 2, ft)
                self.op("dve", lambda e: e.tensor_tensor(v3(Y[:, ft, 0, :]), v3(bre[:, :]), fre, ALU.mult), reads=[bre, F], writes=[yk])
                self.op("dve", lambda e: e.tensor_tensor(v3(tm[:]), v3(bim[:, :]), fim, ALU.mult), reads=[bim, F], writes=[tm])
                self.op("pool", lambda e: e.tensor_tensor(Y[:, ft, 0, :], Y[:, ft, 0, :], tm[:], ALU.subtract), reads=[yk, tm], writes=[yk])
                self.op("dve", lambda e: e.tensor_tensor(v3(Y[:, ft, 1, :]), v3(bre[:, :]), fim, ALU.mult), reads=[bre, F], writes=[yk])
                self.op("dve", lambda e: e.tensor_tensor(v3(tm[:]), v3(bim[:, :]), fre, ALU.mult), reads=[bim, F], writes=[tm])
                self.op("pool", lambda e: e.tensor_tensor(Y[:, ft, 1, :], Y[:, ft, 1, :], tm[:], ALU.add), reads=[yk, tm], writes=[yk])
            for tt in range(2):
                b_ = self.bank()
                n_ = 0
                for ft in range(4):
                    for ri in range(2):
                        self.mm(b_, b_[:, :], Wi[:, ft, ri, tt * 128:(tt + 1) * 128], Y[:, ft, ri, :], [Wi, ("hY", it % 2, ft)], start=(n_ == 0), stop=(n_ == 7))
                        n_ += 1
                bv = b_[:, :].rearrange("p (s c) -> p s c", c=256)
                if stage == 0:
                    gate = pv4[:, s0:s0 + 2, tt, 256:512]
                    zk = [("hZ", (s0 + i) * nt + tt) for i in range(2)]
                    self.op("dve", lambda e: e.tensor_tensor(z4[:, s0:s0 + 2, tt, :], bv, gate, ALU.mult),
                            reads=[b_] + [("pcv", (s0 + i) * nt + tt) for i in range(2)], writes=zk)
                else:
                    gate = pv4[:, s0:s0 + 2, tt, 512:768]
                    o_ = outs[(s0 // 2 * 2 + tt) % 2]
                    self.op("dve", lambda e: e.tensor_tensor(o_[:], bv, gate, ALU.mult),
                            reads=[b_] + [("pcv", (s0 + i) * nt + tt) for i in range(2)], writes=[o_])
                    for i in range(2):
                        r0 = (s0 + i) * L + tt * 128
                        self.store(mix_d[r0:r0 + 128, 0:256], o_[:, i, :], o_, ("mix", tag, "hy", s0 + i, tt))
    self.release(m)


Kern.phase_hyena_direct = _hyena_direct
```

```python
import math
import numpy as np
import concourse.bass as bass
import concourse.mybir as mybir
from concourse.bass_utils import run_bass_kernel_spmd

F32 = mybir.dt.float32
AF = mybir.ActivationFunctionType
ALU = mybir.AluOpType
AX = mybir.AxisListType

D = 1024
DEPTH = 4
GW = 256
HD = 64
P_IN = 3344
OFF_HY, OFF_ML, OFF_RW, OFF_AT = 0, 768, 1808, 2832
D_FF = 4096
ALPHA = (2.0 * DEPTH) ** 0.25
PAST = 512
RW_DECAY = 0.606531
MAGIC = 12582912.0
TWO_PI = 2.0 * math.pi


class T:
    __slots__ = ("h", "key")

    def __init__(self, h, key):
        self.h = h
        self.key = key

    def __getitem__(self, idx):
        return self.h[idx]


class TP(T):
    __slots__ = ("f", "np_")

    def __init__(self, h, key, np_):
        T.__init__(self, h, key)
        self.f = h
        self.np_ = np_

    def __getitem__(self, idx):
        return self.h[0:self.np_][idx]


class FW:
    def __init__(self, n_dma_sems=32):
        self.nc = bass.Bass("TRN2", target_bir_lowering=False)
        nc = self.nc
        self.eng = {"pe": nc.tensor, "dve": nc.vector, "act": nc.scalar, "pool": nc.gpsimd, "sp": nc.sync}
        self.sem = {}
        self.cnt = {}
        self._ctx = []
        for e in self.eng:
            cm = nc.semaphore("s_" + e)
            self.sem[e] = cm.__enter__()
            self.cnt[e] = 0
        cm = nc.semaphore("s_bar")
        self.sem_bar = cm.__enter__()
        self.n_bar = 0
        self.dma_sems = []
        for i in range(n_dma_sems):
            cm = nc.semaphore("d_%d" % i)
            self.dma_sems.append(cm.__enter__())
        self.dma_use = [0] * n_dma_sems
        self.dma_rr = {"hw": 0, "sw": 0}
        self.n_hw = n_dma_sems - 8
        self.waited = {e: {} for e in self.eng}
        self.lastw = {}
        self.readers = {}
        self.uid = 0
        self.n_inst = 0
        self.psum_keys = set()

    def _name(self, name):
        self.uid += 1
        return "%s_%d" % (name, self.uid)

    def sb(self, name, shape, dtype=F32):
        nm = self._name(name)
        cm = self.nc.sbuf_tensor(nm, list(shape), dtype)
        h = cm.__enter__()
        self._ctx.append(cm)
        return T(h, nm)

    def sbp(self, name, shape, dtype=F32):
        nm = self._name(name)
        np_ = shape[0]
        cm = self.nc.sbuf_tensor(nm, [128] + list(shape[1:]), dtype)
        h = cm.__enter__()
        self._ctx.append(cm)
        t = TP(h, nm, np_)
        if np_ == 64:
            self.op("pool", lambda e: e.memset(h[64:128], 0.0), writes=[t])
        else:
            self.op("pool", lambda e: e.memset(h[0:128], 0.0), writes=[t])
        return t

    def ps(self, name, shape, dtype=F32):
        nm = self._name(name)
        cm = self.nc.psum_tensor(nm, list(shape), dtype)
        h = cm.__enter__()
        self._ctx.append(cm)
        self.psum_keys.add(nm)
        return T(h, nm)

    def dram(self, name, shape, dtype=F32, kind="Internal"):
        return self.nc.dram_tensor(name, list(shape), dtype, kind=kind).ap()

    def _wait_token(self, e, tok):
        sk, val = tok
        w = self.waited[e]
        if w.get(sk, 0) >= val:
            return
        w[sk] = val
        if sk == "bar":
            sem = self.sem_bar
        elif isinstance(sk, str):
            sem = self.sem[sk]
        else:
            sem = self.dma_sems[sk]
        self.eng[e].wait_ge(sem, val)

    def _deps(self, e, reads, writes):
        for k in reads:
            for t in self.lastw.get(k, ()):
                if not (e == "pe" and t[0] == "pe"):
                    self._wait_token(e, t)
        for k in writes:
            for t in self.lastw.get(k, ()):
                if not (e == "pe" and t[0] == "pe"):
                    self._wait_token(e, t)
            for t in self.readers.get(k, ()):
                if not (e == "pe" and t[0] == "pe"):
                    self._wait_token(e, t)

    def _record(self, tok, reads, writes):
        for k in writes:
            self.lastw[k] = [tok]
            self.readers[k] = []
        for k in reads:
            if k in writes:
                continue
            self.readers.setdefault(k, []).append(tok)

    @staticmethod
    def _keys(lst):
        return [x.key if isinstance(x, T) else x for x in lst]

    def op(self, e, fn, reads=(), writes=()):
        reads = self._keys(reads)
        writes = self._keys(writes)
        pr = [k for k in reads if k in self.psum_keys and k not in writes]
        if pr:
            reads = [k for k in reads if k not in self.psum_keys]
            writes = writes + pr
        self._deps(e, reads, writes)
        ins = fn(self.eng[e])
        self.cnt[e] += 1
        ins.then_inc(self.sem[e], 1)
        self._record((e, self.cnt[e]), reads, writes)
        self.n_inst += 1

    def dma(self, q, out, in_, reads=(), writes=(), **kw):
        reads = self._keys(reads)
        writes = self._keys(writes)
        self._deps(q, reads, writes)
        if q == "pool":
            i = self.n_hw + self.dma_rr["sw"]
            self.dma_rr["sw"] = (self.dma_rr["sw"] + 1) % (len(self.dma_sems) - self.n_hw)
        else:
            i = self.dma_rr["hw"]
            self.dma_rr["hw"] = (self.dma_rr["hw"] + 1) % self.n_hw
        if self.dma_use[i] > 0:
            self._wait_token(q, (i, 16 * self.dma_use[i]))
        self.dma_use[i] += 1
        ins = self.eng[q].dma_start(out=out, in_=in_, **kw)
        ins.then_inc(self.dma_sems[i], 16)
        self._record((i, 16 * self.dma_use[i]), reads, writes)
        self.n_inst += 1

    def barrier(self):
        sp = "sp"
        for i, u in enumerate(self.dma_use):
            if u > 0:
                self._wait_token(sp, (i, 16 * u))
        for x in ("pe", "dve", "act", "pool"):
            if self.cnt[x] > 0:
                self._wait_token(sp, (x, self.cnt[x]))
        self.n_bar += 1
        self.eng[sp].sem_inc(self.sem_bar, 1)
        for x in ("pe", "dve", "act", "pool"):
            self._wait_token(x, ("bar", self.n_bar))
        for e in self.eng:
            w = self.waited[e]
            for x in self.eng:
                w[x] = self.cnt[x]
            for i, u in enumerate(self.dma_use):
                w[i] = 16 * u
        self.lastw = {}
        self.readers = {}

    def mark(self):
        return len(self._ctx)

    def release(self, mark):
        self.barrier()
        while len(self._ctx) > mark:
            cm = self._ctx.pop()
            cm.__exit__(None, None, None)

    def finish(self):
        self.barrier()


def _bc(ap, n):
    return ap.partition_broadcast(n)


class Kern(FW):
    def __init__(self):
        super().__init__()
        self.ins = {}
        self.outs = {}
        self.banks = None
        self.bank_rr = 0
        self.q_rr = 0

    def inp(self, name, shape):
        ap = self.nc.dram_tensor(name, list(shape), F32, kind="ExternalInput").ap()
        self.ins[name] = ap
        return ap

    def outp(self, name, shape):
        ap = self.nc.dram_tensor(name, list(shape), F32, kind="ExternalOutput").ap()
        self.outs[name] = ap
        return ap

    def init_banks(self):
        self.banks = [self.ps("bank%d" % i, [128, 512]) for i in range(8)]

    def bank(self):
        b = self.banks[self.bank_rr]
        self.bank_rr = (self.bank_rr + 1) % 8
        return b

    def load(self, t, ap_out, ap_in, reads=(), q="sp", **kw):
        self.dma(q, ap_out, ap_in, reads=reads, writes=[t], **kw)

    def store(self, ap_out, ap_in, t, key, q="pool", **kw):
        self.dma(q, ap_out, ap_in, reads=[t], writes=[key], **kw)

    def mm(self, ps, out_ap, lhsT, rhs, reads, start=True, stop=True):
        self.op("pe", lambda e: e.matmul(out_ap, lhsT, rhs, start=start, stop=stop), reads=reads, writes=[ps])

    def tr(self, ps, out_ap, in_ap, ident_ap, reads):
        self.op("pe", lambda e: e.transpose(out_ap, in_ap, ident_ap), reads=list(reads) + [self.ident], writes=[ps])

    def consts(self):
        c = self.inp("c_ident", [128, 128])
        self.ident = self.sb("ident", [128, 128])
        self.load(self.ident, self.ident[:], c)
        self.c_tri = self.inp("c_tri", [4, 128, 128])
        self.tri = self.sb("tri", [128, 4, 128])
        self.load(self.tri, self.tri[:], self.c_tri.rearrange("a p n -> p a n"))
        self.ones = self.sb("ones", [128, 128])
        self.op("dve", lambda e: e.memset(self.ones[:], 1.0), writes=[self.ones])
        self.zrow = self.dram("zrow", [1, 1024])
        mz = self.mark()
        zt = self.sb("zt", [1, 1024])
        self.op("dve", lambda e: e.memset(zt[:], 0.0), writes=[zt])
        self.store(self.zrow, zt[:], zt, "zrow", q="sp")
        self.release(mz)

    def phase_mod(self, cond_ap, wmod_l, bmod_l, tag):
        m0 = self.mark()
        mod = self.sb("mod" + tag, [128, 6 * D])
        m1 = self.mark()
        cT = self.sb("cT", [128, 8])
        self.load(cT, cT[:], cond_ap.rearrange("(k p) -> p k", p=128), allow_slow_non_contiguous=True)
        sg = self.sb("sg", [128, 8])
        self.op("act", lambda e: e.activation(out=sg[:], in_=cT[:], func=AF.Sigmoid), reads=[cT], writes=[sg])
        self.op("dve", lambda e: e.tensor_tensor(sg[:], sg[:], cT[:], ALU.mult), reads=[sg, cT], writes=[sg])
        lb = self.sb("lb", [128, 8, 128])
        self.op("dve", lambda e: e.tensor_copy(lb[:], sg[:].unsqueeze(2).to_broadcast([128, 8, 128])), reads=[sg], writes=[lb])
        bm = self.sb("bm", [128, 6 * D])
        self.load(bm, bm[:], _bc(bmod_l.rearrange("(o n) -> o n", o=1), 128))
        wv = wmod_l.rearrange("(k p) n -> p k n", p=128)
        wts = [self.sb("wm%d" % i, [128, 8, 512]) for i in range(2)]
        for c in range(12):
            wt = wts[c % 2]
            self.load(wt, wt[:], wv[:, :, c * 512:(c + 1) * 512])
            b = self.bank()
            for k in range(8):
                self.mm(b, b[:, :], lb[:, k, :], wt[:, k, :], [lb, wt], start=(k == 0), stop=(k == 7))
            self.op("dve", lambda e: e.tensor_tensor(mod[:, c * 512:(c + 1) * 512], b[:, :], bm[:, c * 512:(c + 1) * 512], ALU.add),
                    reads=[b, bm], writes=[mod])
        for j in (1, 4):
            self.op("dve", lambda e: e.tensor_scalar_add(mod[:, j * D:(j + 1) * D], mod[:, j * D:(j + 1) * D], 1.0), reads=[mod], writes=[mod])
        self.release(m1)
        return mod, m0

    def to_fm(self, src, dstT, col0, reads, width=128):
        for half in range(2):
            b = self.bank()
            for kk in range(4):
                k = half * 4 + kk
                self.tr(b, b[:, kk * 128:(kk + 1) * 128], src[:, k * 128:(k + 1) * 128], self.ident[:], reads)
            eng = "act" if half == 0 else "dve"
            if eng == "act":
                self.op("act", lambda e: e.copy(dstT[:, half * 4:half * 4 + 4, col0:col0 + 128],
                                                 b[:, :].rearrange("p (k n) -> p k n", n=128)), reads=[b], writes=[dstT])
            else:
                self.op("dve", lambda e: e.tensor_copy(dstT[:, half * 4:half * 4 + 4, col0:col0 + 128],
                                                       b[:, :].rearrange("p (k n) -> p k n", n=128)), reads=[b], writes=[dstT])

    def phase_proj(self, x_d, Ltot, mod, w_in_l, p_d, tag):
        m = self.mark()
        nblk = Ltot // 512
        wv = w_in_l.rearrange("(k p) n -> p k n", p=128)
        xs = [self.sb("px%d" % i, [128, D]) for i in range(2)]
        hT = [self.sb("phT%d" % i, [128, 8, 512]) for i in range(2)]
        wts = [self.sb("pw%d" % i, [128, 8, 512]) for i in range(2)]
        po = [self.sb("po%d" % i, [128, 512]) for i in range(2)]
        cols = [(c * 512, min(512, P_IN - c * 512)) for c in range(7)]
        it = 0
        for blk in range(nblk):
            h_t = hT[blk % 2]
            for j in range(4):
                tt = blk * 4 + j
                x = xs[tt % 2]
                self.load(x, x[:], x_d[tt * 128:(tt + 1) * 128, :], reads=[("x" + tag, tt)])
                self.op("dve", lambda e: e.tensor_tensor(x[:], x[:], mod[:, D:2 * D], ALU.mult), reads=[x, mod], writes=[x])
                self.op("pool", lambda e: e.tensor_tensor(x[:], x[:], mod[:, 0:D], ALU.add), reads=[x, mod], writes=[x])
                self.to_fm(x, h_t, j * 128, [x])
            for ci, (c0, cw) in enumerate(cols):
                wt = wts[it % 2]
                it += 1
                self.load(wt, wt[:, :, 0:cw], wv[:, :, c0:c0 + cw])
                for j in range(4):
                    tt = blk * 4 + j
                    b = self.bank()
                    for k in range(8):
                        self.mm(b, b[:, 0:cw], h_t[:, k, j * 128:(j + 1) * 128], wt[:, k, 0:cw], [h_t, wt], start=(k == 0), stop=(k == 7))
                    o = po[(ci * 4 + j) % 2]
                    if j % 2 == 0:
                        self.op("act", lambda e: e.copy(o[:, 0:cw], b[:, 0:cw]), reads=[b], writes=[o])
                    else:
                        self.op("dve", lambda e: e.tensor_copy(o[:, 0:cw], b[:, 0:cw]), reads=[b], writes=[o])
                    self.store(p_d[tt * 128:(tt + 1) * 128, c0:c0 + cw], o[:, 0:cw], o, ("p" + tag, tt, ci))
        self.release(m)

    @staticmethod
    def pkeys(tag, tt, c0, c1):
        return [("p" + tag, tt, ci) for ci in range(c0 // 512, (c1 - 1) // 512 + 1)]


def _attn_phase(self, p_d, tag, nseq, L, lp, mix_d, ctx=None, rope=None, outk=None, outv=None):
    m = self.mark()
    nt = L // 128
    nkc = nt + (4 if ctx is not None else 0)
    Lk = nkc * 128
    QB = min(512, L)
    nq = QB // 128
    qT = self.sbp("qT", [64, 4, L])
    kT = self.sbp("kT", [64, 2, Lk])
    Vx = self.sb("Vx", [128, nkc, 2, 65])
    gq = self.sb("gq", [128, 6, 64])
    g1 = self.sb("g1", [128, 2, 64])
    self.load(g1, g1[:, 0, :], _bc(lp["at_qn"].rearrange("(o n) -> o n", o=1), 128))
    self.load(g1, g1[:, 1, :], _bc(lp["at_kn"].rearrange("(o n) -> o n", o=1), 128))
    self.op("dve", lambda e: e.tensor_copy(gq[:, 0:4, :], g1[:, 0:1, :].to_broadcast([128, 4, 64])), reads=[g1], writes=[gq])
    self.op("dve", lambda e: e.tensor_copy(gq[:, 4:6, :], g1[:, 1:2, :].to_broadcast([128, 2, 64])), reads=[g1], writes=[gq])
    pas = [self.sb("pa%d" % i, [128, 512]) for i in range(2)]
    sqs = [self.sb("sq%d" % i, [128, 384]) for i in range(2)]
    sss = [self.sb("ss%d" % i, [128, 6]) for i in range(2)]
    qks = [self.sb("qk%d" % i, [128, 6, 64]) for i in range(2)]
    if rope is not None:
        cs = [self.sb("rc%d" % i, [128, 2, 64]) for i in range(2)]
        sw = [self.sb("sw%d" % i, [128, 6, 64]) for i in range(2)]
    pts = [self.sb("PT%d" % i, [128, 512]) for i in range(3)]
    yo = [self.sb("yo%d" % i, [128, 4, 64]) for i in range(2)]
    rcp = [self.sb("rcp%d" % i, [128, 4, 1]) for i in range(2)]
    for s in range(nseq):
        self.op("pool", lambda e: e.memset(Vx[:], 1.0), writes=[Vx])
        for tt in range(nt):
            g = s * nt + tt
            i2 = g % 2
            pa, sq, ss, qk = pas[i2], sqs[i2], sss[i2], qks[i2]
            self.load(pa, pa[:], p_d[g * 128:(g + 1) * 128, OFF_AT:OFF_AT + 512], reads=self.pkeys(tag, g, OFF_AT, OFF_AT + 512))
            self.op("act", lambda e: e.activation(out=sq[:], in_=pa[:, 0:384], func=AF.Square), reads=[pa], writes=[sq])
            self.op("dve", lambda e: e.tensor_reduce(ss[:], sq[:].rearrange("p (a b) -> p a b", b=64), AX.X, ALU.add), reads=[sq], writes=[ss])
            self.op("dve", lambda e: e.tensor_scalar(ss[:], ss[:], 1.0 / 64, 1e-6, ALU.mult, ALU.add), reads=[ss], writes=[ss])
            self.op("act", lambda e: e.activation(out=ss[:], in_=ss[:], func=AF.Sqrt), reads=[ss], writes=[ss]); self.op("dve", lambda e: e.reciprocal(ss[:], ss[:]), reads=[ss], writes=[ss])
            self.op("dve", lambda e: e.tensor_tensor(qk[:], pa[:, 0:384].rearrange("p (a b) -> p a b", b=64),
                                                     ss[:].unsqueeze(2).to_broadcast([128, 6, 64]), ALU.mult), reads=[pa, ss], writes=[qk])
            self.op("pool", lambda e: e.tensor_tensor(qk[:], qk[:], gq[:], ALU.mult), reads=[qk, gq], writes=[qk])
            if rope is not None:
                c, w = cs[i2], sw[i2]
                self.load(c, c[:, 0, :], rope[0][tt * 128:(tt + 1) * 128, :])
                self.load(c, c[:, 1, :], rope[1][tt * 128:(tt + 1) * 128, :])
                qv = qk[:].rearrange("p h (a t f) -> p (h a) t f", a=2, t=2)
                wv = w[:].rearrange("p h (a t f) -> p (h a) t f", a=2, t=2)
                self.op("dve", lambda e: e.tensor_copy(wv[:, :, 0, :], qv[:, :, 1, :]), reads=[qk], writes=[w])
                self.op("dve", lambda e: e.tensor_copy(wv[:, :, 1, :], qv[:, :, 0, :]), reads=[qk], writes=[w])
                self.op("dve", lambda e: e.tensor_tensor(qk[:], qk[:], c[:, 0:1, :].to_broadcast([128, 6, 64]), ALU.mult), reads=[qk, c], writes=[qk])
                self.op("pool", lambda e: e.tensor_tensor(w[:], w[:], c[:, 1:2, :].to_broadcast([128, 6, 64]), ALU.mult), reads=[w, c], writes=[w])
                self.op("dve", lambda e: e.tensor_tensor(qk[:], qk[:], w[:], ALU.add), reads=[qk, w], writes=[qk])
            if outk is not None:
                self.store(outk[s][tt * 128:(tt + 1) * 128, :], qk[:, 4:6, :], qk, ("outk", s, tt))
                self.store(outv[s][tt * 128:(tt + 1) * 128, :], pa[:, 384:512], pa, ("outv", s, tt))
            ba = self.banks[(2 * g) % 4]
            for h in range(4):
                self.tr(ba, ba[0:64, h * 128:(h + 1) * 128], qk[:, h, :], self.ident[:], [qk])
            self.op("act", lambda e: e.copy(qT[:, :, tt * 128:(tt + 1) * 128], ba[0:64, :].rearrange("p (h n) -> p h n", n=128)), reads=[ba], writes=[qT])
            bb = self.banks[(2 * g + 1) % 4]
            for h in range(2):
                self.tr(bb, bb[0:64, h * 128:(h + 1) * 128], qk[:, 4 + h, :], self.ident[:], [qk])
            self.op("dve", lambda e: e.tensor_copy(kT[:, :, tt * 128:(tt + 1) * 128], bb[0:64, 0:256].rearrange("p (h n) -> p h n", n=128)), reads=[bb], writes=[kT])
            self.op("pool", lambda e: e.tensor_copy(Vx[:, tt, :, 0:64], pa[:, 384:512].rearrange("p (a b) -> p a b", b=64)), reads=[pa], writes=[Vx])
        if ctx is not None:
            kc_d, vc_d = ctx
            for c in range(4):
                pa = pas[c % 2]
                self.load(pa, pa[:, 0:128], kc_d[c * 128:(c + 1) * 128, :])
                self.load(pa, pa[:, 128:256], vc_d[c * 128:(c + 1) * 128, :])
                bb = self.banks[c % 4]
                for h in range(2):
                    self.tr(bb, bb[0:64, h * 128:(h + 1) * 128], pa[:, h * 64:(h + 1) * 64], self.ident[:], [pa])
                self.op("dve", lambda e: e.tensor_copy(kT[:, :, L + c * 128:L + (c + 1) * 128], bb[0:64, 0:256].rearrange("p (h n) -> p h n", n=128)), reads=[bb], writes=[kT])
                self.op("pool", lambda e: e.tensor_copy(Vx[:, nt + c, :, 0:64], pa[:, 128:256].rearrange("p (a b) -> p a b", b=64)), reads=[pa], writes=[Vx])
        iters = []
        for g in range(2):
            for qb in range(L // QB):
                for hh in range(2):
                    for kc in range(nkc):
                        iters.append((g, qb, 2 * g + hh, kc))
        oas = [self.banks[4 + qs] for qs in range(nq)]

        def emit_S(i):
            g, qb, head, kc = iters[i]
            sb_ = self.banks[i % 4]
            self.mm(sb_, sb_[:, 0:QB], kT.f[:, g, kc * 128:(kc + 1) * 128], qT.f[:, head, qb * QB:(qb + 1) * QB], [kT, qT])

        emit_S(0)
        for i, (g, qb, head, kc) in enumerate(iters):
            if i + 1 < len(iters):
                emit_S(i + 1)
            sb_ = self.banks[i % 4]
            pt = pts[i % 3]
            self.op("act", lambda e: e.activation(out=pt[:, 0:QB], in_=sb_[:, 0:QB], func=AF.Exp, scale=0.125), reads=[sb_], writes=[pt])
            for qs in range(nq):
                self.mm(oas[qs], oas[qs][:, 0:65], pt[:, qs * 128:(qs + 1) * 128], Vx[:, kc, g, :], [pt, Vx],
                        start=(kc == 0), stop=(kc == nkc - 1))
            if kc == nkc - 1:
                u_ = i // nkc
                y, r = yo[u_ % 2], rcp[u_ % 2]
                for qs in range(nq):
                    oa = oas[qs]
                    self.op("dve", lambda e: e.reciprocal(r[:, qs, :], oa[:, 64:65]), reads=[oa], writes=[r])
                    self.op("dve", lambda e: e.tensor_scalar(y[:, qs, :], oa[:, 0:64], r[:, qs, :], None, ALU.mult), reads=[oa, r], writes=[y])
                t0 = s * L + qb * QB
                dst = mix_d[t0:t0 + QB, 768 + head * 64:768 + (head + 1) * 64].rearrange("(q p) c -> p q c", p=128)
                self.store(dst, y[:, 0:nq, :], y, ("mix", tag, "at", s, qb, head))
    self.release(m)


Kern.phase_attn = _attn_phase


def _mlstm_phase(self, p_d, tag, nseq, L, lp, mix_d, st0=None, outs=None):
    m = self.mark()
    nc_ = L // 128
    hf = self.sb("hf", [128, nc_, 256])
    gb = self.sb("gb", [128, 16])
    self.load(gb, gb[:], _bc(lp["ml_gate_b"].rearrange("(o n) -> o n", o=1), 128))
    ng = self.sb("ng", [128, 256])
    self.load(ng, ng[:], _bc(lp["ml_norm_g"].rearrange("(o n) -> o n", o=1), 128))
    pms = [self.sb("pm%d" % i, [128, 1040]) for i in range(2)]
    Cn = [self.sbp("Cn%d" % i, [64, 4, 65]) for i in range(2)]
    mst = self.sb("mst", [4, 1])
    W = {}
    for nm, shp in (("g", [128, 16]), ("lf", [128, 4]), ("b", [128, 4]), ("Es", [128, 4]), ("eb", [128, 4]), ("ebl", [64, 4]),
                    ("kE", [128, 4, 64]), ("qT", [64, 4, 128]), ("kT", [64, 4, 128]), ("Vx", [128, 4, 65]), ("sT", [128, 4, 128]),
                    ("den", [128, 4, 1]), ("hh", [128, 4, 64]), ("A4", [4, 1]), ("bl4", [4, 1]), ("a", [128, 4]),
                    ("st", [128, 4, 2]), ("hs", [128, 4, 64]), ("sg", [128, 256]), ("sq", [128, 4, 64]), ("y", [128, 256])):
        W[nm] = [(self.sbp if nm in ("qT", "kT") else self.sb)("ml_" + nm + str(i), shp) for i in range(2)]
    cnt = [0]
    for s in range(nseq):
        for d in range(2):
            C = Cn[d]
            if st0 is None:
                self.op("dve", lambda e: e.memset(C[:], 0.0), writes=[C])
                self.op("dve", lambda e: e.memset(mst[:], 0.0), writes=[mst])
            else:
                C0, n0, m0 = st0
                self.load(C, C[:, :, 0:64], C0[d].rearrange("h d e -> d h e"))
                self.load(C, C[:, :, 64], n0[d].rearrange("h d -> d h"), allow_slow_non_contiguous=True)
                em = W["ebl"][0]
                self.load(em, em[:], _bc(m0[d:d + 1, :], 64))
                self.op("act", lambda e: e.activation(out=em[:], in_=em[:], func=AF.Exp), reads=[em], writes=[em])
                self.op("dve", lambda e: e.tensor_tensor(C[:], C[:], em[:].unsqueeze(2).to_broadcast([64, 4, 65]), ALU.mult), reads=[C, em], writes=[C])
            order = range(nc_) if d == 0 else range(nc_ - 1, -1, -1)
            for c in order:
                i2 = cnt[0] % 2
                cnt[0] += 1
                w = {k: v[i2] for k, v in W.items()}
                g = s * nc_ + c
                pm = pms[i2]
                self.load(pm, pm[:], p_d[g * 128:(g + 1) * 128, OFF_ML:OFF_ML + 1040], reads=self.pkeys(tag, g, OFF_ML, OFF_ML + 1040))
                gg, lf, b, Es, eb, ebl = w["g"], w["lf"], w["b"], w["Es"], w["eb"], w["ebl"]
                self.op("dve", lambda e: e.tensor_tensor(gg[:], pm[:, 1024:1040], gb[:], ALU.add), reads=[pm, gb], writes=[gg])
                self.op("act", lambda e: e.activation(out=lf[:], in_=gg[:, d * 8 + 4:d * 8 + 8], func=AF.Exp, scale=-1.0), reads=[gg], writes=[lf])
                self.op("act", lambda e: e.activation(out=lf[:], in_=lf[:], func=AF.Ln, bias=1.0), reads=[lf], writes=[lf])
                self.op("dve", lambda e: e.tensor_scalar_mul(lf[:], lf[:], -1.0), reads=[lf], writes=[lf])
                bk = self.bank()
                self.mm(bk, bk[:, 0:4], self.tri[:, d, :], lf[:], [self.tri, lf])
                self.mm(bk, bk[0:64, 8:12], self.ones[:, 0:64], lf[:], [self.ones, lf])
                self.mm(bk, bk[0:4, 16:17], lf[:], self.ones[:, 0:1], [self.ones, lf])
                self.op("dve", lambda e: e.tensor_copy(b[:], bk[:, 0:4]), reads=[bk], writes=[b])
                self.op("act", lambda e: e.activation(out=ebl[:], in_=bk[0:64, 8:12], func=AF.Exp), reads=[bk], writes=[ebl])
                self.op("dve", lambda e: e.tensor_copy(w["bl4"][:], bk[0:4, 16:17]), reads=[bk], writes=[w["bl4"]])
                a = w["a"]
                self.op("dve", lambda e: e.tensor_tensor(a[:], gg[:, d * 8:d * 8 + 4], b[:], ALU.subtract), reads=[gg, b], writes=[a])
                self.op("act", lambda e: e.activation(out=Es[:], in_=a[:], func=AF.Exp), reads=[a], writes=[Es])
                self.op("act", lambda e: e.activation(out=eb[:], in_=b[:], func=AF.Exp, scale=-1.0), reads=[b], writes=[eb])
                kE, qT, kT, Vx, sT = w["kE"], w["qT"], w["kT"], w["Vx"], w["sT"]
                self.op("pool", lambda e: e.tensor_tensor(kE[:], pm[:, 256:512].rearrange("p (h x) -> p h x", x=64),
                                                          Es[:].unsqueeze(2).to_broadcast([128, 4, 64]), ALU.mult), reads=[pm, Es], writes=[kE])
                self.op("pool", lambda e: e.memset(Vx[:, :, 64:65], 1.0), writes=[Vx])
                self.op("pool", lambda e: e.tensor_copy(Vx[:, :, 0:64], pm[:, 512:768].rearrange("p (h x) -> p h x", x=64)), reads=[pm], writes=[Vx])
                bq = self.bank()
                for h in range(4):
                    self.tr(bq, bq[0:64, h * 128:(h + 1) * 128], pm[:, h * 64:(h + 1) * 64], self.ident[:], [pm])
                self.op("act", lambda e: e.mul(qT[:], bq[0:64, :].rearrange("p (h n) -> p h n", n=128), 0.125), reads=[bq], writes=[qT])
                bk2 = self.bank()
                for h in range(4):
                    self.tr(bk2, bk2[0:64, h * 128:(h + 1) * 128], pm[:, 256 + h * 64:256 + (h + 1) * 64], self.ident[:], [pm])
                self.op("dve", lambda e: e.tensor_copy(kT[:], bk2[0:64, :].rearrange("p (h n) -> p h n", n=128)), reads=[bk2], writes=[kT])
                bg = self.bank()
                for h in range(4):
                    self.mm(bg, bg[:, h * 128:(h + 1) * 128], kT.f[:, h, :], qT.f[:, h, :], [kT, qT])
                for h in range(4):
                    self.op("dve", lambda e: e.scalar_tensor_tensor(sT[:, h, :], bg[:, h * 128:(h + 1) * 128], Es[:, h:h + 1], self.tri[:, d, :],
                                                                    ALU.mult, ALU.mult), reads=[bg, Es, self.tri], writes=[sT])
                hh, den = w["hh"], w["den"]
                for h in range(4):
                    bn = self.bank()
                    self.mm(bn, bn[:, 0:65], sT[:, h, :], Vx[:, h, :], [sT, Vx], start=True, stop=False)
                    self.mm(bn, bn[:, 0:65], qT.f[:, h, :], C.f[:, h, :], [qT, C], start=False, stop=True)
                    self.op("act", lambda e: e.activation(out=den[:, h, :], in_=bn[:, 64:65], func=AF.Abs), reads=[bn], writes=[den])
                    self.op("dve", lambda e: e.tensor_tensor(den[:, h, :], den[:, h, :], eb[:, h:h + 1], ALU.max), reads=[den, eb], writes=[den])
                    self.op("dve", lambda e: e.reciprocal(den[:, h, :], den[:, h, :]), reads=[den], writes=[den])
                    self.op("dve", lambda e: e.tensor_scalar(hh[:, h, :], bn[:, 0:64], den[:, h, :], None, ALU.mult), reads=[bn, den], writes=[hh])
                for h in range(4):
                    bs = self.bank()
                    self.mm(bs, bs[0:64, 0:65], kE[:, h, :], Vx[:, h, :], [kE, Vx], start=True, stop=False)
                    self.mm(bs, bs[0:64, 0:65], self.ident[:, 0:64], C.f[:, h, :], [self.ident, C], start=False, stop=True)
                    self.op("dve", lambda e: e.tensor_scalar(C[:, h, :], bs[0:64, 0:65], ebl[:, h:h + 1], None, ALU.mult), reads=[bs, ebl], writes=[C])
                if outs is not None:
                    bt = self.bank()
                    self.tr(bt, bt[0:4, 0:128], a[:], self.ident[:], [a])
                    self.op("dve", lambda e: e.tensor_reduce(w["A4"][:], bt[0:4, 0:128], AX.X, ALU.max), reads=[bt], writes=[w["A4"]])
                    self.op("dve", lambda e: e.scalar_tensor_tensor(mst[:], mst[:], w["A4"][:, 0:1], w["bl4"][:], ALU.max, ALU.add),
                            reads=[mst, w["A4"], w["bl4"]], writes=[mst])
                if d == 0:
                    self.op("pool", lambda e: e.tensor_copy(hf[:, c, :], hh[:].rearrange("p h x -> p (h x)")), reads=[hh], writes=[hf])
                else:
                    hs, st, sq, sg, y = w["hs"], w["st"], w["sq"], w["sg"], w["y"]
                    self.op("dve", lambda e: e.tensor_tensor(hs[:], hh[:], hf[:, c, :].rearrange("p (h x) -> p h x", x=64), ALU.add), reads=[hh, hf], writes=[hs])
                    self.op("dve", lambda e: e.tensor_reduce(st[:, :, 0], hs[:], AX.X, ALU.add), reads=[hs], writes=[st])
                    self.op("dve", lambda e: e.tensor_scalar_mul(st[:, :, 0], st[:, :, 0], 1.0 / 64), reads=[st], writes=[st])
                    self.op("dve", lambda e: e.tensor_tensor(hs[:], hs[:], st[:, :, 0:1].to_broadcast([128, 4, 64]), ALU.subtract), reads=[hs, st], writes=[hs])
                    self.op("act", lambda e: e.activation(out=sq[:], in_=hs[:], func=AF.Square), reads=[hs], writes=[sq])
                    self.op("dve", lambda e: e.tensor_reduce(st[:, :, 1], sq[:], AX.X, ALU.add), reads=[sq], writes=[st])
                    self.op("dve", lambda e: e.tensor_scalar(st[:, :, 1], st[:, :, 1], 1.0 / 64, 1e-6, ALU.mult, ALU.add), reads=[st], writes=[st])
                    self.op("act", lambda e: e.activation(out=st[:, :, 1], in_=st[:, :, 1], func=AF.Sqrt), reads=[st], writes=[st]); self.op("dve", lambda e: e.reciprocal(st[:, :, 1], st[:, :, 1]), reads=[st], writes=[st])
                    self.op("dve", lambda e: e.tensor_tensor(hs[:], hs[:], st[:, :, 1:2].to_broadcast([128, 4, 64]), ALU.mult), reads=[hs, st], writes=[hs])
                    self.op("act", lambda e: e.activation(out=sg[:], in_=pm[:, 768:1024], func=AF.Sigmoid), reads=[pm], writes=[sg])
                    self.op("pool", lambda e: e.tensor_tensor(sg[:], sg[:], ng[:], ALU.mult), reads=[sg, ng], writes=[sg])
                    self.op("dve", lambda e: e.tensor_tensor(y[:], hs[:].rearrange("p h x -> p (h x)"), sg[:], ALU.mult), reads=[hs, sg], writes=[y])
                    self.store(mix_d[g * 128:(g + 1) * 128, 256:512], y[:], y, ("mix", tag, "ml", g))
            if outs is not None:
                Co, no, mo = outs[s]
                dg = W["sT"][0]
                self.op("dve", lambda e: e.tensor_scalar(dg[0:4, 0, 0:4], self.ident[0:4, 0:4], mst[:, 0:1], None, ALU.mult), reads=[mst, self.ident], writes=[dg])
                bt = self.bank()
                self.mm(bt, bt[0:64, 0:4], self.ones[0:4, 0:64], dg[0:4, 0, 0:4], [self.ones, dg])
                em = W["ebl"][0]
                self.op("act", lambda e: e.activation(out=em[:], in_=bt[0:64, 0:4], func=AF.Exp, scale=-1.0), reads=[bt], writes=[em])
                Co_t = W["hh"][0]
                self.op("dve", lambda e: e.tensor_tensor(C[:], C[:], em[:].unsqueeze(2).to_broadcast([64, 4, 65]), ALU.mult), reads=[C, em], writes=[C])
                self.store(Co[d].rearrange("h d e -> d h e"), C[:, :, 0:64], C, ("oC", s, d))
                self.store(no[d].rearrange("h d -> d h"), C[:, :, 64], C, ("on", s, d), allow_slow_non_contiguous=True)
                self.store(mo[d].rearrange("(h o) -> h o", o=1), mst[:], mst, ("om", s, d))
    self.release(m)


Kern.phase_mlstm = _mlstm_phase


def _rwkv_phase(self, p_d, tag, nseq, L, lp, mix_d, yf_d, S0=None, outS=None):
    m = self.mark()
    nt = L // 128
    row = lambda ap: ap.rearrange("(o n) -> o n", o=1)
    mu = self.sb("mu", [64, 1024]); self.load(mu, mu[:], _bc(row(lp["rw_mu"]), 64))
    bcs = {}
    for nm in ("rw_kk", "rw_ka", "rw_rk", "rw_ln_g", "rw_ln_b"):
        t = self.sb(nm, [64, 256]); self.load(t, t[:], _bc(row(lp[nm]), 64)); bcs[nm] = t
    w0 = self.sb("w0", [64, 2, 256]); a0 = self.sb("a0", [64, 2, 256])
    w2 = self.sbp("w2", [64, 2, 256]); a2 = self.sbp("a2", [64, 2, 256])
    for d in range(2):
        self.load(w0, w0[:, d, :], _bc(lp["rw_w0"][d:d + 1, :], 64))
        self.load(a0, a0[:, d, :], _bc(lp["rw_a0"][d:d + 1, :], 64))
        self.load(w2, w2[:, d, :], lp["rw_w2"][d])
        self.load(a2, a2[:, d, :], lp["rw_a2"][d])
    g2 = self.sb("g2", [128, 256]); self.load(g2, g2[:], lp["rw_g2"])
    pcs = [self.sbp("pc%d" % i, [64, 2, 1024]) for i in range(3)]
    pp = self.sb("pp", [64, 2, 1024]); pn = self.sb("pn", [64, 2, 1024])
    S = [self.sbp("S%d" % i, [64, 4, 64]) for i in range(2)]
    t2 = lambda nm: self.sbp("rw_" + nm, [64, 2, 256])
    kk, sq, lwt, aa, t1, kt, bv, ecl, encl, At, Rt, yy, sq2 = [t2(n) for n in
        ("kk", "sq", "lwt", "aa", "t1", "kt", "bv", "ecl", "encl", "At", "Rt", "yy", "sq2")]
    kt0, yf = [t2(n) for n in ("kt0", "yf")]
    ss = self.sb("rw_ss", [64, 2, 4]); st = self.sb("rw_st", [64, 8, 2])
    tw = self.sb("rw_tw", [64, 2, 128]); twT = self.sbp("rw_twT", [64, 2, 2, 64])
    sgl = self.sb("rw_sgl", [64, 2, 128]); sgT = self.sb("rw_sgT", [128, 2, 64])
    f3 = lambda nm: self.sbp("rw_" + nm, [64, 8, 64])
    SBUF2 = [[f3(n + str(i)) for i in range(2)] for n in ("Pm", "Nak", "Nbr", "Nkr")]
    DBUF = [[t2(n + str(i)) for i in range(3)] for n in ("Bt", "Kt")] + [[f3(n + str(i)) for i in range(3)] for n in ("AT", "BT", "KT", "RT")] \
        + [[self.sb("rw_wcT%d" % i, [64, 8]) for i in range(3)]] + [[t2(n + str(i)) for i in range(3)] for n in ("gS", "bon")]
    Aj = [f3("Aj%d" % i) for i in range(2)]; AjT = [f3("AjT%d" % i) for i in range(2)]
    Xs = self.sbp("rw_Xs", [64, 4, 64]); UTs = self.sbp("rw_UTs", [64, 4, 64])
    I64 = self.ident[0:64, 0:64]

    def mask(idx):
        return self.tri[0:64, idx, 0:64].unsqueeze(1).to_broadcast([64, 8, 64])

    def hv(t, c2, h):
        return t[:, c2, h * 64:(h + 1) * 64]

    def hvf(t, c2, h):
        return t.f[:, c2, h * 64:(h + 1) * 64]

    I128 = self.ident[:, 0:64]

    def cm(ap_rows):
        return ap_rows.rearrange("(c p) n -> p c n", p=64)

    def mk_alloc(ids):
        st_ = [0]

        def f():
            b_ = self.banks[ids[st_[0] % len(ids)]]
            st_[0] += 1
            return b_
        return f

    bkP, bkS, bkQ = mk_alloc([0, 1, 2]), mk_alloc([3, 4, 5]), mk_alloc([6, 7])

    def mm8(alloc, lh, rh, reads):
        b_ = alloc()
        for u in range(8):
            self.mm(b_, b_[0:64, u * 64:(u + 1) * 64], lh.f[:, u, :], rh.f[:, u, :], reads)
        return b_, b_[0:64, :].rearrange("p (u n) -> p u n", n=64)

    for d in range(2):
        if True:
            Sd = S[d]

            def init_state(Sd=Sd, d=d):
                if S0 is None:
                    self.op("dve", lambda e: e.memset(Sd[:], 0.0), writes=[Sd])
                else:
                    self.load(Xs, Xs[:], S0[d].rearrange("h v k -> v h k"))
                    bk = bkQ()
                    for h in range(4):
                        self.tr(bk, bk[0:64, h * 64:(h + 1) * 64], Xs[:, h, :], I64, [Xs])
                    self.op("dve", lambda e: e.tensor_copy(Sd[:], bk[0:64, 0:256].rearrange("p (h n) -> p h n", n=64)), reads=[bk], writes=[Sd])

            def out_state(s, Sd=Sd, d=d):
                if outS is not None:
                    bk = bkQ()
                    for h in range(4):
                        self.tr(bk, bk[0:64, h * 64:(h + 1) * 64], Sd[:, h, :], I64, [Sd])
                    self.op("dve", lambda e: e.tensor_copy(Xs[:], bk[0:64, 0:256].rearrange("p (h n) -> p h n", n=64)), reads=[bk], writes=[Xs])
                    self.store(outS[s][d].rearrange("h v k -> v h k"), Xs[:], Xs, ("oS", s, d))
                return

            order = range(nt) if d == 0 else range(nt - 1, -1, -1)
            def prep(it_, s, tt, d=d, Sd=Sd):
                g = s * nt + tt
                base = g * 128
                pc = pcs[it_ % 3]
                Bt, Kt, AT, BT, KT, RT, wcT, gS, bon = [x[it_ % 3] for x in DBUF]
                Pm, Nak, Nbr, Nkr = [x[it_ % 2] for x in SBUF2]
                r_, k_, v_ = pc[:, :, 0:256], pc[:, :, 256:512], pc[:, :, 512:768]
                bcv = lambda nm: bcs[nm][:].unsqueeze(1).to_broadcast([64, 2, 256])
                g = s * nt + tt
                base = g * 128
                pc = pcs[it_ % 3]
                C0, C1 = OFF_RW, OFF_RW + 1024
                rk_ = self.pkeys(tag, g, C0, C1)
                self.load(pc, pc[:], cm(p_d[base:base + 128, C0:C1]), reads=rk_)
                if tt == 0:
                    self.load(pp, pp[0:1, 0, :], self.zrow[0:1, 0:1024], reads=["zrow"])
                    self.load(pp, pp[1:64, 0, :], p_d[base:base + 63, C0:C1], reads=rk_)
                    self.load(pp, pp[:, 1, :], p_d[base + 63:base + 127, C0:C1], reads=rk_)
                else:
                    self.load(pp, pp[:], cm(p_d[base - 1:base + 127, C0:C1]), reads=rk_ + self.pkeys(tag, g - 1, C0, C1))
                if tt == nt - 1:
                    self.load(pn, pn[:, 0, :], p_d[base + 1:base + 65, C0:C1], reads=rk_)
                    self.load(pn, pn[0:63, 1, :], p_d[base + 65:base + 128, C0:C1], reads=rk_)
                    self.load(pn, pn[63:64, 1, :], self.zrow[0:1, 0:1024], reads=["zrow"])
                else:
                    self.load(pn, pn[:], cm(p_d[base + 1:base + 129, C0:C1]), reads=rk_ + self.pkeys(tag, g + 1, C0, C1))
                mub = mu[:].unsqueeze(1).to_broadcast([64, 2, 1024])
                self.op("pool", lambda e: e.tensor_tensor(pp[:], pp[:], pn[:], ALU.add), reads=[pp, pn], writes=[pp])
                yield
                self.op("dve", lambda e: e.scalar_tensor_tensor(pp[:], pp[:], 0.5, pc[:], ALU.mult, ALU.subtract), reads=[pp, pc], writes=[pp])
                self.op("pool", lambda e: e.tensor_tensor(pp[:], pp[:], mub, ALU.mult), reads=[pp, mu], writes=[pp])
                yield
                self.op("dve", lambda e: e.tensor_tensor(pc[:], pc[:], pp[:], ALU.add), reads=[pc, pp], writes=[pc])
                r_, k_, v_ = pc[:, :, 0:256], pc[:, :, 256:512], pc[:, :, 512:768]
                bcv = lambda nm: bcs[nm][:].unsqueeze(1).to_broadcast([64, 2, 256])
                self.op("dve", lambda e: e.tensor_tensor(kk[:], k_, bcv("rw_kk"), ALU.mult), reads=[pc, bcs["rw_kk"]], writes=[kk])
                yield
                self.op("act", lambda e: e.activation(out=sq[:], in_=kk[:], func=AF.Square), reads=[kk], writes=[sq])
                self.op("dve", lambda e: e.tensor_reduce(ss[:], sq[:].rearrange("p c (h x) -> p c h x", x=64), AX.X, ALU.add), reads=[sq], writes=[ss])
                yield
                self.op("dve", lambda e: e.tensor_scalar_max(ss[:], ss[:], 1e-24), reads=[ss], writes=[ss])
                self.op("act", lambda e: e.activation(out=ss[:], in_=ss[:], func=AF.Sqrt), reads=[ss], writes=[ss]); self.op("dve", lambda e: e.reciprocal(ss[:], ss[:]), reads=[ss], writes=[ss])
                yield
                self.op("dve", lambda e: e.tensor_tensor(kk[:].rearrange("p c (h x) -> p c h x", x=64), kk[:].rearrange("p c (h x) -> p c h x", x=64),
                                                         ss[:].unsqueeze(3).to_broadcast([64, 2, 4, 64]), ALU.mult), reads=[kk, ss], writes=[kk])
                self.op("act", lambda e: e.activation(out=tw[:, :, 0:64], in_=pc[:, :, 768:832], func=AF.Tanh), reads=[pc], writes=[tw])
                self.op("pool", lambda e: e.tensor_copy(tw[:, :, 64:128], pc[:, :, 832:896]), reads=[pc], writes=[tw])
                yield
                bk = bkP()
                for c2 in range(2):
                    for j in range(2):
                        self.tr(bk, bk[0:64, (c2 * 2 + j) * 64:(c2 * 2 + j + 1) * 64], tw[:, c2, j * 64:(j + 1) * 64], I64, [tw])
                self.op("dve", lambda e: e.tensor_copy(twT[:], bk[0:64, 0:256].rearrange("p (c j n) -> p c j n", j=2, n=64)), reads=[bk], writes=[twT])

                def lowrank(dd, dst_w, dst_a):
                    bw = bkP()
                    for c2 in range(2):
                        self.mm(bw, bw[0:64, c2 * 256:(c2 + 1) * 256], twT.f[:, c2, 0, :], w2.f[:, dd, :], [twT, w2])
                    ba = bkP()
                    for c2 in range(2):
                        self.mm(ba, ba[0:64, c2 * 256:(c2 + 1) * 256], twT.f[:, c2, 1, :], a2.f[:, dd, :], [twT, a2])
                    if dst_w is not None:
                        self.op("dve", lambda e: e.tensor_tensor(dst_w[:], bw[0:64, :].rearrange("p (c n) -> p c n", n=256),
                                                                 w0[:, dd:dd + 1, :].to_broadcast([64, 2, 256]), ALU.add), reads=[bw, w0], writes=[dst_w])
                        self.op("act", lambda e: e.activation(out=dst_w[:], in_=dst_w[:], func=AF.Sigmoid), reads=[dst_w], writes=[dst_w])
                        self.op("dve", lambda e: e.tensor_scalar_mul(dst_w[:], dst_w[:], -RW_DECAY), reads=[dst_w], writes=[dst_w])
                    self.op("dve", lambda e: e.tensor_tensor(dst_a[:], ba[0:64, :].rearrange("p (c n) -> p c n", n=256),
                                                             a0[:, dd:dd + 1, :].to_broadcast([64, 2, 256]), ALU.add), reads=[ba, a0], writes=[dst_a])
                    self.op("act", lambda e: e.activation(out=dst_a[:], in_=dst_a[:], func=AF.Sigmoid), reads=[dst_a], writes=[dst_a])

                def make_kt(dst, a_t):
                    self.op("dve", lambda e: e.scalar_tensor_tensor(t1[:], a_t[:], -1.0, bcv("rw_ka"), ALU.add, ALU.mult), reads=[a_t, bcs["rw_ka"]], writes=[t1])
                    self.op("dve", lambda e: e.scalar_tensor_tensor(dst[:], t1[:], 1.0, k_, ALU.add, ALU.mult), reads=[t1, pc], writes=[dst])

                if d == 1:
                    lowrank(0, None, aa)
                    make_kt(kt0, aa)
                lowrank(d, lwt, aa)
                yield
                make_kt(kt, aa)
                self.op("pool", lambda e: e.tensor_tensor(bv[:], kk[:], aa[:], ALU.mult), reads=[kk, aa], writes=[bv])
                yield
                bc_ = bkP()
                self.mm(bc_, bc_[0:64, :], self.tri[:, d, 0:64], lwt.f[:].rearrange("p c n -> p (c n)"), [self.tri, lwt])
                clv = bc_[0:64, :].rearrange("p (c n) -> p c n", n=256)
                self.op("act", lambda e: e.activation(out=ecl[:], in_=clv, func=AF.Exp), reads=[bc_], writes=[ecl])
                self.op("act", lambda e: e.activation(out=encl[:], in_=clv, func=AF.Exp, scale=-1.0), reads=[bc_], writes=[encl])
                yield
                self.op("dve", lambda e: e.tensor_tensor(t1[:], clv, lwt[:], ALU.subtract), reads=[bc_, lwt], writes=[t1])
                self.op("act", lambda e: e.activation(out=t1[:], in_=t1[:], func=AF.Exp), reads=[t1], writes=[t1])
                yield
                self.op("dve", lambda e: e.scalar_tensor_tensor(At[:], t1[:], -1.0, kk[:], ALU.mult, ALU.mult), reads=[t1, kk], writes=[At])
                self.op("pool", lambda e: e.tensor_tensor(Bt[:], bv[:], encl[:], ALU.mult), reads=[bv, encl], writes=[Bt])
                yield
                self.op("dve", lambda e: e.tensor_tensor(Kt[:], kt[:], encl[:], ALU.mult), reads=[kt, encl], writes=[Kt])
                self.op("pool", lambda e: e.tensor_tensor(Rt[:], r_, ecl[:], ALU.mult), reads=[pc, ecl], writes=[Rt])
                yield
                bwc = bkP()
                for c2 in range(2):
                    for h in range(4):
                        u = c2 * 4 + h
                        self.mm(bwc, bwc[0:64, u:u + 1], hvf(lwt, c2, h), self.ones[:, 0:1], [lwt, self.ones])
                self.op("act", lambda e: e.activation(out=wcT[:], in_=bwc[0:64, 0:8], func=AF.Exp), reads=[bwc], writes=[wcT])
                for src, dst, eng in ((At, AT, "dve"), (Bt, BT, "act"), (Kt, KT, "dve"), (Rt, RT, "act")):
                    bt_ = bkP()
                    for c2 in range(2):
                        for h in range(4):
                            u = c2 * 4 + h
                            self.tr(bt_, bt_[0:64, u * 64:(u + 1) * 64], hv(src, c2, h), I64, [src])
                    v3 = bt_[0:64, :].rearrange("p (u n) -> p u n", n=64)
                    if eng == "dve":
                        self.op("dve", lambda e: e.tensor_copy(dst[:], v3), reads=[bt_], writes=[dst])
                    else:
                        self.op("act", lambda e: e.copy(dst[:], v3), reads=[bt_], writes=[dst])


                if d == 1:
                    self.op("act", lambda e: e.activation(out=sgl[:], in_=pc[:, :, 896:1024], func=AF.Sigmoid), reads=[pc], writes=[sgl])
                    bk = bkP()
                    for c2 in range(2):
                        self.tr(bk, bk[:, c2 * 64:(c2 + 1) * 64], sgl[:, c2, :], I64, [sgl])
                    self.op("dve", lambda e: e.tensor_copy(sgT[:], bk[:, 0:128].rearrange("p (c n) -> p c n", n=64)), reads=[bk], writes=[sgT])
                    bgm = bkP()
                    for c2 in range(2):
                        self.mm(bgm, bgm[0:64, c2 * 256:(c2 + 1) * 256], sgT[:, c2, :], g2[:], [sgT, g2])
                    self.op("act", lambda e: e.copy(gS[:], bgm[0:64, :].rearrange("p (c n) -> p c n", n=256)), reads=[bgm], writes=[gS])
                    self.op("pool", lambda e: e.tensor_tensor(kt0[:], kt0[:], kt[:], ALU.add), reads=[kt0, kt], writes=[kt0])
                    self.op("pool", lambda e: e.tensor_tensor(kt0[:], kt0[:], bcv("rw_rk"), ALU.mult), reads=[kt0, bcs["rw_rk"]], writes=[kt0])
                    self.op("dve", lambda e: e.tensor_tensor(kt0[:], kt0[:], r_, ALU.mult), reads=[kt0, pc], writes=[kt0])
                    self.op("dve", lambda e: e.tensor_reduce(ss[:], kt0[:].rearrange("p c (h x) -> p c h x", x=64), AX.X, ALU.add), reads=[kt0], writes=[ss])
                    self.op("dve", lambda e: e.tensor_tensor(bon[:].rearrange("p c (h x) -> p c h x", x=64), v_.rearrange("p c (h x) -> p c h x", x=64),
                                                             ss[:].unsqueeze(3).to_broadcast([64, 2, 4, 64]), ALU.mult), reads=[pc, ss], writes=[bon])
                yield

            def solve(it_, s, tt, d=d, Sd=Sd):
                g = s * nt + tt
                base = g * 128
                pc = pcs[it_ % 3]
                Bt, Kt, AT, BT, KT, RT, wcT, gS, bon = [x[it_ % 3] for x in DBUF]
                Pm, Nak, Nbr, Nkr = [x[it_ % 2] for x in SBUF2]
                r_, k_, v_ = pc[:, :, 0:256], pc[:, :, 256:512], pc[:, :, 512:768]
                bcv = lambda nm: bcs[nm][:].unsqueeze(1).to_broadcast([64, 2, 256])
                sm, smT, im_ = (2, 3, 0) if d == 0 else (3, 2, 1)
                b_, v3 = mm8(bkS, BT, AT, [BT, AT])
                yield
                self.op("dve", lambda e: e.tensor_tensor(Aj[0][:], v3, mask(sm), ALU.mult), reads=[b_, self.tri], writes=[Aj[0]])
                yield
                b_, v3 = mm8(bkS, AT, BT, [BT, AT])
                yield
                self.op("dve", lambda e: e.tensor_tensor(AjT[0][:], v3, mask(smT), ALU.mult), reads=[b_, self.tri], writes=[AjT[0]])
                yield
                self.op("pool", lambda e: e.tensor_tensor(Pm[:], Aj[0][:], I64.unsqueeze(1).to_broadcast([64, 8, 64]), ALU.add), reads=[Aj[0], self.ident], writes=[Pm])
                yield
                b_, v3 = mm8(bkS, KT, AT, [KT, AT])
                yield
                self.op("dve", lambda e: e.tensor_tensor(Nak[:], v3, mask(sm), ALU.mult), reads=[b_, self.tri], writes=[Nak])
                yield
                b_, v3 = mm8(bkS, BT, RT, [BT, RT])
                yield
                self.op("dve", lambda e: e.tensor_tensor(Nbr[:], v3, mask(im_), ALU.mult), reads=[b_, self.tri], writes=[Nbr])
                yield
                b_, v3 = mm8(bkS, KT, RT, [KT, RT])
                yield
                self.op("dve", lambda e: e.tensor_tensor(Nkr[:], v3, mask(im_), ALU.mult), reads=[b_, self.tri], writes=[Nkr])
                yield
                for j in range(1, 6):
                    pa_, pat = Aj[(j - 1) % 2], AjT[(j - 1) % 2]
                    na, nat = Aj[j % 2], AjT[j % 2]
                    b1, v1 = mm8(bkS, pa_, pat, [pa_, pat])
                    if j < 5:
                        b2, v2 = mm8(bkS, pat, pa_, [pa_, pat])
                    self.op("act", lambda e: e.copy(nat[:], v1), reads=[b1], writes=[nat])
                    if j < 5:
                        self.op("dve", lambda e: e.tensor_copy(na[:], v2), reads=[b2], writes=[na])
                    b3, v3 = mm8(bkS, nat, Pm, [nat, Pm])
                    self.op("dve", lambda e: e.tensor_tensor(Pm[:], v3, Pm[:], ALU.add), reads=[b3, Pm], writes=[Pm])
                    yield
                yield

            def seq(it_, s, tt, first, last, d=d, Sd=Sd):
                g = s * nt + tt
                base = g * 128
                pc = pcs[it_ % 3]
                Bt, Kt, AT, BT, KT, RT, wcT, gS, bon = [x[it_ % 3] for x in DBUF]
                Pm, Nak, Nbr, Nkr = [x[it_ % 2] for x in SBUF2]
                r_, k_, v_ = pc[:, :, 0:256], pc[:, :, 256:512], pc[:, :, 512:768]
                bcv = lambda nm: bcs[nm][:].unsqueeze(1).to_broadcast([64, 2, 256])
                if first:
                    init_state()
                    yield
                for c2 in ((0, 1) if d == 0 else (1, 0)):
                    bx = bkQ()
                    for h in range(4):
                        u = c2 * 4 + h
                        o = bx[0:64, h * 64:(h + 1) * 64]
                        self.mm(bx, o, AT.f[:, u, :], Sd.f[:, h, :], [AT, Sd], start=True, stop=False)
                        self.mm(bx, o, Nak.f[:, u, :], hvf(pc, c2, 8 + h), [Nak, pc], start=False, stop=True)
                    self.op("act", lambda e: e.copy(Xs[:], bx[0:64, 0:256].rearrange("p (h n) -> p h n", n=64)), reads=[bx], writes=[Xs])
                    yield
                    bu = bkQ()
                    for h in range(4):
                        u = c2 * 4 + h
                        self.mm(bu, bu[0:64, h * 64:(h + 1) * 64], Pm.f[:, u, :], Xs.f[:, h, :], [Pm, Xs])
                    self.op("dve", lambda e: e.tensor_copy(UTs[:], bu[0:64, 0:256].rearrange("p (h n) -> p h n", n=64)), reads=[bu], writes=[UTs])
                    yield
                    by = bkQ()
                    for h in range(4):
                        u = c2 * 4 + h
                        o = by[0:64, h * 64:(h + 1) * 64]
                        self.mm(by, o, RT.f[:, u, :], Sd.f[:, h, :], [RT, Sd], start=True, stop=False)
                        self.mm(by, o, Nbr.f[:, u, :], UTs.f[:, h, :], [Nbr, UTs], start=False, stop=False)
                        self.mm(by, o, Nkr.f[:, u, :], hvf(pc, c2, 8 + h), [Nkr, pc], start=False, stop=True)
                    self.op("act", lambda e: e.copy(yy[:, c2, :], by[0:64, 0:256]), reads=[by], writes=[yy])
                    yield
                    bs = bkQ()
                    for h in range(4):
                        o = bs[0:64, h * 64:(h + 1) * 64]
                        self.mm(bs, o, I128, Sd.f[:, h, :], [self.ident, Sd], start=True, stop=False)
                        self.mm(bs, o, hvf(Bt, c2, h), UTs.f[:, h, :], [Bt, UTs], start=False, stop=False)
                        self.mm(bs, o, hvf(Kt, c2, h), hvf(pc, c2, 8 + h), [Kt, pc], start=False, stop=True)
                    self.op("dve", lambda e: e.tensor_tensor(Sd[:], bs[0:64, 0:256].rearrange("p (h n) -> p h n", n=64),
                                                             wcT[:, c2 * 4:(c2 + 1) * 4].unsqueeze(2).to_broadcast([64, 4, 64]), ALU.mult), reads=[bs, wcT], writes=[Sd])
                if d == 0:
                    self.store(cm(yf_d[base:base + 128, :]), yy[:], yy, ("yf", tag, g))
                else:
                    self.load(yf, yf[:], cm(yf_d[base:base + 128, :]), reads=[("yf", tag, g)])
                    self.op("dve", lambda e: e.tensor_tensor(yy[:], yy[:], yf[:], ALU.add), reads=[yy, yf], writes=[yy])
                    y4 = yy[:].rearrange("p c (h x) -> p (c h) x", x=64)
                    self.op("dve", lambda e: e.tensor_reduce(st[:, :, 0], y4, AX.X, ALU.add), reads=[yy], writes=[st])
                    self.op("dve", lambda e: e.tensor_scalar_mul(st[:, :, 0], st[:, :, 0], 1.0 / 64), reads=[st], writes=[st])
                    self.op("dve", lambda e: e.tensor_tensor(y4, y4, st[:, :, 0:1].to_broadcast([64, 8, 64]), ALU.subtract), reads=[yy, st], writes=[yy])
                    self.op("act", lambda e: e.activation(out=sq2[:], in_=yy[:], func=AF.Square), reads=[yy], writes=[sq2])
                    self.op("dve", lambda e: e.tensor_reduce(st[:, :, 1], sq2[:].rearrange("p c (h x) -> p (c h) x", x=64), AX.X, ALU.add), reads=[sq2], writes=[st])
                    self.op("dve", lambda e: e.tensor_scalar(st[:, :, 1], st[:, :, 1], 1.0 / 64, 64e-5, ALU.mult, ALU.add), reads=[st], writes=[st])
                    self.op("act", lambda e: e.activation(out=st[:, :, 1], in_=st[:, :, 1], func=AF.Sqrt), reads=[st], writes=[st]); self.op("dve", lambda e: e.reciprocal(st[:, :, 1], st[:, :, 1]), reads=[st], writes=[st])
                    self.op("dve", lambda e: e.tensor_tensor(y4, y4, st[:, :, 1:2].to_broadcast([64, 8, 64]), ALU.mult), reads=[yy, st], writes=[yy])
                    self.op("pool", lambda e: e.tensor_tensor(yy[:], yy[:], bcv("rw_ln_g"), ALU.mult), reads=[yy, bcs["rw_ln_g"]], writes=[yy])
                    self.op("dve", lambda e: e.tensor_tensor(yy[:], yy[:], bcv("rw_ln_b"), ALU.add), reads=[yy, bcs["rw_ln_b"]], writes=[yy])
                    self.op("pool", lambda e: e.tensor_tensor(yy[:], yy[:], bon[:], ALU.add), reads=[yy, bon], writes=[yy])
                    self.op("dve", lambda e: e.tensor_tensor(yy[:], yy[:], gS[:], ALU.mult), reads=[yy, gS], writes=[yy])
                    self.store(cm(mix_d[base:base + 128, 512:768]), yy[:], yy, ("mix", tag, "rw", g))
                if last:
                    out_state(s)
                yield

            def runN(*gs):
                alive = [x for x in gs if x is not None]
                while alive:
                    for x in list(alive):
                        try:
                            next(x)
                        except StopIteration:
                            alive.remove(x)

            ol = list(order)
            tiles = [(s_, tt_) for s_ in range(nseq) for tt_ in ol]
            n_ = len(tiles)
            mk = lambda f, i: f(i, *tiles[i]) if i < n_ else None
            runN(mk(prep, 0))
            runN(mk(prep, 1), mk(solve, 0))
            for i_ in range(n_):
                s_, tt_ = tiles[i_]
                runN(mk(prep, i_ + 2), mk(solve, i_ + 1), seq(i_, s_, tt_, tt_ == ol[0], tt_ == ol[-1]))
    self.release(m)


Kern.phase_rwkv = _rwkv_phase


def hy_consts_np(L):
    N = 2 * L
    N1 = N // 128
    S1 = N1 // 2
    s1 = np.arange(S1)[:, None, None]
    s2 = np.arange(128)[None, :, None]
    f1 = np.arange(N1)[None, None, :]
    ang = 2 * np.pi * ((f1 * (128 * s1 + s2)) % N) / N
    mA = np.stack([np.cos(ang), -np.sin(ang)], axis=2).astype(np.float32)
    angD = np.transpose(ang, (2, 1, 0))
    mD = np.stack([np.cos(angD), -np.sin(angD)], axis=2).astype(np.float32)
    k = np.arange(128)
    a2 = 2 * np.pi * ((k[:, None] * k[None, :]) % 128) / 128
    cs = np.stack([np.cos(a2), np.sin(a2), -np.sin(a2)], axis=0).astype(np.float32)
    t = np.linspace(0.0, 1.0, L, dtype=np.float32)[:, None]
    angz = (2.0 * np.pi / L) * np.arange(L, dtype=np.float32)[:, None]
    bands = np.linspace(1e-4, 15, 16, dtype=np.float32)[None, :]
    z = np.concatenate([t, np.cos(bands * angz), np.sin(bands * angz)], axis=-1).astype(np.float32)
    mn, mx = math.log(1e-2) / 1.5, math.log(1e-2) / 0.3
    deltas = np.abs(np.linspace(mn, mx, 256, dtype=np.float32))
    dec = np.exp(-t * deltas[None, :]).astype(np.float32)
    return {"mA": mA, "mD": mD, "cs": cs, "zT": np.ascontiguousarray(z.T), "dec": dec}


def _hyena_phase(self, p_d, tag, nseq, L, lp, mix_d, hc, scr):
    N = 2 * L
    N1 = N // 128
    S1 = N1 // 2
    TB = 8
    FB = 2
    row = lambda ap: ap.rearrange("(o n) -> o n", o=1)
    m = self.mark()
    mAs = [self.sbp("mA%d" % i, [S1, TB, 2, N1]) for i in range(2)]
    mDs = [self.sbp("mD%d" % i, [N1, TB, 2, S1]) for i in range(2)]
    cs = self.sb("cs", [128, 3, 128]); self.load(cs, cs[:], hc["cs"].rearrange("a p n -> p a n"))
    ua = [self.sbp("hy_u%d" % i, [S1, TB, 256]) for i in range(2)]
    ea = [self.sb("hy_ea%d" % i, [2 * N1, 256]) for i in range(2)]
    rn = self.sb("hy_rn", [128, 1024])
    bias2 = self.sb("hy_bias", [1, 512]); self.load(bias2, bias2[:], lp["hy_bias"].rearrange("(o f) c -> o (f c)", o=1))
    Bds, Fd, Dd = [scr["Bd"], scr["Bd2"]], scr["Fd"], scr["Dd"]

    def stageA(src_rows, c0, kin, filt_k=None, bi=0):
        src3 = src_rows[:, c0:c0 + 256].rearrange("(a b) c -> a b c", b=128)
        Bd = Bds[bi]
        for blk in range(128 // TB):
            u = ua[blk % 2]
            self.load(u, u[:], src3[:, blk * TB:(blk + 1) * TB, :], reads=kin)
            mA = mAs[blk % 2]
            self.load(mA, mA[:], hc["mA"][:, blk * TB:(blk + 1) * TB, :, :])
            if filt_k is not None:
                self.op("dve", lambda e: e.tensor_tensor(u[:], u[:], rn[0:S1, filt_k * 256:(filt_k + 1) * 256].unsqueeze(1).to_broadcast([S1, TB, 256]), ALU.mult),
                        reads=[u, rn], writes=[u])
                if filt_k < 2 and blk == 0:
                    self.op("dve", lambda e: e.tensor_tensor(u[0:1, 0, :], u[0:1, 0, :], bias2[0:1, filt_k * 256:(filt_k + 1) * 256], ALU.add),
                            reads=[u, bias2], writes=[u])
            for j in range(TB):
                s2 = blk * TB + j
                b_ = self.bank()
                self.mm(b_, b_[0:2 * N1, 0:256], mA.f[:, j, :, :].rearrange("p r f -> p (r f)"), u.f[:, j, :], [mA, u])
                e_ = ea[s2 % 2]
                if s2 % 2 == 0:
                    self.op("act", lambda e: e.copy(e_[:], b_[0:2 * N1, 0:256]), reads=[b_], writes=[e_])
                else:
                    self.op("dve", lambda e: e.tensor_copy(e_[:], b_[0:2 * N1, 0:256]), reads=[b_], writes=[e_])
                self.store(Bd[s2].rearrange("f r c -> (f r) c"), e_[:], e_, ("Bd", bi, s2))
                yield

    bb = [self.sb("hy_bb%d" % i, [128, FB, 2, 256]) for i in range(2)]
    ff = [self.sb("hy_ff%d" % i, [128, FB, 2, 256]) for i in range(2)]
    uu = [self.sb("hy_uu%d" % i, [128, 2, FB, 256]) for i in range(2)]
    yy = [self.sb("hy_yy%d" % i, [128, 2, FB, 256]) for i in range(2)]
    tm = [self.sb("hy_tm%d" % i, [128, FB, 256]) for i in range(2)]
    co = [self.sb("hy_co%d" % i, [128, FB, 2, 256]) for i in range(2)]

    def stageBC(filt_out=None, filt_in=None, bi=0):
        Bd = Bds[bi]
        for ib, f0 in enumerate(range(0, N1, FB)):
            b = bb[ib % 2]
            BdR = Bd.rearrange("s f r c -> s (f r) c")
            for ri_ in range(2):
                self.load(b, b[:, :, ri_, :], BdR[:, ri_ * N1 + f0:ri_ * N1 + f0 + FB, :], reads=[("Bd", bi, s2) for s2 in range(128)])
            bre = b[:, :, 0, :]
            bim = b[:, :, 1, :]
            p_re = self.bank()
            self.mm(p_re, p_re[:, :].rearrange("p (f c) -> p f c", c=256), cs[:, 0, :], bre, [cs, b], start=True, stop=False)
            self.mm(p_re, p_re[:, :].rearrange("p (f c) -> p f c", c=256), cs[:, 1, :], bim, [cs, b], start=False, stop=True)
            p_im = self.bank()
            self.mm(p_im, p_im[:, :].rearrange("p (f c) -> p f c", c=256), cs[:, 0, :], bim, [cs, b], start=True, stop=False)
            self.mm(p_im, p_im[:, :].rearrange("p (f c) -> p f c", c=256), cs[:, 2, :], bre, [cs, b], start=False, stop=True)
            u = uu[ib % 2]
            self.op("act", lambda e: e.copy(u[:, 0, :, :], p_re[:, :].rearrange("p (f c) -> p f c", c=256)), reads=[p_re], writes=[u])
            self.op("dve", lambda e: e.tensor_copy(u[:, 1, :, :], p_im[:, :].rearrange("p (f c) -> p f c", c=256)), reads=[p_im], writes=[u])
            if filt_out is not None:
                self.store(Fd[filt_out][:, f0:f0 + FB, 0, :], u[:, 0, :, :], u, ("Fd", filt_out, f0, 0))
                self.store(Fd[filt_out][:, f0:f0 + FB, 1, :], u[:, 1, :, :], u, ("Fd", filt_out, f0, 1))
                yield
                continue
            f = ff[ib % 2]
            self.load(f, f[:, :, 0, :], Fd[filt_in][:, f0:f0 + FB, 0, :], reads=[("Fd", filt_in, f0, 0)])
            self.load(f, f[:, :, 1, :], Fd[2 + filt_in][:, f0:f0 + FB, 1, :], reads=[("Fd", 2 + filt_in, f0, 1)])
            y = yy[ib % 2]
            t_ = tm[ib % 2]
            self.op("dve", lambda e: e.tensor_tensor(y[:, 0], u[:, 0], f[:, :, 0, :], ALU.mult), reads=[u, f], writes=[y])
            self.op("pool", lambda e: e.tensor_tensor(t_[:], u[:, 1], f[:, :, 1, :], ALU.mult), reads=[u, f], writes=[t_])
            self.op("dve", lambda e: e.tensor_tensor(y[:, 0], y[:, 0], t_[:], ALU.subtract), reads=[y, t_], writes=[y])
            self.op("pool", lambda e: e.tensor_tensor(y[:, 1], u[:, 0], f[:, :, 1, :], ALU.mult), reads=[u, f], writes=[y])
            self.op("dve", lambda e: e.tensor_tensor(t_[:], u[:, 1], f[:, :, 0, :], ALU.mult), reads=[u, f], writes=[t_])
            self.op("pool", lambda e: e.tensor_tensor(y[:, 1], y[:, 1], t_[:], ALU.add), reads=[y, t_], writes=[y])
            q_re = self.bank()
            self.mm(q_re, q_re[:, :].rearrange("p (f c) -> p f c", c=256), cs[:, 0, :], y[:, 0], [cs, y], start=True, stop=False)
            self.mm(q_re, q_re[:, :].rearrange("p (f c) -> p f c", c=256), cs[:, 2, :], y[:, 1], [cs, y], start=False, stop=True)
            q_im = self.bank()
            self.mm(q_im, q_im[:, :].rearrange("p (f c) -> p f c", c=256), cs[:, 0, :], y[:, 1], [cs, y], start=True, stop=False)
            self.mm(q_im, q_im[:, :].rearrange("p (f c) -> p f c", c=256), cs[:, 1, :], y[:, 0], [cs, y], start=False, stop=True)
            c_ = co[ib % 2]
            self.op("act", lambda e: e.copy(c_[:, :, 0, :], q_re[:, :].rearrange("p (f c) -> p f c", c=256)), reads=[q_re], writes=[c_])
            self.op("dve", lambda e: e.tensor_copy(c_[:, :, 1, :], q_im[:, :].rearrange("p (f c) -> p f c", c=256)), reads=[q_im], writes=[c_])
            self.store(Dd[f0:f0 + FB].rearrange("f t r c -> t f r c"), c_[:], c_, ("Dd", f0))
            yield

    dl = [self.sbp("hy_dl%d" % i, [N1, TB, 2, 256]) for i in range(2)]
    gl = [self.sb("hy_gl%d" % i, [S1, TB, 256]) for i in range(2)]
    yo = [self.sb("hy_yo%d" % i, [S1, TB, 256]) for i in range(2)]

    def stageD(gate_rows, gc0, gkeys, dst_rows, dc0, dkey):
        g3 = gate_rows[:, gc0:gc0 + 256].rearrange("(a b) c -> a b c", b=128)
        d3 = dst_rows[:, dc0:dc0 + 256].rearrange("(a b) c -> a b c", b=128)
        for blk in range(128 // TB):
            dt_ = dl[blk % 2]
            self.load(dt_, dt_[:], Dd[:, blk * TB:(blk + 1) * TB, :, :], reads=[("Dd", f0) for f0 in range(0, N1, FB)])
            g_ = gl[blk % 2]
            self.load(g_, g_[:], g3[:, blk * TB:(blk + 1) * TB, :], reads=gkeys)
            mD = mDs[blk % 2]
            self.load(mD, mD[:], hc["mD"][:, blk * TB:(blk + 1) * TB, :, :])
            o_ = yo[blk % 2]
            for j in range(0, TB, 2):
                b_ = self.bank()
                for jj in range(2):
                    t2 = blk * TB + j + jj
                    o = b_[0:S1, jj * 256:(jj + 1) * 256]
                    self.mm(b_, o, mD.f[:, j + jj, 0, :], dt_.f[:, j + jj, 0, :], [mD, dt_], start=True, stop=False)
                    self.mm(b_, o, mD.f[:, j + jj, 1, :], dt_.f[:, j + jj, 1, :], [mD, dt_], start=False, stop=True)
                self.op("dve", lambda e: e.scalar_tensor_tensor(o_[:, j:j + 2, :], b_[0:S1, :].rearrange("p (t c) -> p t c", c=256), 1.0 / N,
                                                                g_[:, j:j + 2, :], ALU.mult, ALU.mult), reads=[b_, g_], writes=[o_])
            self.store(d3[:, blk * TB:(blk + 1) * TB, :], o_[:], o_, (dkey, blk))

    mf = self.mark()
    w1 = self.sb("hw1", [33, 64]); self.load(w1, w1[:], lp["hy_w1"])
    w2_ = self.sb("hw2", [64, 64]); self.load(w2_, w2_[:], lp["hy_w2"])
    w3 = self.sb("hw3", [64, 1024]); self.load(w3, w3[:], lp["hy_w3"])
    col = lambda ap: ap.rearrange("(n o) -> n o", o=1)
    b1 = self.sb("hb1", [64, 1]); self.load(b1, b1[:], col(lp["hy_b1"]))
    b2 = self.sb("hb2", [64, 1]); self.load(b2, b2[:], col(lp["hy_b2"]))
    fr = self.sb("hfr", [64, 1]); self.load(fr, fr[:], col(lp["hy_freq"]))
    zT = self.sb("hzT", [33, 512]); hid = self.sb("hhid", [64, 512]); nq = self.sb("hnq", [64, 512]); hid2 = self.sb("hhid2", [64, 512])
    dc = self.sb("hdec", [128, 256])
    ht = self.sb("hht", [128, 2, 2, 256])
    g12 = self.sb("hg12", [128, 2, 2, 256])
    ab = self.sb("hab", [128, 2, 256])
    pab = self.banks[7]
    gd = scr["gd"]

    def sin_layer(dst, ps, bb_, w):
        self.op("dve", lambda e: e.tensor_scalar(dst[:, 0:w], ps, bb_[:, 0:1], fr[:, 0:1], ALU.add, ALU.mult), reads=[ps_t, bb_, fr], writes=[dst])
        self.op("dve", lambda e: e.tensor_scalar(nq[:, 0:w], dst[:, 0:w], 1.0 / TWO_PI, MAGIC, ALU.mult, ALU.add), reads=[dst], writes=[nq])
        self.op("dve", lambda e: e.tensor_scalar(nq[:, 0:w], nq[:, 0:w], MAGIC, -TWO_PI, ALU.subtract, ALU.mult), reads=[nq], writes=[nq])
        self.op("dve", lambda e: e.tensor_tensor(dst[:, 0:w], dst[:, 0:w], nq[:, 0:w], ALU.add), reads=[dst, nq], writes=[dst])
        self.op("act", lambda e: e.activation(out=dst[:, 0:w], in_=dst[:, 0:w], func=AF.Sin), reads=[dst], writes=[dst])

    ntile = L // 128
    for tb in range((L + 511) // 512):
        w = min(512, L - tb * 512)
        self.load(zT, zT[:, 0:w], hc["zT"][:, tb * 512:tb * 512 + w])
        ps_t = self.banks[0]
        self.mm(ps_t, ps_t[0:64, 0:w], w1[:], zT[:, 0:w], [w1, zT])
        sin_layer(hid, ps_t[0:64, 0:w], b1, w)
        ps_t = self.banks[1]
        self.mm(ps_t, ps_t[0:64, 0:w], w2_[:], hid[:, 0:w], [w2_, hid])
        sin_layer(hid2, ps_t[0:64, 0:w], b2, w)
        for j in range(w // 128):
            tt = tb * 4 + j
            self.load(dc, dc[:], hc["dec"][tt * 128:(tt + 1) * 128, :])
            for half in range(2):
                ph = self.banks[2 + half]
                self.mm(ph, ph[:, :], hid2[:, j * 128:(j + 1) * 128], w3[:, half * 512:(half + 1) * 512], [hid2, w3])
                self.op("dve", lambda e: e.tensor_tensor(ht[:, half], ph[:, :].rearrange("p (d c) -> p d c", c=256),
                                                         dc[:].unsqueeze(1).to_broadcast([128, 2, 256]), ALU.mult), reads=[ph, dc], writes=[ht])
            if tt == 0:
                self.op("dve", lambda e: e.memset(ht[0:1, :, 1, :], 0.0), writes=[ht])
            self.op("dve", lambda e: e.tensor_tensor(g12[:, 0], ht[:, :, 0, :], ht[:, :, 1, :], ALU.add), reads=[ht], writes=[g12])
            self.op("pool", lambda e: e.tensor_tensor(g12[:, 1], ht[:, :, 0, :], ht[:, :, 1, :], ALU.subtract), reads=[ht], writes=[g12])
            self.store(gd[tt * 128:(tt + 1) * 128, :], g12[:].rearrange("p a f c -> p (a f c)"), g12, ("gd", tt))
            self.op("act", lambda e: e.activation(out=ht[:].rearrange("p f d c -> p (f d c)"), in_=ht[:].rearrange("p f d c -> p (f d c)"), func=AF.Abs), reads=[ht], writes=[ht])
            self.op("dve", lambda e: e.tensor_tensor(ab[:], ht[:, :, 0, :], ht[:, :, 1, :], ALU.add), reads=[ht], writes=[ab])
            self.mm(pab, pab[:, :], self.ones[:, :], ab[:].rearrange("p f c -> p (f c)"), [self.ones, ab], start=(tt == 0), stop=(tt == ntile - 1))
    self.op("dve", lambda e: e.reciprocal(rn[:, 0:512], pab[:, :]), reads=[pab], writes=[rn])
    self.op("dve", lambda e: e.tensor_copy(rn[:, 512:1024], rn[:, 0:512]), reads=[rn], writes=[rn])
    gkeys = [("gd", tt) for tt in range(ntile)]
    def runN(*gs):
        alive = [x for x in gs if x is not None]
        while alive:
            for x in list(alive):
                try:
                    next(x)
                except StopIteration:
                    alive.remove(x)

    runN(stageA(gd, 0, gkeys, filt_k=0, bi=0))
    for k in range(4):
        runN(stageA(gd, (k + 1) * 256, gkeys, filt_k=k + 1, bi=(k + 1) % 2) if k < 3 else None, stageBC(filt_out=k, bi=k % 2))
    self.release(mf)
    cw = self.sb("hcw", [128, 3, 768])
    for i in range(3):
        self.load(cw, cw[:, i, :], _bc(lp["hy_conv"][i:i + 1, :], 128))
    xc = [self.sb("hxc%d" % i, [128, 768]) for i in range(2)]
    xp = [self.sb("hxp%d" % i, [128, 768]) for i in range(2)]
    xn = [self.sb("hxn%d" % i, [128, 768]) for i in range(2)]
    pcv, zd = scr["pcv"], scr["zd"]
    nt = L // 128
    for s in range(nseq):
        for tt in range(nt):
            g = s * nt + tt
            base = g * 128
            i2 = g % 2
            c_, p_, n_ = xc[i2], xp[i2], xn[i2]
            rk_ = self.pkeys(tag, g, 0, 768)
            self.load(c_, c_[:], p_d[base:base + 128, 0:768], reads=rk_)
            if tt == 0:
                self.load(p_, p_[0:1, :], self.zrow[0:1, 0:768], reads=["zrow"])
                self.load(p_, p_[1:128, :], p_d[base:base + 127, 0:768], reads=rk_)
            else:
                self.load(p_, p_[:], p_d[base - 1:base + 127, 0:768], reads=rk_ + self.pkeys(tag, g - 1, 0, 768))
            if tt == nt - 1:
                self.load(n_, n_[0:127, :], p_d[base + 1:base + 128, 0:768], reads=rk_)
                self.load(n_, n_[127:128, :], self.zrow[0:1, 0:768], reads=["zrow"])
            else:
                self.load(n_, n_[:], p_d[base + 1:base + 129, 0:768], reads=rk_ + self.pkeys(tag, g + 1, 0, 768))
            self.op("dve", lambda e: e.tensor_tensor(c_[:], c_[:], cw[:, 1, :], ALU.mult), reads=[c_, cw], writes=[c_])
            self.op("pool", lambda e: e.tensor_tensor(p_[:], p_[:], cw[:, 0, :], ALU.mult), reads=[p_, cw], writes=[p_])
            self.op("pool", lambda e: e.tensor_tensor(n_[:], n_[:], cw[:, 2, :], ALU.mult), reads=[n_, cw], writes=[n_])
            self.op("dve", lambda e: e.tensor_tensor(c_[:], c_[:], p_[:], ALU.add), reads=[c_, p_], writes=[c_])
            self.op("dve", lambda e: e.tensor_tensor(c_[:], c_[:], n_[:], ALU.add), reads=[c_, n_], writes=[c_])
            self.store(pcv[base:base + 128, :], c_[:], c_, ("pcv", g))
    for s in range(nseq):
        rows = pcv[s * L:(s + 1) * L, :]
        ck = [("pcv", s * nt + tt) for tt in range(nt)]
        runN(stageA(rows, 0, ck))
        runN(stageBC(filt_in=0))
        stageD(rows, 256, ck, zd[s * L:(s + 1) * L, :], 0, ("zd", s))
        zk = [(("zd", s), blk) for blk in range(128 // TB)]
        runN(stageA(zd[s * L:(s + 1) * L, :], 0, zk))
        runN(stageBC(filt_in=1))
        stageD(rows, 512, ck, mix_d[s * L:(s + 1) * L, :], 0, ("mix", tag, "hy", s))
    self.release(m)


Kern.phase_hyena = _hyena_phase


def _ln(self, x2, g_bc, b_bc, W):
    st, sq = W["st"], W["sq"]
    self.op("dve", lambda e: e.tensor_reduce(st[:, 0:1], x2[:], AX.X, ALU.add), reads=[x2], writes=[st])
    self.op("dve", lambda e: e.tensor_scalar_mul(st[:, 0:1], st[:, 0:1], -1.0 / D), reads=[st], writes=[st])
    self.op("dve", lambda e: e.tensor_scalar(x2[:], x2[:], st[:, 0:1], None, ALU.add), reads=[x2, st], writes=[x2])
    self.op("act", lambda e: e.activation(out=sq[:], in_=x2[:], func=AF.Square), reads=[x2], writes=[sq])
    self.op("dve", lambda e: e.tensor_reduce(st[:, 1:2], sq[:], AX.X, ALU.add), reads=[sq], writes=[st])
    self.op("dve", lambda e: e.tensor_scalar(st[:, 1:2], st[:, 1:2], 1.0 / D, 1e-5, ALU.mult, ALU.add), reads=[st], writes=[st])
    self.op("act", lambda e: e.activation(out=st[:, 1:2], in_=st[:, 1:2], func=AF.Sqrt), reads=[st], writes=[st]); self.op("dve", lambda e: e.reciprocal(st[:, 1:2], st[:, 1:2]), reads=[st], writes=[st])
    self.op("dve", lambda e: e.scalar_tensor_tensor(x2[:], x2[:], st[:, 1:2], g_bc[:], ALU.mult, ALU.mult), reads=[x2, st, g_bc], writes=[x2])
    self.op("pool", lambda e: e.tensor_tensor(x2[:], x2[:], b_bc[:], ALU.add), reads=[x2, b_bc], writes=[x2])


def _out_phase(self, x_d, mix_d, Ltot, mod, w_out_l, lng, lnb, x1_d, tag):
    m = self.mark()
    row = lambda ap: ap.rearrange("(o n) -> o n", o=1)
    wo = self.sb("wo", [128, 8, D]); self.load(wo, wo[:], w_out_l.rearrange("(k p) n -> p k n", p=128))
    g_bc = self.sb("lng", [128, D]); self.load(g_bc, g_bc[:], _bc(row(lng), 128))
    b_bc = self.sb("lnb", [128, D]); self.load(b_bc, b_bc[:], _bc(row(lnb), 128))
    mx = [self.sb("omx%d" % i, [128, D]) for i in range(2)]
    mT = [self.sb("omT%d" % i, [128, 8, 128]) for i in range(2)]
    xs = [self.sb("oxs%d" % i, [128, D]) for i in range(2)]
    tp = [self.sb("otp%d" % i, [128, D]) for i in range(2)]
    W = {"st": self.sb("ost", [128, 2]), "sq": self.sb("osq", [128, D])}
    for tt in range(Ltot // 128):
        i2 = tt % 2
        mt, t_, x_, tmp = mx[i2], mT[i2], xs[i2], tp[i2]
        self.load(mt, mt[:], mix_d[tt * 128:(tt + 1) * 128, :], reads=[])
        self.to_fm(mt, t_, 0, [mt])
        self.load(x_, x_[:], x_d[tt * 128:(tt + 1) * 128, :], reads=[("x" + tag, tt)])
        for c in range(2):
            b = self.bank()
            for k in range(8):
                self.mm(b, b[:, :], t_[:, k, :], wo[:, k, c * 512:(c + 1) * 512], [t_, wo], start=(k == 0), stop=(k == 7))
            self.op("dve", lambda e: e.tensor_tensor(tmp[:, c * 512:(c + 1) * 512], b[:, :], mod[:, 2 * D + c * 512:2 * D + (c + 1) * 512], ALU.mult),
                    reads=[b, mod], writes=[tmp])
        self.op("dve", lambda e: e.scalar_tensor_tensor(x_[:], x_[:], ALPHA, tmp[:], ALU.mult, ALU.add), reads=[x_, tmp], writes=[x_])
        _ln(self, x_, g_bc, b_bc, W)
        self.store(x1_d[tt * 128:(tt + 1) * 128, :], x_[:], x_, ("x1" + tag, tt))
    self.release(m)


def _mlp_phase(self, x1_d, Ltot, mod, w1_l, w2_l, lng, lnb, xo_d, tag):
    m = self.mark()
    row = lambda ap: ap.rearrange("(o n) -> o n", o=1)
    g_bc = self.sb("lng2", [128, D]); self.load(g_bc, g_bc[:], _bc(row(lng), 128))
    b_bc = self.sb("lnb2", [128, D]); self.load(b_bc, b_bc[:], _bc(row(lnb), 128))
    w1v = w1_l.rearrange("(k p) n -> p k n", p=128)
    w2v = w2_l.rearrange("(f p) n -> p f n", p=128)
    xs = [self.sb("mxs%d" % i, [128, D]) for i in range(2)]
    tp = [self.sb("mtp%d" % i, [128, D]) for i in range(2)]
    hT = self.sb("mhT", [128, 8, 512])
    hid = self.sb("mhid", [128, 32, 512])
    w1t = [self.sb("mw1%d" % i, [128, 8, 512]) for i in range(2)]
    w2t = [self.sb("mw2%d" % i, [128, 4, D]) for i in range(2)]
    W = {"st": self.sb("mst", [128, 2]), "sq": self.sb("msq", [128, D])}
    for blk in range(Ltot // 512):
        for j in range(4):
            tt = blk * 4 + j
            x_ = xs[tt % 2]
            self.load(x_, x_[:], x1_d[tt * 128:(tt + 1) * 128, :], reads=[("x1" + tag, tt)])
            self.op("dve", lambda e: e.tensor_tensor(x_[:], x_[:], mod[:, 4 * D:5 * D], ALU.mult), reads=[x_, mod], writes=[x_])
            self.op("pool", lambda e: e.tensor_tensor(x_[:], x_[:], mod[:, 3 * D:4 * D], ALU.add), reads=[x_, mod], writes=[x_])
            self.to_fm(x_, hT, j * 128, [x_])
        for fc in range(8):
            wt = w1t[fc % 2]
            self.load(wt, wt[:], w1v[:, :, fc * 512:(fc + 1) * 512])
            for f4 in range(4):
                f = fc * 4 + f4
                b = self.banks[f % 8]
                for k in range(8):
                    self.mm(b, b[:, :], wt[:, k, f4 * 128:(f4 + 1) * 128], hT[:, k, :], [wt, hT], start=(k == 0), stop=(k == 7))
                self.op("act", lambda e: e.activation(out=hid[:, f, :], in_=b[:, :], func=AF.Relu), reads=[b], writes=[("hid", f)])
                self.op("pool", lambda e: e.tensor_tensor(hid[:, f, :], hid[:, f, :], hid[:, f, :], ALU.mult), reads=[("hid", f)], writes=[("hid", f)])
        for fg in range(8):
            wt = w2t[fg % 2]
            self.load(wt, wt[:], w2v[:, fg * 4:(fg + 1) * 4, :])
            for j in range(4):
                for c in range(2):
                    b = self.banks[j * 2 + c]
                    for i in range(4):
                        f = fg * 4 + i
                        self.mm(b, b[:, :], hid[:, f, j * 128:(j + 1) * 128], wt[:, i, c * 512:(c + 1) * 512], [("hid", f), wt],
                                start=(fg == 0 and i == 0), stop=(fg == 7 and i == 3))
        for j in range(4):
            tt = blk * 4 + j
            x_, tmp = xs[tt % 2], tp[tt % 2]
            for c in range(2):
                b = self.banks[j * 2 + c]
                self.op("dve", lambda e: e.tensor_tensor(tmp[:, c * 512:(c + 1) * 512], b[:, :], mod[:, 5 * D + c * 512:5 * D + (c + 1) * 512], ALU.mult),
                        reads=[b, mod], writes=[tmp])
            self.load(x_, x_[:], x1_d[tt * 128:(tt + 1) * 128, :], reads=[("x1" + tag, tt)])
            self.op("dve", lambda e: e.scalar_tensor_tensor(x_[:], x_[:], ALPHA, tmp[:], ALU.mult, ALU.add), reads=[x_, tmp], writes=[x_])
            _ln(self, x_, g_bc, b_bc, W)
            self.store(xo_d[tt * 128:(tt + 1) * 128, :], x_[:], x_, ("x" + tag, tt))
    self.release(m)


Kern.phase_out = _out_phase
Kern.phase_mlp = _mlp_phase

LAYER_PARAMS = {"w_in": [D, P_IN], "hy_conv": [3, 768], "hy_w1": [33, 64], "hy_b1": [64], "hy_freq": [64], "hy_w2": [64, 64], "hy_b2": [64],
                "hy_w3": [64, 1024], "hy_bias": [2, 256], "ml_gate_b": [16], "ml_norm_g": [256], "rw_mu": [1024], "rw_w0": [2, 256],
                "rw_w2": [2, 64, 256], "rw_a0": [2, 256], "rw_a2": [2, 64, 256], "rw_g2": [128, 256], "rw_kk": [256], "rw_ka": [256],
                "rw_rk": [256], "rw_ln_g": [256], "rw_ln_b": [256], "at_qn": [64], "at_kn": [64], "w_out": [D, D], "ln1_g": [D], "ln1_b": [D],
                "mlp_w1": [D, D_FF], "mlp_w2": [D_FF, D], "ln2_g": [D], "ln2_b": [D], "w_mod": [D, 6 * D], "b_mod": [6 * D]}


def rope_np():
    rows = 4096 // 64
    pos_r = np.repeat(np.arange(rows, dtype=np.float32), 64)
    pos_c = np.tile(np.arange(64, dtype=np.float32), rows)
    inv = (10000.0 ** (-np.arange(16, dtype=np.float32) / 16)).astype(np.float32)
    C = np.zeros((4096, 64), np.float32); S = np.zeros((4096, 64), np.float32)
    for a, pos in enumerate((pos_r, pos_c)):
        ang = pos[:, None] * inv[None, :]
        c, s = np.cos(ang), np.sin(ang)
        C[:, a * 32:a * 32 + 16] = c; C[:, a * 32 + 16:a * 32 + 32] = c
        S[:, a * 32:a * 32 + 16] = -s; S[:, a * 32 + 16:a * 32 + 32] = s
    return C, S


def consts_np():
    tri = np.zeros((4, 128, 128), np.float32)
    s = np.arange(128)[:, None]; t = np.arange(128)[None, :]
    tri[0] = (s <= t); tri[1] = (s >= t); tri[2] = (s < t); tri[3] = (s > t)
    return {"c_ident": np.eye(128, dtype=np.float32), "c_tri": tri}


_BUILD = {}


def build(depth=DEPTH, groups=("p", "s"), phases=None):
    ph = lambda n: phases is None or n in phases
    K = Kern(); K.init_banks(); K.consts()
    LP = {k: K.inp(k, [DEPTH] + v) for k, v in LAYER_PARAMS.items()}
    xp = K.inp("x_prompt", [1024, D]); xsm = K.inp("x_sample", [4096, D])
    ck = K.inp("cache_attn_k", [DEPTH, 512, 128]); cv = K.inp("cache_attn_v", [DEPTH, 512, 128])
    sC = K.inp("state_mlstm_C", [DEPTH, 2, 4, 64, 64]); sn = K.inp("state_mlstm_n", [DEPTH, 2, 4, 64]); sm = K.inp("state_mlstm_m", [DEPTH, 2, 4])
    sS = K.inp("state_rwkv_S", [DEPTH, 2, 4, 64, 64])
    cc = K.inp("c", [D]); cctx = K.inp("c_ctx", [D])
    rc = K.inp("rope_c", [4096, 64]); rs = K.inp("rope_s", [4096, 64])
    hcs = {}
    for L in (256, 4096):
        hn = hy_consts_np(L)
        hcs[L] = {k: K.inp("hc%d_%s" % (L, k), list(v.shape)) for k, v in hn.items()}
    hdn = hyd_consts_np()
    hdc = {k: K.inp("hd_" + k, list(v.shape)) for k, v in hdn.items()}
    yp = K.outp("y_prompt", [1024, D]); ys = K.outp("y_sample", [4096, D])
    ok = K.outp("new_attn_k", [4, DEPTH, 256, 128]); ov = K.outp("new_attn_v", [4, DEPTH, 256, 128])
    oC = K.outp("new_mlstm_C", [4, DEPTH, 2, 4, 64, 64]); on = K.outp("new_mlstm_n", [4, DEPTH, 2, 4, 64]); om = K.outp("new_mlstm_m", [4, DEPTH, 2, 4])
    oS = K.outp("new_rwkv_S", [4, DEPTH, 2, 4, 64, 64])
    p_d = K.dram("p_d", [4096, P_IN]); mix_d = K.dram("mix_d", [4096, D]); x1_d = K.dram("x1_d", [4096, D])
    xa = K.dram("xa_d", [4096, D]); xb = K.dram("xb_d", [4096, D]); yf_d = K.dram("yf_d", [4096, 256])
    scr = {"pcv": K.dram("pcv", [4096, 768]), "gd": K.dram("gd", [4096, 1024]), "Bd": K.dram("Bd", [128, 64, 2, 256]), "Bd2": K.dram("Bd2", [128, 64, 2, 256]),
           "Fd": K.dram("Fd", [4, 128, 64, 2, 256]), "Dd": K.dram("Dd", [64, 128, 2, 256]), "zd": K.dram("zd", [4096, 256])}
    for tag in groups:
        if tag == "p":
            nseq, L, x_in, x_out, cond = 4, 256, xp, yp, cctx
        else:
            nseq, L, x_in, x_out, cond = 1, 4096, xsm, ys, cc
        Ltot = nseq * L
        N1 = 2 * L // 128
        sc = {"pcv": scr["pcv"], "gd": scr["gd"][0:L], "Bd": scr["Bd"][:, 0:N1], "Bd2": scr["Bd2"][:, 0:N1], "Fd": scr["Fd"][:, :, 0:N1], "Dd": scr["Dd"][0:N1], "zd": scr["zd"]}
        x_cur = x_in
        for l in range(depth):
            lp = {k: v[l] for k, v in LP.items()}
            x_next = x_out if l == depth - 1 else (xa if l % 2 == 0 else xb)
            mod, m0 = K.phase_mod(cond, lp["w_mod"], lp["b_mod"], tag)
            if ph('proj'):
                K.phase_proj(x_cur, Ltot, mod, lp["w_in"], p_d, tag)
            if ph('hy'):
                if L == 256:
                    K.phase_hyena_direct(p_d, tag, nseq, lp, mix_d, hcs[L], hdc)
                else:
                    K.phase_hyena(p_d, tag, nseq, L, lp, mix_d, hcs[L], sc)
            if tag == "p":
                if ph('ml'):
                    K.phase_mlstm(p_d, tag, nseq, L, lp, mix_d, outs=[(oC[s, l], on[s, l], om[s, l]) for s in range(4)])
                if ph('rw'):
                    K.phase_rwkv(p_d, tag, nseq, L, lp, mix_d, yf_d, outS=[oS[s, l] for s in range(4)])
                if ph('at'):
                    K.phase_attn(p_d, tag, nseq, L, lp, mix_d, outk=[ok[s, l] for s in range(4)], outv=[ov[s, l] for s in range(4)])
            else:
                if ph('ml'):
                    K.phase_mlstm(p_d, tag, nseq, L, lp, mix_d, st0=(sC[l], sn[l], sm[l]))
                if ph('rw'):
                    K.phase_rwkv(p_d, tag, nseq, L, lp, mix_d, yf_d, S0=sS[l])
                if ph('at'):
                    K.phase_attn(p_d, tag, nseq, L, lp, mix_d, ctx=(ck[l], cv[l]), rope=(rc, rs))
            if ph('out'):
                K.phase_out(x_cur, mix_d, Ltot, mod, lp["w_out"], lp["ln1_g"], lp["ln1_b"], x1_d, tag)
            if ph('mlp'):
                K.phase_mlp(x1_d, Ltot, mod, lp["mlp_w1"], lp["mlp_w2"], lp["ln2_g"], lp["ln2_b"], x_next, tag)
            K.release(m0)
            x_cur = x_next
    K.finish()
    return K


def kernel(**inputs):
    f = lambda a: np.ascontiguousarray(np.asarray(a, dtype=np.float32))
    K = build()
    C, S = rope_np()
    common = dict(consts_np())
    common.update({"rope_c": C, "rope_s": S})
    for L in (256, 4096):
        for k, v in hy_consts_np(L).items():
            common["hc%d_%s" % (L, k)] = v
    for k, v in hyd_consts_np().items():
        common["hd_" + k] = v
    for k in LAYER_PARAMS:
        common[k] = f(inputs[k])
    common["c_ctx"] = f(inputs["c_ctx"])
    in_maps = []
    for core in range(8):
        b = core % 2
        im = dict(common)
        im["x_prompt"] = f(inputs["x_prompt"][core * 4:(core + 1) * 4]).reshape(1024, D)
        im["x_sample"] = f(inputs["x_sample"][b])
        im["cache_attn_k"] = f(inputs["cache_attn_k"][b]).reshape(DEPTH, 512, 128)
        im["cache_attn_v"] = f(inputs["cache_attn_v"][b]).reshape(DEPTH, 512, 128)
        im["state_mlstm_C"] = f(inputs["state_mlstm_C"][b]); im["state_mlstm_n"] = f(inputs["state_mlstm_n"][b])
        im["state_mlstm_m"] = f(inputs["state_mlstm_m"][b]); im["state_rwkv_S"] = f(inputs["state_rwkv_S"][b])
        im["c"] = f(inputs["c"][b])
        in_maps.append(im)
    res = run_bass_kernel_spmd(K.nc, in_maps, core_ids=list(range(8)))
    R = res.results
    cat = lambda name: np.concatenate([R[c][name] for c in range(8)], axis=0)
    y_prompt = cat("y_prompt").reshape(32, 256, D)
    y_sample = np.stack([R[0]["y_sample"], R[1]["y_sample"]], axis=0)
    nk = cat("new_attn_k").reshape(32, DEPTH, 256, 2, 64)
    nv = cat("new_attn_v").reshape(32, DEPTH, 256, 2, 64)
    return (y_prompt, y_sample, nk, nv, cat("new_mlstm_C"), cat("new_mlstm_n"), cat("new_mlstm_m"), cat("new_rwkv_S"))


def hyd_consts_np():
    L, N = 256, 512
    s = np.arange(L)[:, None]
    f = np.arange(N)[None, :]
    ang = 2 * np.pi * ((s * f) % N) / N
    Wf = np.stack([np.cos(ang), -np.sin(ang)], axis=1).astype(np.float32)
    Wf = np.ascontiguousarray(Wf.reshape(2, 128, 2, 512))
    angi = ang.T
    Wi = (np.stack([np.cos(angi), -np.sin(angi)], axis=1) / N).astype(np.float32)
    Wi = np.ascontiguousarray(Wi.reshape(4, 128, 2, 256))
    return {"Wf": Wf, "Wi": Wi}


def _hyena_direct(self, p_d, tag, nseq, lp, mix_d, hc, hd):
    L = 256
    m = self.mark()
    Wf = self.sb("Wf", [128, 2, 2, 512]); self.load(Wf, Wf[:], hd["Wf"].rearrange("k p r f -> p k r f"))
    Wi = self.sb("Wi", [128, 4, 2, 256]); self.load(Wi, Wi[:], hd["Wi"].rearrange("k p r t -> p k r t"))
    G = self.sb("hG", [128, 2, 1024])
    F = self.sb("hF", [128, 4, 2, 512])
    rn = self.sb("hy_rn", [128, 512])
    bias2 = self.sb("hy_bias", [1, 512]); self.load(bias2, bias2[:], lp["hy_bias"].rearrange("(o f) c -> o (f c)", o=1))
    mf = self.mark()
    w1 = self.sb("hw1", [33, 64]); self.load(w1, w1[:], lp["hy_w1"])
    w2_ = self.sb("hw2", [64, 64]); self.load(w2_, w2_[:], lp["hy_w2"])
    w3 = self.sb("hw3", [64, 1024]); self.load(w3, w3[:], lp["hy_w3"])
    col = lambda ap: ap.rearrange("(n o) -> n o", o=1)
    b1 = self.sb("hb1", [64, 1]); self.load(b1, b1[:], col(lp["hy_b1"]))
    b2 = self.sb("hb2", [64, 1]); self.load(b2, b2[:], col(lp["hy_b2"]))
    fr = self.sb("hfr", [64, 1]); self.load(fr, fr[:], col(lp["hy_freq"]))
    zT = self.sb("hzT", [33, 256]); hid = self.sb("hhid", [64, 256]); nq = self.sb("hnq", [64, 256]); hid2 = self.sb("hhid2", [64, 256])
    dc = self.sb("hdec", [128, 2, 256])
    ht = self.sb("hht", [128, 2, 2, 256])
    ab = self.sb("hab", [128, 2, 256])
    pab = self.banks[7]

    def sin_layer(dst, ps_t, bb_):
        self.op("dve", lambda e: e.tensor_scalar(dst[:], ps_t[0:64, 0:256], bb_[:, 0:1], fr[:, 0:1], ALU.add, ALU.mult), reads=[ps_t, bb_, fr], writes=[dst])
        self.op("dve", lambda e: e.tensor_scalar(nq[:], dst[:], 1.0 / TWO_PI, MAGIC, ALU.mult, ALU.add), reads=[dst], writes=[nq])
        self.op("dve", lambda e: e.tensor_scalar(nq[:], nq[:], MAGIC, -TWO_PI, ALU.subtract, ALU.mult), reads=[nq], writes=[nq])
        self.op("dve", lambda e: e.tensor_tensor(dst[:], dst[:], nq[:], ALU.add), reads=[dst, nq], writes=[dst])
        self.op("act", lambda e: e.activation(out=dst[:], in_=dst[:], func=AF.Sin), reads=[dst], writes=[dst])

    self.load(zT, zT[:], hc["zT"])
    self.load(dc, dc[:], hc["dec"].rearrange("(k p) c -> p k c", p=128))
    ps_t = self.banks[0]
    self.mm(ps_t, ps_t[0:64, 0:256], w1[:], zT[:], [w1, zT])
    sin_layer(hid, ps_t, b1)
    ps_t = self.banks[1]
    self.mm(ps_t, ps_t[0:64, 0:256], w2_[:], hid[:], [w2_, hid])
    sin_layer(hid2, ps_t, b2)
    for tt in range(2):
        for half in range(2):
            ph = self.banks[2 + half]
            self.mm(ph, ph[:, :], hid2[:, tt * 128:(tt + 1) * 128], w3[:, half * 512:(half + 1) * 512], [hid2, w3])
            self.op("dve", lambda e: e.tensor_tensor(ht[:, half], ph[:, :].rearrange("p (d c) -> p d c", c=256),
                                                     dc[:, tt:tt + 1, :].to_broadcast([128, 2, 256]), ALU.mult), reads=[ph, dc], writes=[ht])
        if tt == 0:
            self.op("dve", lambda e: e.memset(ht[0:1, :, 1, :], 0.0), writes=[ht])
        Gv = G[:, tt, :].rearrange("p (a f c) -> p a f c", a=2, f=2)
        self.op("dve", lambda e: e.tensor_tensor(Gv[:, 0], ht[:, :, 0, :], ht[:, :, 1, :], ALU.add), reads=[ht], writes=[G])
        self.op("pool", lambda e: e.tensor_tensor(Gv[:, 1], ht[:, :, 0, :], ht[:, :, 1, :], ALU.subtract), reads=[ht], writes=[G])
        self.op("act", lambda e: e.activation(out=ht[:].rearrange("p f d c -> p (f d c)"), in_=ht[:].rearrange("p f d c -> p (f d c)"), func=AF.Abs), reads=[ht], writes=[ht])
        self.op("dve", lambda e: e.tensor_tensor(ab[:], ht[:, :, 0, :], ht[:, :, 1, :], ALU.add), reads=[ht], writes=[ab])
        self.mm(pab, pab[:, :], self.ones[:, :], ab[:].rearrange("p f c -> p (f c)"), [self.ones, ab], start=(tt == 0), stop=(tt == 1))
    self.op("dve", lambda e: e.reciprocal(rn[:], pab[:, :]), reads=[pab], writes=[rn])
    self.op("dve", lambda e: e.tensor_tensor(G[:].rearrange("p k (a n) -> p (k a) n", a=2), G[:].rearrange("p k (a n) -> p (k a) n", a=2),
                                             rn[:].unsqueeze(1).to_broadcast([128, 4, 512]), ALU.mult), reads=[G, rn], writes=[G])
    self.op("dve", lambda e: e.tensor_tensor(G[0:1, 0, 0:512], G[0:1, 0, 0:512], bias2[0:1, :], ALU.add), reads=[G, bias2], writes=[G])
    for ft in range(4):
        for ri in range(2):
            b_ = self.bank()
            for kt in range(2):
                self.mm(b_, b_[:, :], Wf[:, kt, ri, ft * 128:(ft + 1) * 128], G[:, kt, ri * 512:(ri + 1) * 512], [Wf, G], start=(kt == 0), stop=(kt == 1))
            if ri == 0:
                self.op("act", lambda e: e.copy(F[:, ft, ri, :], b_[:, :]), reads=[b_], writes=[F])
            else:
                self.op("dve", lambda e: e.tensor_copy(F[:, ft, ri, :], b_[:, :]), reads=[b_], writes=[F])
    self.release(mf)
    cw = self.sb("hcw", [128, 3, 768])
    for i in range(3):
        self.load(cw, cw[:, i, :], _bc(lp["hy_conv"][i:i + 1, :], 128))
    nt = 2
    pcv = self.sb("hpcv", [128, nseq * nt, 768])
    xp = [self.sb("hxp%d" % i, [128, 768]) for i in range(2)]
    xn = [self.sb("hxn%d" % i, [128, 768]) for i in range(2)]
    for s in range(nseq):
        for tt in range(nt):
            g = s * nt + tt
            base = g * 128
            p_, n_ = xp[g % 2], xn[g % 2]
            c_ = pcv[:, g, :]
            ck = ("pcv", g)
            rk_ = self.pkeys(tag, g, 0, 768)
            self.dma("sp", c_, p_d[base:base + 128, 0:768], reads=rk_, writes=[ck])
            if tt == 0:
                self.load(p_, p_[0:1, :], self.zrow[0:1, 0:768], reads=["zrow"])
                self.load(p_, p_[1:128, :], p_d[base:base + 127, 0:768], reads=rk_)
            else:
                self.load(p_, p_[:], p_d[base - 1:base + 127, 0:768], reads=rk_ + self.pkeys(tag, g - 1, 0, 768))
            if tt == nt - 1:
                self.load(n_, n_[0:127, :], p_d[base + 1:base + 128, 0:768], reads=rk_)
                self.load(n_, n_[127:128, :], self.zrow[0:1, 0:768], reads=["zrow"])
            else:
                self.load(n_, n_[:], p_d[base + 1:base + 129, 0:768], reads=rk_ + self.pkeys(tag, g + 1, 0, 768))
            self.op("dve", lambda e: e.tensor_tensor(c_, c_, cw[:, 1, :], ALU.mult), reads=[ck, cw], writes=[ck])
            self.op("pool", lambda e: e.tensor_tensor(p_[:], p_[:], cw[:, 0, :], ALU.mult), reads=[p_, cw], writes=[p_])
            self.op("pool", lambda e: e.tensor_tensor(n_[:], n_[:], cw[:, 2, :], ALU.mult), reads=[n_, cw], writes=[n_])
            self.op("dve", lambda e: e.tensor_tensor(c_, c_, p_[:], ALU.add), reads=[ck, p_], writes=[ck])
            self.op("dve", lambda e: e.tensor_tensor(c_, c_, n_[:], ALU.add), reads=[ck, n_], writes=[ck])
    Z = self.sb("hZ", [128, nseq * nt, 256])
    Ys = [self.sb("hY%d" % i, [128, 4, 2, 512]) for i in range(2)]
    tms = [self.sb("htm%d" % i, [128, 512]) for i in range(2)]
    outs = [self.sb("hout%d" % i, [128, 2, 256]) for i in range(2)]
    pv4 = pcv[:].rearrange("p (s k) c -> p s k c", k=nt)
    z4 = Z[:].rearrange("p (s k) c -> p s k c", k=nt)
    it = 0
    for stage in range(2):
        for s0 in range(0, nseq, 2):
            Y = Ys[it % 2]
            it += 1
            if stage == 0:
                src = lambda kt: pv4[:, s0:s0 + 2, kt, 0:256]
                skeys = [("pcv", (s0 + i) * nt + k) for i in range(2) for k in range(nt)]
            else:
                src = lambda kt: z4[:, s0:s0 + 2, kt, :]
                skeys = [("hZ", (s0 + i) * nt + k) for i in range(2) for k in range(nt)]
            for ft in range(4):
                bre = self.bank()
                for kt in range(2):
                    self.mm(bre, bre[:, :].rearrange("p (s c) -> p s c", c=256), Wf[:, kt, 0, ft * 128:(ft + 1) * 128], src(kt), [Wf] + skeys, start=(kt == 0), stop=(kt == 1))
                bim = self.bank()
                for kt in range(2):
                    self.mm(bim, bim[:, :].rearrange("p (s c) -> p s c", c=256), Wf[:, kt, 1, ft * 128:(ft + 1) * 128], src(kt), [Wf] + skeys, start=(kt == 0), stop=(kt == 1))
                fre = F[:, ft, 0, stage * 256:(stage + 1) * 256].unsqueeze(1).to_broadcast([128, 2, 256])
                fim = F[:, ft, 1, stage * 256:(stage + 1) * 256].unsqueeze(1).to_broadcast([128, 2, 256])
                v3 = lambda ap: ap.rearrange("p (s c) -> p s c", c=256)
                tm = tms[ft % 2]
                yk = ("hY", it % 2, ft)
                self.op("dve", lambda e: e.tensor_tensor(v3(Y[:, ft, 0, :]), v3(bre[:, :]), fre, ALU.mult), reads=[bre, F], writes=[yk])
                self.op("dve", lambda e: e.tensor_tensor(v3(tm[:]), v3(bim[:, :]), fim, ALU.mult), reads=[bim, F], writes=[tm])
                self.op("pool", lambda e: e.tensor_tensor(Y[:, ft, 0, :], Y[:, ft, 0, :], tm[:], ALU.subtract), reads=[yk, tm], writes=[yk])
                self.op("dve", lambda e: e.tensor_tensor(v3(Y[:, ft, 1, :]), v3(bre[:, :]), fim, ALU.mult), reads=[bre, F], writes=[yk])
                self.op("dve", lambda e: e.tensor_tensor(v3(tm[:]), v3(bim[:, :]), fre, ALU.mult), reads=[bim, F], writes=[tm])
                self.op("pool", lambda e: e.tensor_tensor(Y[:, ft, 1, :], Y[:, ft, 1, :], tm[:], ALU.add), reads=[yk, tm], writes=[yk])
            for tt in range(2):
                b_ = self.bank()
                n_ = 0
                for ft in range(4):
                    for ri in range(2):
                        self.mm(b_, b_[:, :], Wi[:, ft, ri, tt * 128:(tt + 1) * 128], Y[:, ft, ri, :], [Wi, ("hY", it % 2, ft)], start=(n_ == 0), stop=(n_ == 7))
                        n_ += 1
                bv = b_[:, :].rearrange("p (s c) -> p s c", c=256)
                if stage == 0:
                    gate = pv4[:, s0:s0 + 2, tt, 256:512]
                    zk = [("hZ", (s0 + i) * nt + tt) for i in range(2)]
                    self.op("dve", lambda e: e.tensor_tensor(z4[:, s0:s0 + 2, tt, :], bv, gate, ALU.mult),
                            reads=[b_] + [("pcv", (s0 + i) * nt + tt) for i in range(2)], writes=zk)
                else:
                    gate = pv4[:, s0:s0 + 2, tt, 512:768]
                    o_ = outs[(s0 // 2 * 2 + tt) % 2]
                    self.op("dve", lambda e: e.tensor_tensor(o_[:], bv, gate, ALU.mult),
                            reads=[b_] + [("pcv", (s0 + i) * nt + tt) for i in range(2)], writes=[o_])
                    for i in range(2):
                        r0 = (s0 + i) * L + tt * 128
                        self.store(mix_d[r0:r0 + 128, 0:256], o_[:, i, :], o_, ("mix", tag, "hy", s0 + i, tt))
    self.release(m)


Kern.phase_hyena_direct = _hyena_direct
```
